# Optimizing a Trainium2 kernel written in Bass

```python
import math
import numpy as np
import jax
import jax.numpy as jnp
from jax import lax

D_MODEL = 2048
BATCH = 4
SEQ = 8192
DEPTH = 2

F32 = jnp.float32
HEAD_DIM = 128
EPS = 1e-6
D_FF = 128 * ((8 * D_MODEL // 3 + 127) // 128)

DN_HEADS = 8
DN_WIDTH = DN_HEADS * HEAD_DIM
DN_CONV = 4
DN_CHUNK = 64

NSA_HEADS = 8
NSA_GROUPS = 2
NSA_HPG = NSA_HEADS // NSA_GROUPS
NSA_WIDTH = NSA_HEADS * HEAD_DIM
NSA_KV = NSA_GROUPS * HEAD_DIM
CMP_STRIDE = 16
CMP_BLOCK = 2 * CMP_STRIDE
CMP_HIDDEN = 256
SEL_BLOCK = 64
SEL_TOPN = 16
WINDOW = 512
Q_BLOCK = 128
ALIBI_MAX = 8.0

SGU_GROUPS = 8
SGU_WIDTH = SGU_GROUPS * HEAD_DIM
SGU_CHUNK = 128

IN_SIZES = (DN_WIDTH, DN_WIDTH, DN_WIDTH, DN_WIDTH, DN_HEADS, DN_HEADS,
            NSA_WIDTH, NSA_KV, NSA_KV, NSA_KV, NSA_KV, NSA_KV, NSA_KV, 3 * NSA_HEADS,
            SGU_WIDTH, SGU_WIDTH,
            D_MODEL, D_MODEL, D_MODEL)
D_IN = sum(IN_SIZES)
IN_OFFSETS = tuple(int(o) for o in np.cumsum(IN_SIZES)[:-1])

kernel_name = 'hybrid_deltanet_nsa_gmlp_macaron'


def rms_norm(x, w):
    xf = x.astype(F32)
    y = xf * lax.rsqrt(jnp.mean(xf * xf, axis=-1, keepdims=True) + EPS)
    return (y * w.astype(F32)).astype(x.dtype)


def l2_norm(x):
    xf = x.astype(F32)
    return xf * lax.rsqrt(jnp.sum(xf * xf, axis=-1, keepdims=True) + EPS)


def swiglu(h, w_gate, w_up, w_down):
    return (jax.nn.silu(h @ w_gate) * (h @ w_up)) @ w_down


def causal_depthwise_conv(x, w):
    k, c = w.shape
    return lax.conv_general_dilated(x, w[:, None, :].astype(x.dtype), window_strides=(1,),
                                    padding=[(k - 1, 0)], dimension_numbers=('NWC', 'WIO', 'NWC'),
                                    feature_group_count=c)


def masked_softmax(s, mask):
    p = jax.nn.softmax(jnp.where(mask, s, -1e30), axis=-1)
    return jnp.where(mask, p, 0.0)


def gated_deltanet(q, k, v, z, a, b, conv_w, a_log, dt_bias, out_norm):
    bsz, seq, _ = q.shape
    qkv = jax.nn.silu(causal_depthwise_conv(jnp.concatenate([q, k, v], axis=-1), conv_w))
    q, k, v = jnp.split(qkv.astype(F32), 3, axis=-1)
    heads = lambda t: t.reshape(bsz, seq, DN_HEADS, HEAD_DIM)
    q = l2_norm(heads(q)) * HEAD_DIM ** -0.5
    k = l2_norm(heads(k))
    v = heads(v)
    beta = jax.nn.sigmoid(b.astype(F32))
    g = -jnp.exp(a_log.astype(F32)) * jax.nn.softplus(a.astype(F32) + dt_bias.astype(F32))
    n_chunk = seq // DN_CHUNK

    def to_chunks(t):
        t = t.reshape((bsz, n_chunk, DN_CHUNK) + t.shape[2:])
        return jnp.moveaxis(t, (1, 3), (0, 2))

    qc, kc, vc, bc = to_chunks(q), to_chunks(k), to_chunks(v), to_chunks(beta)
    gc = jnp.cumsum(to_chunks(g), axis=-1)
    idx = jnp.arange(DN_CHUNK)
    incl = idx[:, None] >= idx[None, :]
    strict = idx[:, None] > idx[None, :]
    diff = gc[..., :, None] - gc[..., None, :]
    decay = jnp.where(incl, jnp.exp(jnp.where(incl, diff, 0.0)), 0.0)
    kb = kc * bc[..., None]
    lower = jnp.where(strict, jnp.einsum('nbhid,nbhjd->nbhij', kb, kc) * decay, 0.0)
    eye = jnp.eye(DN_CHUNK, dtype=F32)
    rhs = jnp.concatenate([vc * bc[..., None], kb * jnp.exp(gc)[..., None]], axis=-1)
    sol = lax.linalg.triangular_solve(eye + lower, rhs, left_side=True, lower=True, unit_diagonal=True)
    u, w = jnp.split(sol, [HEAD_DIM], axis=-1)
    attn = jnp.einsum('nbhid,nbhjd->nbhij', qc, kc) * decay
    q_dec = qc * jnp.exp(gc)[..., None]
    k_dec = kc * jnp.exp(gc[..., -1:] - gc)[..., None]
    g_last = jnp.exp(gc[..., -1])

    def step(state, xs):
        u_n, w_n, attn_n, q_n, k_n, gl_n = xs
        v_new = u_n - jnp.einsum('bhck,bhkv->bhcv', w_n, state)
        o = jnp.einsum('bhck,bhkv->bhcv', q_n, state) + jnp.einsum('bhij,bhjv->bhiv', attn_n, v_new)
        state = state * gl_n[..., None, None] + jnp.einsum('bhck,bhcv->bhkv', k_n, v_new)
        return state, o

    s0 = jnp.zeros((bsz, DN_HEADS, HEAD_DIM, HEAD_DIM), F32)
    _, o = lax.scan(step, s0, (u, w, attn, q_dec, k_dec, g_last))
    o = jnp.moveaxis(o, (0, 2), (1, 3)).reshape(bsz, seq, DN_HEADS, HEAD_DIM)
    zg = jax.nn.silu(z.astype(F32).reshape(bsz, seq, DN_HEADS, HEAD_DIM))
    return (rms_norm(o, out_norm) * zg).reshape(bsz, seq, DN_WIDTH)


def native_sparse_attention(q, k_cmp, v_cmp, k_slc, v_slc, k_win, v_win, gate_logits,
                            q_norm, k_norm, cmpk_pos, cmpk_w1, cmpk_w2, cmpv_pos, cmpv_w1, cmpv_w2):
    bsz, seq, _ = q.shape
    G, HPG, DH = NSA_GROUPS, NSA_HPG, HEAD_DIM
    q = (rms_norm(q.reshape(bsz, seq, G, HPG, DH), q_norm).astype(F32) * DH ** -0.5)
    kv = lambda t: t.reshape(bsz, seq, G, DH)

    def compress(t, pos, w1, w2):
        sub = kv(t).reshape(bsz, seq // CMP_STRIDE, CMP_STRIDE, G, DH)
        blk = jnp.concatenate([sub[:, :-1], sub[:, 1:]], axis=2) + pos[None, None, :, None, :]
        blk = jnp.transpose(blk, (0, 1, 3, 2, 4)).reshape(bsz, -1, G, CMP_BLOCK * DH)
        return jax.nn.silu(blk @ w1) @ w2

    kc = rms_norm(compress(k_cmp, cmpk_pos, cmpk_w1, cmpk_w2), k_norm[0]).astype(F32)
    vc = compress(v_cmp, cmpv_pos, cmpv_w1, cmpv_w2).astype(F32)
    n_cmp = kc.shape[1]
    pos_c = jnp.arange(n_cmp) * CMP_STRIDE + (CMP_BLOCK - 1)

    n_sel = seq // SEL_BLOCK
    top_n = min(SEL_TOPN, n_sel)
    sel_blocks = lambda t: t.reshape(bsz, n_sel, SEL_BLOCK, G, DH).transpose(0, 3, 1, 2, 4)
    ks = sel_blocks(rms_norm(kv(k_slc), k_norm[1]).astype(F32))
    vs = sel_blocks(kv(v_slc).astype(F32))
    ci = np.arange(n_cmp)[:, None]
    sj = np.arange(n_sel)[None, :]
    per = SEL_BLOCK // CMP_STRIDE
    overlap = jnp.asarray((ci // per == sj).astype(np.float32) + ((ci + 1) // per == sj).astype(np.float32))

    pad = ((0, 0), (WINDOW, 0), (0, 0), (0, 0))
    kw = jnp.pad(rms_norm(kv(k_win), k_norm[2]).astype(F32), pad)
    vw = jnp.pad(kv(v_win).astype(F32), pad)
    gates = jax.nn.sigmoid(gate_logits.astype(F32)).reshape(bsz, seq, 3, G, HPG)
    slopes = (2.0 ** (-ALIBI_MAX * jnp.arange(1, NSA_HEADS + 1, dtype=F32) / NSA_HEADS)).reshape(G, HPG)
    sl = slopes[None, :, :, None, None]
    bi = jnp.arange(bsz)[:, None, None]
    gi = jnp.arange(G)[None, :, None]
    jsel = jnp.arange(n_sel)

    def block(qb):
        t0 = qb * Q_BLOCK
        t = t0 + jnp.arange(Q_BLOCK)
        qblk = lax.dynamic_slice_in_dim(q, t0, Q_BLOCK, axis=1)
        gblk = lax.dynamic_slice_in_dim(gates, t0, Q_BLOCK, axis=1)
        dist_c = t[:, None] - pos_c[None, :]
        s_c = jnp.einsum('bqghd,bcgd->bghqc', qblk, kc) - sl * dist_c.astype(F32)
        p_c = masked_softmax(s_c, dist_c >= 0)
        o_c = jnp.einsum('bghqc,bcgd->bqghd', p_c, vc)
        imp = jnp.einsum('bghqc,cj->bqgj', p_c, overlap)
        cur = t // SEL_BLOCK
        causal_j = jsel[None, :] <= cur[:, None]
        forced = (jsel[None, :] == 0) | (jsel[None, :] == cur[:, None]) | (jsel[None, :] == cur[:, None] - 1)
        score = jnp.where(forced[:, None, :], 1e6, jnp.where(causal_j[:, None, :], imp, -1e6))
        _, idx = lax.top_k(score, top_n)
        idx = jnp.transpose(idx, (0, 2, 1, 3)).reshape(bsz, G, Q_BLOCK * top_n)
        k_sel = ks[bi, gi, idx].reshape(bsz, G, Q_BLOCK, top_n * SEL_BLOCK, DH)
        v_sel = vs[bi, gi, idx].reshape(bsz, G, Q_BLOCK, top_n * SEL_BLOCK, DH)
        kpos = (idx.reshape(bsz, G, Q_BLOCK, top_n, 1) * SEL_BLOCK + jnp.arange(SEL_BLOCK)).reshape(bsz, G, Q_BLOCK, top_n * SEL_BLOCK)
        dist_s = (t[None, None, :, None] - kpos)[:, :, None]
        s_s = jnp.einsum('bqghd,bgqkd->bghqk', qblk, k_sel) - sl * dist_s.astype(F32)
        p_s = masked_softmax(s_s, dist_s >= 0)
        o_s = jnp.einsum('bghqk,bgqkd->bqghd', p_s, v_sel)
        kwb = lax.dynamic_slice_in_dim(kw, t0, WINDOW + Q_BLOCK, axis=1)
        vwb = lax.dynamic_slice_in_dim(vw, t0, WINDOW + Q_BLOCK, axis=1)
        s_pos = t0 - WINDOW + jnp.arange(WINDOW + Q_BLOCK)
        dist_w = t[:, None] - s_pos[None, :]
        mask_w = (dist_w >= 0) & (dist_w < WINDOW) & (s_pos[None, :] >= 0)
        s_w = jnp.einsum('bqghd,bkgd->bghqk', qblk, kwb) - sl * dist_w.astype(F32)
        p_w = masked_softmax(s_w, mask_w)
        o_w = jnp.einsum('bghqk,bkgd->bqghd', p_w, vwb)
        o = gblk[:, :, 0, :, :, None] * o_c + gblk[:, :, 1, :, :, None] * o_s + gblk[:, :, 2, :, :, None] * o_w
        return o.reshape(bsz, Q_BLOCK, NSA_WIDTH)

    out = lax.map(block, jnp.arange(seq // Q_BLOCK))
    return jnp.transpose(out, (1, 0, 2, 3)).reshape(bsz, seq, NSA_WIDTH)


def spatial_gating_unit(u, v, norm_w, norm_b, w_s, b_s):
    bsz, seq, _ = u.shape
    u = jax.nn.gelu(u.astype(F32))
    vf = jax.nn.gelu(v.astype(F32))
    mu = jnp.mean(vf, axis=-1, keepdims=True)
    var = jnp.mean(jnp.square(vf - mu), axis=-1, keepdims=True)
    vn = (vf - mu) * lax.rsqrt(var + EPS) * norm_w.astype(F32) + norm_b.astype(F32)
    vn = vn.reshape(bsz, seq // SGU_CHUNK, SGU_CHUNK, SGU_GROUPS, HEAD_DIM)
    causal = np.tril(np.ones((SGU_CHUNK, SGU_CHUNK), dtype=bool))
    w = jnp.where(causal, w_s.astype(F32), 0.0)
    s = jnp.einsum('gij,bcjgd->bcigd', w, vn) + b_s.astype(F32).T[None, None, :, :, None]
    return u * s.reshape(bsz, seq, SGU_WIDTH)


def hybrid_layer(x, ffn1_norm, ffn1_gate, ffn1_up, ffn1_down, mix_norm, w_in,
                 dn_conv, dn_a_log, dn_dt_bias, dn_out_norm,
                 nsa_q_norm, nsa_k_norm, cmpk_pos, cmpk_w1, cmpk_w2, cmpv_pos, cmpv_w1, cmpv_w2,
                 sgu_norm_w, sgu_norm_b, sgu_w, sgu_b,
                 w_branch_a, w_branch_b, w_branch_c, w_out,
                 ffn2_norm, ffn2_gate, ffn2_up, ffn2_down):
    x = x + 0.5 * swiglu(rms_norm(x, ffn1_norm), ffn1_gate, ffn1_up, ffn1_down)
    h = rms_norm(x, mix_norm)
    (dq, dk, dv, dz, da, db, nq, nkc, nvc, nks, nvs, nkw, nvw, ngate,
     su, sv, ga, gb, gc) = jnp.split(h @ w_in, IN_OFFSETS, axis=-1)
    y_a = gated_deltanet(dq, dk, dv, dz, da, db, dn_conv, dn_a_log, dn_dt_bias, dn_out_norm)
    y_b = native_sparse_attention(nq, nkc, nvc, nks, nvs, nkw, nvw, ngate, nsa_q_norm, nsa_k_norm,
                                  cmpk_pos, cmpk_w1, cmpk_w2, cmpv_pos, cmpv_w1, cmpv_w2)
    y_c = spatial_gating_unit(su, sv, sgu_norm_w, sgu_norm_b, sgu_w, sgu_b)
    merged = (jax.nn.sigmoid(ga) * (y_a.astype(x.dtype) @ w_branch_a)
              + jax.nn.sigmoid(gb) * (y_b.astype(x.dtype) @ w_branch_b)
              + jax.nn.sigmoid(gc) * (y_c.astype(x.dtype) @ w_branch_c))
    x = x + merged @ w_out
    x = x + 0.5 * swiglu(rms_norm(x, ffn2_norm), ffn2_gate, ffn2_up, ffn2_down)
    return x


def setup_inputs(seed: int = 0) -> dict:
    key = jax.random.key(seed)
    ks = jax.random.split(key, 32)
    L = DEPTH

    def nrm(k, shape, scale):
        return jax.random.normal(k, shape, F32) * scale

    def gain(k, shape):
        return 1.0 + 0.02 * jax.random.normal(k, shape, F32)

    dt = jnp.exp(jax.random.uniform(ks[9], (L, DN_HEADS), F32, math.log(1e-3), math.log(1e-1)))
    return {
        'x': nrm(ks[0], (BATCH, SEQ, D_MODEL), 1.0),
        'ffn1_norm': gain(ks[1], (L, D_MODEL)),
        'ffn1_gate': nrm(ks[2], (L, D_MODEL, D_FF), D_MODEL ** -0.5),
        'ffn1_up': nrm(ks[3], (L, D_MODEL, D_FF), D_MODEL ** -0.5),
        'ffn1_down': nrm(ks[4], (L, D_FF, D_MODEL), D_FF ** -0.5),
        'mix_norm': gain(ks[5], (L, D_MODEL)),
        'w_in': nrm(ks[6], (L, D_MODEL, D_IN), D_MODEL ** -0.5),
        'dn_conv': nrm(ks[7], (L, DN_CONV, 3 * DN_WIDTH), DN_CONV ** -0.5),
        'dn_a_log': jnp.log(jax.random.uniform(ks[8], (L, DN_HEADS), F32, 1.0, 16.0)),
        'dn_dt_bias': dt + jnp.log(-jnp.expm1(-dt)),
        'dn_out_norm': gain(ks[10], (L, HEAD_DIM)),
        'nsa_q_norm': gain(ks[11], (L, HEAD_DIM)),
        'nsa_k_norm': gain(ks[12], (L, 3, HEAD_DIM)),
        'cmpk_pos': nrm(ks[13], (L, CMP_BLOCK, HEAD_DIM), 0.02),
        'cmpk_w1': nrm(ks[14], (L, CMP_BLOCK * HEAD_DIM, CMP_HIDDEN), (CMP_BLOCK * HEAD_DIM) ** -0.5),
        'cmpk_w2': nrm(ks[15], (L, CMP_HIDDEN, HEAD_DIM), CMP_HIDDEN ** -0.5),
        'cmpv_pos': nrm(ks[16], (L, CMP_BLOCK, HEAD_DIM), 0.02),
        'cmpv_w1': nrm(ks[17], (L, CMP_BLOCK * HEAD_DIM, CMP_HIDDEN), (CMP_BLOCK * HEAD_DIM) ** -0.5),
        'cmpv_w2': nrm(ks[18], (L, CMP_HIDDEN, HEAD_DIM), CMP_HIDDEN ** -0.5),
        'sgu_norm_w': gain(ks[19], (L, SGU_WIDTH)),
        'sgu_norm_b': nrm(ks[20], (L, SGU_WIDTH), 0.02),
        'sgu_w': nrm(ks[21], (L, SGU_GROUPS, SGU_CHUNK, SGU_CHUNK), SGU_CHUNK ** -0.5),
        'sgu_b': gain(ks[22], (L, SGU_GROUPS, SGU_CHUNK)),
        'w_branch_a': nrm(ks[23], (L, DN_WIDTH, D_MODEL), DN_WIDTH ** -0.5),
        'w_branch_b': nrm(ks[24], (L, NSA_WIDTH, D_MODEL), NSA_WIDTH ** -0.5),
        'w_branch_c': nrm(ks[25], (L, SGU_WIDTH, D_MODEL), SGU_WIDTH ** -0.5),
        'w_out': nrm(ks[26], (L, D_MODEL, D_MODEL), D_MODEL ** -0.5),
        'ffn2_norm': gain(ks[27], (L, D_MODEL)),
        'ffn2_gate': nrm(ks[28], (L, D_MODEL, D_FF), D_MODEL ** -0.5),
        'ffn2_up': nrm(ks[29], (L, D_MODEL, D_FF), D_MODEL ** -0.5),
        'ffn2_down': nrm(ks[30], (L, D_FF, D_MODEL), D_FF ** -0.5),
    }


def reference(x, ffn1_norm, ffn1_gate, ffn1_up, ffn1_down, mix_norm, w_in,
              dn_conv, dn_a_log, dn_dt_bias, dn_out_norm,
              nsa_q_norm, nsa_k_norm, cmpk_pos, cmpk_w1, cmpk_w2, cmpv_pos, cmpv_w1, cmpv_w2,
              sgu_norm_w, sgu_norm_b, sgu_w, sgu_b,
              w_branch_a, w_branch_b, w_branch_c, w_out,
              ffn2_norm, ffn2_gate, ffn2_up, ffn2_down):
    for l in range(DEPTH):
        x = hybrid_layer(x, ffn1_norm[l], ffn1_gate[l], ffn1_up[l], ffn1_down[l], mix_norm[l], w_in[l],
                         dn_conv[l], dn_a_log[l], dn_dt_bias[l], dn_out_norm[l],
                         nsa_q_norm[l], nsa_k_norm[l], cmpk_pos[l], cmpk_w1[l], cmpk_w2[l],
                         cmpv_pos[l], cmpv_w1[l], cmpv_w2[l],
                         sgu_norm_w[l], sgu_norm_b[l], sgu_w[l], sgu_b[l],
                         w_branch_a[l], w_branch_b[l], w_branch_c[l], w_out[l],
                         ffn2_norm[l], ffn2_gate[l], ffn2_up[l], ffn2_down[l])
    return x
```

```python
import numpy as np
from contextlib import ExitStack
import concourse.bass as bass
import concourse.mybir as mybir
from concourse.bass_utils import run_bass_kernel_spmd

F32 = mybir.dt.float32
BF16 = mybir.dt.bfloat16
ALU = mybir.AluOpType
AF = mybir.ActivationFunctionType
AX = mybir.AxisListType

D = 2048
KC = D // 128
FF = 5504
FC = FF // 128
EPS = 1e-6


class Buf:
    __slots__ = ("name", "w", "r")

    def __init__(self, name=""):
        self.name = name
        self.w = None
        self.r = {}


class Ins:
    __slots__ = ("eng", "fn", "deps", "tick", "dma", "sem", "val", "needs_inc")

    def __init__(self, eng, fn):
        self.eng = eng
        self.fn = fn
        self.deps = ()
        self.tick = 0
        self.dma = False
        self.sem = None
        self.val = 0
        self.needs_inc = False


class Sched:
    ENG = ("pe", "act", "dve", "pool", "sp")

    def __init__(self, nc, stack):
        self.nc = nc
        self.stack = stack
        self.q = {e: [] for e in self.ENG}
        self.engsem = {e: stack.enter_context(nc.semaphore("es_" + e)) for e in self.ENG}
        self.dmasem = {}
        self.last_dma = {}
        self.pending = {}
        self.freesem = []
        self.nsem = 0

    def _track(self, ins, reads, writes):
        deps = {}
        for b in reads:
            d = b.w
            if d is not None:
                deps[id(d)] = d
        for b in writes:
            d = b.w
            if d is not None:
                deps[id(d)] = d
            for d in b.r.values():
                deps[id(d)] = d
        extra = self.pending.pop(ins.eng, None)
        if extra:
            for d in extra:
                deps[id(d)] = d
        deps.pop(id(ins), None)
        dl = list(deps.values())
        if ins.eng == "pe":
            dl = [d for d in dl if d.dma or d.eng != "pe"]
        ins.deps = dl
        key = ("d", id(ins.sem)) if ins.dma else ins.eng
        for b in reads:
            b.r[key] = ins
        for b in writes:
            b.w = ins
            b.r = {}
        self.q[ins.eng].append(ins)

    def op(self, eng, fn, reads=(), writes=()):
        ins = Ins(eng, fn)
        self._track(ins, reads, writes)
        return ins

    def dma(self, queue, fn, semkey, reads=(), writes=()):
        ins = Ins(queue, fn)
        ins.dma = True
        ent = self.dmasem.get(semkey)
        if ent is None:
            if self.freesem:
                ent = self.freesem.pop()
            else:
                ent = [self.stack.enter_context(self.nc.semaphore("ds%d" % self.nsem)), 0]
                self.nsem += 1
            self.dmasem[semkey] = ent
        ent[1] += 16
        ins.sem = ent[0]
        ins.val = ent[1]
        self._track(ins, reads, writes)
        self.last_dma[id(ins.sem)] = ins
        return ins

    def barrier(self):
        deps = list(self.last_dma.values())
        for e in self.ENG:
            if self.q[e]:
                for ins in reversed(self.q[e]):
                    if not ins.dma and ins.fn is not None:
                        deps.append(ins)
                        break
        for e in self.ENG:
            self.pending[e] = list(deps) + list(self.pending.get(e, []))
        self.freesem.extend(self.dmasem.values())
        self.dmasem = {}

    def finalize(self, final_eng="pool"):
        fin = Ins(final_eng, None)
        fin.deps = list(self.last_dma.values())
        self.q[final_eng].append(fin)
        for e in self.ENG:
            for ins in self.q[e]:
                for d in ins.deps:
                    if not d.dma:
                        d.needs_inc = True
        for e in self.ENG:
            t = 0
            for ins in self.q[e]:
                if ins.needs_inc and not ins.dma:
                    t += 1
                    ins.tick = t
        nc = self.nc
        names = {"pe": "tensor", "act": "scalar", "dve": "vector", "pool": "gpsimd", "sp": "sync"}
        stats = {}
        engsem = self.engsem
        with nc.Block() as block:
            for e in self.ENG:
                def body(eng, q=self.q[e], e=e):
                    obs = {}
                    nw = 0
                    for ins in q:
                        for d in ins.deps:
                            if d.dma:
                                sem, val = d.sem, d.val
                            else:
                                sem, val = engsem[d.eng], d.tick
                            k = id(sem)
                            if obs.get(k, 0) < val:
                                eng.wait_ge(sem, val)
                                obs[k] = val
                                nw += 1
                        if ins.fn is None:
                            continue
                        r = ins.fn(eng)
                        if ins.dma:
                            r.then_inc(ins.sem, 16)
                        elif ins.needs_inc:
                            r.then_inc(engsem[e], 1)
                    stats[e] = (len(q), nw)
                getattr(block, names[e])(body)
        return stats


NWIN = 116
AW = 50000
SLOPES = [2.0 ** (-(h + 1)) for h in range(8)]


class MK:
    def __init__(self, S, NL, T=512, taps=(), stages=None):
        self.S = S
        self.NL = NL
        self.T = T
        self.taps = set(taps)
        self.stages = stages
        self.nc = bass.Bass("TRN2", target_bir_lowering=False)
        self.stack = ExitStack()
        self.sch = Sched(self.nc, self.stack)
        self.bufs = {}
        self.cnt = {}
        self.stage = "init"
        self.arena = self.sb("arena", [128, AW], F32)
        self.aoff = 0
        self.psall = self.ps("psall", [128, 4096], F32)

    def dram_in(self, name, shape, dt=F32):
        return self.nc.dram_tensor(name, list(shape), dt, kind="ExternalInput").ap()

    def dram_out(self, name, shape, dt=F32):
        return self.nc.dram_tensor(name, list(shape), dt, kind="ExternalOutput").ap()

    def dram_tmp(self, name, shape, dt=F32):
        kind = "ExternalOutput" if name in self.taps else "Internal"
        return self.nc.dram_tensor(name, list(shape), dt, kind=kind).ap()

    def sb(self, name, shape, dt=F32):
        return self.stack.enter_context(self.nc.sbuf_tensor("sb_" + name, list(shape), dt))

    def ps(self, name, shape, dt=F32):
        return self.stack.enter_context(self.nc.psum_tensor("ps_" + name, list(shape), dt))

    def carve(self, shape, dt=F32):
        n = 1
        for d in shape[1:]:
            n *= d
        words = (n + 1) // 2 if dt == BF16 else n
        words = (words + 7) // 8 * 8
        assert self.aoff + words <= AW, ("arena overflow", self.stage, self.aoff, words)
        v = self.arena[0:shape[0], self.aoff:self.aoff + words]
        self.aoff += words
        if dt == BF16:
            v = v.bitcast(BF16)
        v = v[:, 0:n]
        if len(shape) == 3:
            v = v.rearrange("p (a b) -> p a b", a=shape[1])
        elif len(shape) == 4:
            v = v.rearrange("p (a b c) -> p a b c", a=shape[1], b=shape[2])
        return v

    def bank(self, i, n=1):
        return self.psall[:, i * 512:(i + n) * 512]

    def stage_begin(self, name):
        self.sch.barrier()
        self.stage = name
        self.aoff = 0
        self.cnt = {}

    def B(self, key):
        if not (isinstance(key, tuple) and key[0] == "G"):
            key = (self.stage, key)
        b = self.bufs.get(key)
        if b is None:
            b = Buf(str(key))
            self.bufs[key] = b
        return b

    def Bs(self, keys):
        return [k if isinstance(k, Buf) else self.B(k) for k in keys]

    def nxt(self, key, n=2):
        v = self.cnt.get(key, 0)
        self.cnt[key] = v + 1
        return v % n

    def op(self, eng, method, *args, r=(), w=(), **kw):
        self.sch.op(eng, lambda e: getattr(e, method)(*args, **kw), self.Bs(r), self.Bs(w))

    def mm(self, out, lhsT, rhs, start, stop, r, w, skip=False):
        if skip:
            self.sch.op("pe", lambda e: e.matmul(out, lhsT, rhs, start=start, stop=stop, skip_group_check=True),
                        self.Bs(r), self.Bs(w))
        else:
            self.sch.op("pe", lambda e: e.matmul(out, lhsT, rhs, start=start, stop=stop), self.Bs(r), self.Bs(w))

    def tr(self, out, in_, ident, r, w):
        self.sch.op("pe", lambda e: e.transpose(out, in_, ident), self.Bs(r), self.Bs(w))

    def act(self, out, in_, func, r, w, bias=None, scale=1.0):
        if bias is None:
            self.sch.op("act", lambda e: e.activation(out, in_, func, scale=scale), self.Bs(r), self.Bs(w))
        else:
            self.sch.op("act", lambda e: e.activation(out, in_, func, bias=bias, scale=scale),
                        self.Bs(r), self.Bs(w))

    def tt(self, out, in0, in1, op, r, w, eng="dve"):
        self.sch.op(eng, lambda e: e.tensor_tensor(out, in0, in1, op), self.Bs(r), self.Bs(w))

    def ts(self, out, in0, s1, s2, op0, op1, r, w, eng="dve"):
        if op1 is None:
            self.sch.op(eng, lambda e: e.tensor_scalar(out, in0, s1, None, op0), self.Bs(r), self.Bs(w))
        else:
            self.sch.op(eng, lambda e: e.tensor_scalar(out, in0, s1, s2, op0, op1), self.Bs(r), self.Bs(w))

    def stt(self, out, in0, scalar, in1, op0, op1, r, w, eng="dve"):
        self.sch.op(eng, lambda e: e.scalar_tensor_tensor(out=out, in0=in0, scalar=scalar, in1=in1,
                                                          op0=op0, op1=op1), self.Bs(r), self.Bs(w))

    def cp(self, out, in_, r, w, eng="dve"):
        if eng == "act":
            self.sch.op("act", lambda e: e.copy(out, in_), self.Bs(r), self.Bs(w))
        else:
            self.sch.op(eng, lambda e: e.tensor_copy(out, in_), self.Bs(r), self.Bs(w))

    def memset(self, out, val, w, eng="pool"):
        self.sch.op(eng, lambda e: e.memset(out, val), (), self.Bs(w))

    def asel(self, out, in_, pattern, cmp, fill, base, cm, r, w):
        self.sch.op("pool", lambda e: e.affine_select(out=out, in_=in_, pattern=pattern, compare_op=cmp,
                                                      fill=fill, base=base, channel_multiplier=cm),
                    self.Bs(r), self.Bs(w))

    def load(self, out, in_, semkey, r, w, queue="sp"):
        self.sch.dma(queue, lambda e: e.dma_start(out=out, in_=in_), (self.stage, semkey), self.Bs(r), self.Bs(w))

    def store(self, out, in_, semkey, r, w, queue="sp"):
        self.sch.dma(queue, lambda e: e.dma_start(out=out, in_=in_), (self.stage, semkey), self.Bs(r), self.Bs(w))

    def setup_consts(self):
        G = lambda n: ("G", n)
        self.ones_bf = self.sb("ones_bf", [128, 128], BF16)
        self.memset(self.ones_bf[:], 1.0, [G("ones_bf")])
        self.ones_f = self.sb("ones_f", [128, 128], F32)
        self.memset(self.ones_f[:], 1.0, [G("ones_f")])
        self.eps_t = self.sb("eps_t", [128, 1], F32)
        self.memset(self.eps_t[:], EPS, [G("eps")])
        self.ident_f = self.sb("ident_f", [128, 128], F32)
        self.memset(self.ident_f[:], 1.0, [G("ident_f")])
        self.asel(self.ident_f[:], self.ident_f[:], [[-1, 128]], ALU.is_equal, 0.0, 0, 1, [G("ident_f")], [G("ident_f")])
        self.ident_b = self.sb("ident_b", [128, 128], BF16)
        self.cp(self.ident_b[:], self.ident_f[:], [G("ident_f")], [G("ident_b")], eng="pool")
        NG = 3 * self.NL
        self.gam = self.sb("gam", [128, NG, KC], F32)
        self.load(self.gam[:], self.I["gam_in"], "gam", (), [G("gam")])
        self.n128 = self.sb("n128", [128, self.NL, 72], F32)
        self.load(self.n128[:], self.I["n128"].rearrange("l p c -> p l c"), "n128", (), [G("n128")])
        self.qnw = self.sb("qnw", [128, self.NL], F32)
        for l in range(self.NL):
            self.ts(self.qnw[:, l:l + 1], self.n128[:, l, 1:2], 128.0 ** -0.5, None, ALU.mult, None,
                    [G("n128")], [G("qnw")])

    def alloc_rl(self):
        T = self.T
        c = self.carve
        self.xT = c([128, KC, T], F32)
        self.hT = c([128, KC, T], BF16)
        self.actflat = self.arena[:, self.aoff:self.aoff + FC * T // 2]
        self.actT = c([128, FC, T], BF16)
        self.uvT = self.actflat[:, 0:16 * T].rearrange("p (a b) -> p a b", a=16)
        self.wA = [c([128, KC, 128], BF16) for _ in range(2)]
        self.wB = [c([128, KC, 128], BF16) for _ in range(2)]
        self.wD = [c([128, FC, 128], BF16) for _ in range(2)]
        self.sq = [c([128, T], BF16) for _ in range(2)]
        self.sg = [c([128, T], F32) for _ in range(2)]
        self.rstd = c([128, T], F32)
        self.gt = [c([128, 3, T], F32) for _ in range(2)]
        self.ost = [c([128, T], F32) for _ in range(4)]
        self.cv = [c([128, T + 3], F32) for _ in range(2)]
        self.halo = c([128, 24, 3], F32)
        self.cvt = [c([128, T], F32) for _ in range(2)]
        self.convw = c([128, 24, 4], F32)
        self.wsm = c([128, KC, 64], BF16)
        self.smo = [c([128, 4, 64], F32) for _ in range(2)]
        self.smt = c([128, 32], F32)
        self.dnv = c([128, 16], F32)
        self.nA = c([128, 8], F32)
        self.sgnw = c([128, 16], F32)
        self.wsT = c([128, 8, 128], BF16)
        self.bsb = c([128, 8, 128], F32)
        self.lnst = c([128, 4, T], F32)
        self.wsTf = self.lnst[:, 0:2, :].rearrange("p a (b c) -> p (a b) c", c=128)
        self.vtok = [c([128, 4, 128], BF16) for _ in range(2)]
        self.ycT = c([128, 8, T], BF16)
        self.psM = [self.bank(i) for i in range(4)]
        self.psD = [self.bank(4), self.bank(5)]
        self.psS = self.bank(6)
        self.psX = self.bank(7)

    def xb(self):
        return [("xT", k) for k in range(KC)]

    def rmsnorm(self, gi):
        for kc in range(KC):
            s = self.nxt("sq")
            self.act(self.sq[s], self.xT[:, kc, :], AF.Square, [("xT", kc)], [("sq", s)])
            self.mm(self.psS, self.ones_bf[:], self.sq[s], kc == 0, kc == KC - 1, [("sq", s)], ["psS"])
        self.act(self.rstd, self.psS, AF.Sqrt, ["psS"], ["rstd"], bias=self.eps_t[:, 0:1], scale=1.0 / D)
        self.op("dve", "reciprocal", self.rstd, self.rstd, r=["rstd"], w=["rstd"])
        for kc in range(KC):
            self.stt(self.hT[:, kc, :], self.xT[:, kc, :], self.gam[:, gi, kc:kc + 1], self.rstd,
                     ALU.mult, ALU.mult, [("xT", kc), "rstd"], ["hT"])

    def ffn(self, gi, wg, wu, wd):
        self.rmsnorm(gi)
        for fc in range(FC):
            s = self.nxt("wAB")
            self.load(self.wA[s], wg[fc], ("wA", s), (), [("wA", s)], queue="pool")
            self.load(self.wB[s], wu[fc], ("wB", s), (), [("wB", s)], queue="pool")
            p = self.nxt("psM", 2)
            pa, pb = self.psM[2 * p], self.psM[2 * p + 1]
            ka, kb = ("psM", 2 * p), ("psM", 2 * p + 1)
            for kc in range(KC):
                self.mm(pa, self.wA[s][:, kc, :], self.hT[:, kc, :], kc == 0, kc == KC - 1, [("wA", s), "hT"], [ka])
            for kc in range(KC):
                self.mm(pb, self.wB[s][:, kc, :], self.hT[:, kc, :], kc == 0, kc == KC - 1, [("wB", s), "hT"], [kb])
            g = self.nxt("sg")
            self.act(self.sg[g], pa, AF.Silu, [ka], [("sg", g)])
            self.tt(self.actT[:, fc, :], self.sg[g], pb, ALU.mult, [("sg", g), kb], [("actT", fc)])
        for dc in range(KC):
            s = self.nxt("wD")
            self.load(self.wD[s], wd[dc], ("wD", s), (), [("wD", s)], queue="pool")
            p = self.nxt("psD")
            for fc in range(FC):
                self.mm(self.psD[p], self.wD[s][:, fc, :], self.actT[:, fc, :], fc == 0, fc == FC - 1,
                        [("wD", s), ("actT", fc)], [("psD", p)])
            self.stt(self.xT[:, dc, :], self.psD[p], 0.5, self.xT[:, dc, :], ALU.mult, ALU.add,
                     [("psD", p), ("xT", dc)], [("xT", dc)])

    def merge(self, l, t0):
        T = self.T
        yT = self.actT
        yv = self.D["yT"].rearrange("(k p) s -> p k s", p=128)
        self.load(yT[:, 0:24, :], yv[:, :, t0:t0 + T], "yT", [("G", "yT", t0 // T)], [("actT", k) for k in range(24)])
        gv = self.D["gates"].rearrange("(b k p) s -> p b k s", b=3, p=128)
        wbr = self.I["w_br_t"]
        for dc in range(KC):
            gs = self.nxt("gt")
            self.load(self.gt[gs], gv[:, :, dc, t0:t0 + T], ("gt", gs), [("G", "gates", t0 // T)], [("gt", gs)])
            pks = []
            for br in range(3):
                if br < 2:
                    s = self.nxt("wAB")
                    wt = (self.wA, self.wB)[br][s]
                    wk = (("wA", s), ("wB", s))[br]
                else:
                    s = self.nxt("wAB")
                    wt = self.wA[s]
                    wk = ("wA", s)
                self.load(wt[:, 0:8, :], wbr[l, br, dc], wk, (), [wk], queue="pool")
                p = self.nxt("psM4", 4)
                for kc in range(8):
                    self.mm(self.psM[p], wt[:, kc, :], yT[:, br * 8 + kc, :], kc == 0, kc == 7,
                            [wk, ("actT", br * 8 + kc)], [("psM", p)])
                pks.append(p)
            a = self.nxt("ost", 4)
            b = self.nxt("ost", 4)
            self.tt(self.ost[a], self.psM[pks[0]], self.gt[gs][:, 0, :], ALU.mult, [("psM", pks[0]), ("gt", gs)], [("ost", a)])
            self.tt(self.ost[b], self.psM[pks[1]], self.gt[gs][:, 1, :], ALU.mult, [("psM", pks[1]), ("gt", gs)], [("ost", b)])
            self.tt(self.ost[a], self.ost[a], self.ost[b], ALU.add, [("ost", a), ("ost", b)], [("ost", a)])
            self.tt(self.ost[b], self.psM[pks[2]], self.gt[gs][:, 2, :], ALU.mult, [("psM", pks[2]), ("gt", gs)], [("ost", b)])
            self.tt(self.hT[:, dc, :], self.ost[a], self.ost[b], ALU.add, [("ost", a), ("ost", b)], ["hT"])
        wo = self.I["w_out_t"]
        for dc in range(KC):
            s = self.nxt("wAB")
            self.load(self.wA[s], wo[l, dc], ("wA", s), (), [("wA", s)], queue="pool")
            p = self.nxt("psD")
            for kc in range(KC):
                self.mm(self.psD[p], self.wA[s][:, kc, :], self.hT[:, kc, :], kc == 0, kc == KC - 1,
                        [("wA", s), "hT"], [("psD", p)])
            self.tt(self.xT[:, dc, :], self.psD[p], self.xT[:, dc, :], ALU.add, [("psD", p), ("xT", dc)], [("xT", dc)])

    def win_setup(self, l):
        self.load(self.convw, self.I["dn_conv_t"][l], "convw", (), ["convw"])
        self.load(self.wsm, self.I["w_sm_t"][l], "wsm", (), ["wsm"], queue="pool")
        self.load(self.dnv, self.I["dn_vec"][l].partition_broadcast(128), "dnv", (), ["dnv"])
        self.act(self.nA, self.dnv[:, 0:8], AF.Exp, ["dnv"], ["nA"])
        self.ts(self.nA, self.nA, -1.0, None, ALU.mult, None, ["nA"], ["nA"])
        self.load(self.sgnw, self.I["sgu_nw"][l], "sgnw", (), ["sgnw"])
        self.load(self.wsTf, self.I["sgu_wT"][l], "wsTf", (), ["wsTf", "ln_mu", "ln_var"])
        for g in range(8):
            self.asel(self.wsTf[:, g, :], self.wsTf[:, g, :], [[1, 128]], ALU.is_ge, 0.0, 0, -1, ["wsTf"], ["wsTf", "ln_mu", "ln_var"])
        self.cp(self.wsT, self.wsTf, ["wsTf", "ln_mu", "ln_var"], ["wsT"], eng="pool")
        self.load(self.bsb, self.I["sgu_b"][l].rearrange("g i -> (g i)").partition_broadcast(128)
                  .rearrange("p (g i) -> p g i", g=8), "bsb", (), ["bsb"])
        self.memset(self.halo, 0.0, ["halo"])

    def ostage_store(self, dst, src_key_slot, tix, dkey):
        a = src_key_slot
        self.store(dst, self.ost[a], ("ost", a), [("ost", a)], [("G", dkey, tix)])

    def win(self, l, t0):
        T = self.T
        tix = t0 // T
        Dm = self.D
        self.rmsnorm(3 * l + 1)
        so = self.nxt("smo")
        for sub in range(4):
            pX = self.psX[:, sub * 64:(sub + 1) * 64]
            for kc in range(KC):
                self.mm(pX, self.hT[:, kc, sub * 128:(sub + 1) * 128], self.wsm[:, kc, :], kc == 0, kc == KC - 1,
                        ["hT", "wsm"], ["psX"])
        pv = self.psX[:, 0:256].rearrange("p (s c) -> p s c", s=4)
        smt = self.smt.rearrange("p (s c) -> p s c", s=4)
        self.tt(smt, pv[:, :, 0:8], self.dnv[:, 8:16].unsqueeze(1).to_broadcast([128, 4, 8]), ALU.add,
                ["psX", "dnv"], ["smt"])
        self.act(smt, smt, AF.Exp, ["smt"], ["smt"])
        self.act(smt, smt, AF.Ln, ["smt"], ["smt"], bias=1.0)
        self.tt(self.smo[so][:, :, 0:8], smt, self.nA.unsqueeze(1).to_broadcast([128, 4, 8]), ALU.mult,
                ["smt", "nA"], [("smo", so)])
        self.act(self.smo[so][:, :, 8:40], pv[:, :, 8:40], AF.Sigmoid, ["psX"], [("smo", so)])
        self.store(Dm["sm_tok"][t0:t0 + T, :].rearrange("(s p) c -> p s c", p=128), self.smo[so],
                   ("smo", so), [("smo", so)], [("G", "sm_tok", tix)])
        win_t = self.I["w_in_t"]
        uT = self.uvT[:, 0:8, :]
        vfT = self.uvT[:, 8:16, :]
        uvk = lambda j: [("actT", 2 * j), ("actT", 2 * j + 1)]
        for c in range(NWIN):
            s = self.nxt("wAB")
            self.load(self.wA[s], win_t[l, c], ("wA", s), (), [("wA", s)], queue="pool")
            p = self.nxt("psM4", 4)
            ps, pk = self.psM[p], ("psM", p)
            for kc in range(KC):
                self.mm(ps, self.wA[s][:, kc, :], self.hT[:, kc, :], kc == 0, kc == KC - 1, [("wA", s), "hT"], [pk])
            if c < 24:
                v = self.nxt("cv")
                cv = self.cv[v]
                self.cp(cv[:, 3:T + 3], ps, [pk], [("cv", v)], eng="act")
                self.cp(cv[:, 0:3], self.halo[:, c, :], ["halo"], [("cv", v)], eng="pool")
                t = self.nxt("cvt")
                ct = self.cvt[t]
                self.ts(ct, cv[:, 0:T], self.convw[:, c, 0:1], None, ALU.mult, None, [("cv", v), "convw"], [("cvt", t)])
                for j in range(1, 4):
                    self.stt(ct, cv[:, j:T + j], self.convw[:, c, j:j + 1], ct, ALU.mult, ALU.add,
                             [("cv", v), "convw", ("cvt", t)], [("cvt", t)])
                self.cp(self.halo[:, c, :], cv[:, T:T + 3], [("cv", v)], ["halo"], eng="pool")
                a = self.nxt("ost", 4)
                if c < 16:
                    self.act(ct, ct, AF.Silu, [("cvt", t)], [("cvt", t)])
                    q = self.nxt("sq")
                    self.act(self.sq[q], ct, AF.Square, [("cvt", t)], [("sq", q)])
                    self.mm(self.psS, self.ones_bf[:], self.sq[q], True, True, [("sq", q)], ["psS"])
                    self.act(self.rstd, self.psS, AF.Sqrt, ["psS"], ["rstd"], bias=self.eps_t[:, 0:1])
                    self.op("dve", "reciprocal", self.rstd, self.rstd, r=["rstd"], w=["rstd"])
                    self.stt(self.ost[a], ct, (128.0 ** -0.5) if c < 8 else 1.0, self.rstd, ALU.mult, ALU.mult,
                             [("cvt", t), "rstd"], [("ost", a)])
                else:
                    self.act(self.ost[a], ct, AF.Silu, [("cvt", t)], [("ost", a)])
                self.store(Dm["dn_qkv"][c * 128:(c + 1) * 128, t0:t0 + T], self.ost[a], ("ost", a),
                           [("ost", a)], [("G", "dn_qkv", tix)])
            elif c < 32:
                a = self.nxt("ost", 4)
                self.act(self.ost[a], ps, AF.Silu, [pk], [("ost", a)])
                self.store(Dm["dn_zg"][(c - 24) * 128:(c - 23) * 128, t0:t0 + T], self.ost[a], ("ost", a),
                           [("ost", a)], [("G", "dn_zg", tix)])
            elif c < 40 or c in (44, 45, 48, 49):
                q = self.nxt("sq")
                self.act(self.sq[q], ps, AF.Square, [pk], [("sq", q)])
                self.mm(self.psS, self.ones_bf[:], self.sq[q], True, True, [("sq", q)], ["psS"])
                self.act(self.rstd, self.psS, AF.Sqrt, ["psS"], ["rstd"], bias=self.eps_t[:, 0:1], scale=1.0 / 128)
                self.op("dve", "reciprocal", self.rstd, self.rstd, r=["rstd"], w=["rstd"])
                a = self.nxt("ost", 4)
                ob = self.ost[a].bitcast(BF16)[:, 0:T]
                if c < 40:
                    gcol = self.qnw[:, l:l + 1]
                    dst = Dm["ns_q"][(c - 32) * 128:(c - 31) * 128, t0:t0 + T]
                    dk = "ns_q"
                else:
                    kn = 3 if c < 48 else 4
                    gcol = self.n128[:, l, kn:kn + 1]
                    row = {44: 0, 45: 1, 48: 2, 49: 3}[c]
                    dst = Dm["ns_k"][row * 128:(row + 1) * 128, t0:t0 + T]
                    dk = "ns_k"
                self.stt(ob, ps, gcol, self.rstd, ALU.mult, ALU.mult, [pk, "rstd", ("G", "n128"), ("G", "qnw")], [("ost", a)])
                self.store(dst, ob, ("ost", a), [("ost", a)], [("G", dk, tix)])
            elif c < 44:
                a = self.nxt("ost", 4)
                self.cp(self.ost[a], ps, [pk], [("ost", a)], eng="act")
                self.store(Dm["ns_kvc"][(c - 40) * 128:(c - 39) * 128, t0:t0 + T], self.ost[a], ("ost", a),
                           [("ost", a)], [("G", "ns_kvc", tix)])
            elif c < 52:
                a = self.nxt("ost", 4)
                self.cp(self.ost[a], ps, [pk], [("ost", a)], eng="act")
                for sub in range(4):
                    self.tr(self.psX[:, sub * 128:(sub + 1) * 128], self.ost[a][:, sub * 128:(sub + 1) * 128],
                            self.ident_f[:], [("ost", a)], ["psX"])
                vt = self.nxt("vtok")
                self.cp(self.vtok[vt], self.psX.rearrange("p (s c) -> p s c", s=4), ["psX"], [("vtok", vt)])
                row = {46: 0, 47: 1, 50: 2, 51: 3}[c]
                self.store(Dm["ns_v"][row, t0:t0 + T, :].rearrange("(s p) d -> p s d", p=128), self.vtok[vt],
                           ("vtok", vt), [("vtok", vt)], [("G", "ns_v", tix)])
            elif c < 60:
                g = c - 52
                self.act(uT[:, g, :], ps, AF.Gelu_apprx_tanh, [pk], uvk(g))
            elif c < 68:
                g = c - 60
                self.act(vfT[:, g, :], ps, AF.Gelu_apprx_tanh, [pk], uvk(8 + g))
                q = self.nxt("sq")
                self.cp(self.sq[q], vfT[:, g, :], uvk(8 + g), [("sq", q)])
                self.mm(self.psD[0], self.ones_bf[:], self.sq[q], g == 0, g == 7, [("sq", q)], [("psD", 0)])
                q = self.nxt("sq")
                self.act(self.sq[q], vfT[:, g, :], AF.Square, uvk(8 + g), [("sq", q)])
                self.mm(self.psD[1], self.ones_bf[:], self.sq[q], g == 0, g == 7, [("sq", q)], [("psD", 1)])
                if g == 7:
                    self.sgu_finish(l, t0)
            else:
                a = self.nxt("ost", 4)
                self.act(self.ost[a], ps, AF.Sigmoid, [pk], [("ost", a)])
                self.store(Dm["gates"][(c - 68) * 128:(c - 67) * 128, t0:t0 + T], self.ost[a], ("ost", a),
                           [("ost", a)], [("G", "gates", tix)])

    def sgu_finish(self, l, t0):
        T = self.T
        tix = t0 // T
        uT = self.uvT[:, 0:8, :]
        vfT = self.uvT[:, 8:16, :]
        uvk = lambda j: [("actT", 2 * j), ("actT", 2 * j + 1)]
        mu, var, rs, tmp = (self.lnst[:, i, :] for i in range(4))
        self.ts(mu, self.psD[0], 1.0 / 1024, None, ALU.mult, None, [("psD", 0)], ["ln_mu"])
        self.tt(var, mu, mu, ALU.mult, ["ln_mu"], ["ln_var"])
        self.stt(var, self.psD[1], 1.0 / 1024, var, ALU.mult, ALU.subtract, [("psD", 1), "ln_var"], ["ln_var"])
        self.act(rs, var, AF.Sqrt, ["ln_var"], ["ln_rs"], bias=self.eps_t[:, 0:1])
        self.op("dve", "reciprocal", rs, rs, r=["ln_rs"], w=["ln_rs"])
        for g in range(8):
            v = self.nxt("cvt")
            vn = self.cvt[v]
            self.tt(vn, vfT[:, g, :], mu, ALU.subtract, uvk(8 + g) + ["ln_mu"], [("cvt", v)])
            self.tt(vn, vn, rs, ALU.mult, [("cvt", v), "ln_rs"], [("cvt", v)])
            self.ts(vn, vn, self.sgnw[:, g:g + 1], self.sgnw[:, 8 + g:9 + g], ALU.mult, ALU.add,
                    [("cvt", v), "sgnw"], [("cvt", v)])
            for sub in range(4):
                self.tr(self.psX[:, sub * 128:(sub + 1) * 128], vn[:, sub * 128:(sub + 1) * 128], self.ident_f[:],
                        [("cvt", v)], ["psX"])
            vt = self.nxt("vtok")
            self.cp(self.vtok[vt], self.psX.rearrange("p (s c) -> p s c", s=4), ["psX"], [("vtok", vt)], eng="act")
            p = self.nxt("psD")
            for sub in range(4):
                self.mm(self.psD[p][:, sub * 128:(sub + 1) * 128], self.vtok[vt][:, sub, :], self.wsT[:, g, :],
                        True, True, [("vtok", vt), "wsT"], [("psD", p)])
            y = self.nxt("sg")
            self.tt(self.sg[y].rearrange("p (s c) -> p s c", s=4), self.psD[p].rearrange("p (s c) -> p s c", s=4),
                    self.bsb[:, g, :].unsqueeze(1).to_broadcast([128, 4, 128]), ALU.add,
                    [("psD", p), "bsb"], [("sg", y)])
            self.tt(self.ycT[:, g, :], self.sg[y], uT[:, g, :], ALU.mult, [("sg", y)] + uvk(g), ["ycT"])
        yv = self.D["yT"].rearrange("(k p) s -> p k s", p=128)
        self.store(yv[:, 16:24, t0:t0 + T], self.ycT, "ycT", ["ycT"], [("G", "yT", tix)])

    def rl_stage(self, name, l_merge, l_ffn2, l_ffn1, l_win, src, dst, dst_key):
        S, T = self.S, self.T
        self.stage_begin(name)
        self.alloc_rl()
        if l_win is not None:
            self.win_setup(l_win)
        I = self.I
        srcv = src.rearrange("(k p) s -> p k s", p=128)
        dstv = dst.rearrange("(k p) s -> p k s", p=128)
        for ti in range(S // T):
            t0 = ti * T
            self.load(self.xT, srcv[:, :, t0:t0 + T], "xT", [("G", "xTd", ti)], self.xb())
            if l_merge is not None:
                self.merge(l_merge, t0)
            if l_ffn2 is not None:
                self.ffn(3 * l_ffn2 + 2, I["ffn2_gate_t"][l_ffn2], I["ffn2_up_t"][l_ffn2], I["ffn2_down_t"][l_ffn2])
            if l_ffn1 is not None:
                self.ffn(3 * l_ffn1 + 0, I["ffn1_gate_t"][l_ffn1], I["ffn1_up_t"][l_ffn1], I["ffn1_down_t"][l_ffn1])
            self.store(dstv[:, :, t0:t0 + T], self.xT, "xT", self.xb(), [("G", dst_key, ti)])
            if l_win is not None:
                self.win(l_win, t0)

    def pslot(self):
        i = self.nxt("pslot", 4)
        v = self.bank(2 * i, 2).rearrange("p (h c) -> p h c", h=8)
        return v, ("pslot", i)

    def pslot_bf(self):
        i = self.nxt("pslot", 4)
        v = self.bank(2 * i, 1).bitcast(BF16).rearrange("p (h c) -> p h c", h=8)
        return v, ("pslot", i)

    def dn_stage(self, l):
        S, T = self.S, self.T
        self.stage_begin("dn%d" % l)
        c = self.carve
        Dm = self.D
        H = 8
        sh = [128, H, 128]
        triC = c([128, 128], F32)
        pmask = c([128, 128], F32)
        sl01 = c([128, 128], F32)
        self.memset(triC, 1.0, ["triC"])
        self.asel(triC, triC, [[1, 128]], ALU.is_ge, 0.0, 0, -1, ["triC"], ["triC"])
        self.memset(pmask, 0.0, ["pmask"])
        self.asel(pmask, pmask, [[-1, 128]], ALU.is_ge, 30000.0, 0, 1, ["pmask"], ["pmask"])
        self.memset(sl01, 1.0, ["sl01"])
        self.asel(sl01, sl01, [[-1, 128]], ALU.is_gt, 0.0, 0, 1, ["sl01"], ["sl01"])
        St = c(sh, F32)
        Sb = c(sh, BF16)
        self.memset(St, 0.0, ["St"])
        self.memset(Sb, 0.0, ["Sb"])
        qT = [c(sh, F32) for _ in range(2)]
        kT = [c(sh, F32) for _ in range(2)]
        vT = [c(sh, F32) for _ in range(2)]
        zg = [c(sh, F32) for _ in range(2)]
        sm = [c([128, 64], F32) for _ in range(2)]
        gcs = c([128, 16], F32)
        sc = c([128, 40], F32)
        Gbc = c(sh, F32)
        qTb = c(sh, BF16)
        kTb = c(sh, BF16)
        dec = c(sh, F32)
        decs = c(sh, F32)
        t1 = c(sh, F32)
        Lp = [c(sh, F32) for _ in range(2)]
        Up = [c(sh, F32) for _ in range(2)]
        Xp = [c(sh, F32) for _ in range(2)]
        At = c(sh, BF16)
        AtT = c(sh, BF16)
        kbg = c(sh, F32)
        kdec = c(sh, BF16)
        vb = c(sh, F32)
        u = c(sh, F32)
        wTb = c(sh, F32)
        vnew = c(sh, BF16)
        o = c(sh, F32)
        osq = c(sh, F32)
        ss = c([128, 8], F32)
        ya = [c(sh, BF16) for _ in range(2)]
        onw = self.n128[:, l, 0:1]
        qv = Dm["dn_qkv"].rearrange("(t h d) s -> t d h s", t=3, h=H)
        zv = Dm["dn_zg"].rearrange("(h d) s -> d h s", h=H)
        yv = Dm["yT"].rearrange("(k p) s -> p k s", p=128)

        def bc(col):
            return col.unsqueeze(2).to_broadcast(sh)

        for n in range(S // 128):
            t0 = n * 128
            tix = t0 // T
            sl = n % 2
            self.load(qT[sl], qv[0][:, :, t0:t0 + 128], ("qT", sl), [("G", "dn_qkv", tix)], [("qT", sl)])
            self.load(kT[sl], qv[1][:, :, t0:t0 + 128], ("kT", sl), [("G", "dn_qkv", tix)], [("kT", sl)])
            self.load(vT[sl], qv[2][:, :, t0:t0 + 128], ("vT", sl), [("G", "dn_qkv", tix)], [("vT", sl)])
            self.load(zg[sl], zv[:, :, t0:t0 + 128], ("zg", sl), [("G", "dn_zg", tix)], [("zg", sl)])
            self.load(sm[sl], Dm["sm_tok"][t0:t0 + 128, :], ("sm", sl), [("G", "sm_tok", tix)], [("sm", sl)])
            g8 = sm[sl][:, 0:8]
            beta = sm[sl][:, 8:16]
            pG, kG = self.pslot()
            pg = pG[:, 0, 0:16]
            self.mm(pg[:, 0:8], triC, g8, True, True, ["triC", ("sm", sl)], [kG])
            self.mm(pg[:, 8:16], self.ones_f[:], g8, True, True, [("sm", sl)], [kG])
            self.cp(gcs, pg, [kG], ["gcs"])
            gc = gcs[:, 0:8]
            glb = gcs[:, 8:16]
            self.act(sc[:, 0:8], gc, AF.Exp, ["gcs"], ["sc"])
            self.tt(sc[:, 8:16], glb, gc, ALU.subtract, ["gcs"], ["sc"])
            self.act(sc[:, 8:16], sc[:, 8:16], AF.Exp, ["sc"], ["sc"])
            self.act(sc[:, 16:24], glb, AF.Exp, ["gcs"], ["sc"])
            self.tt(sc[:, 24:32], beta, sc[:, 0:8], ALU.mult, [("sm", sl), "sc"], ["sc"])
            egc, edl, egl, bg = sc[:, 0:8], sc[:, 8:16], sc[:, 16:24], sc[:, 24:32]
            self.cp(Gbc, bc(g8), [("sm", sl)], ["Gbc"], eng="pool")
            self.cp(qTb, qT[sl], [("qT", sl)], ["qTb"], eng="pool")
            self.cp(kTb, kT[sl], [("kT", sl)], ["kTb"])
            pR, kR = self.pslot()
            for h in range(H):
                self.mm(pR[:, h, :], Gbc[:, h, :], triC, True, False, ["Gbc", "triC"], [kR])
                self.mm(pR[:, h, :], self.ident_f[:], pmask, False, True, ["pmask"], [kR])
            self.tt(t1, pR, bc(gc), ALU.subtract, [kR, "gcs"], ["t1"])
            self.act(dec, t1, AF.Exp, ["t1"], ["dec"], scale=-1.0)
            self.tt(decs, dec, sl01.unsqueeze(1).to_broadcast(sh), ALU.mult, ["dec", "sl01"], ["decs"], eng="pool")
            pKK, kKK = self.pslot()
            for h in range(H):
                self.mm(pKK[:, h, :], kT[sl][:, h, :], kT[sl][:, h, :], True, True, [("kT", sl)], [kKK])
            pQK, kQK = self.pslot()
            for h in range(H):
                self.mm(pQK[:, h, :], qTb[:, h, :], kTb[:, h, :], True, True, ["qTb", "kTb"], [kQK])
            self.tt(t1, pKK, decs, ALU.mult, [kKK, "decs"], ["t1"])
            self.tt(Lp[0], t1, bc(beta), ALU.mult, ["t1", ("sm", sl)], [("Lp", 0)])
            self.tt(At, pQK, dec, ALU.mult, [kQK, "dec"], ["At"])
            pU, kU = self.pslot()
            for h in range(H):
                self.tr(pU[:, h, :], Lp[0][:, h, :], self.ident_f[:], [("Lp", 0)], [kU])
            self.cp(Up[0], pU, [kU], [("Up", 0)], eng="act")
            pA, kA = self.pslot_bf()
            for h in range(H):
                self.tr(pA[:, h, :], At[:, h, :], self.ident_b[:], ["At"], [kA])
            self.cp(AtT, pA, [kA], ["AtT"], eng="act")
            self.tt(Xp[0], self.ident_f[:].unsqueeze(1).to_broadcast(sh), Up[0], ALU.subtract, [("Up", 0)], [("Xp", 0)])
            cur = 0
            xc = 0
            for step in range(6):
                nx = 1 - cur
                pL, kL = self.pslot()
                for h in range(H):
                    self.mm(pL[:, h, :], Up[cur][:, h, :], Lp[cur][:, h, :], True, True, [("Up", cur), ("Lp", cur)], [kL])
                if step < 5:
                    pU2, kU2 = self.pslot()
                    for h in range(H):
                        self.mm(pU2[:, h, :], Lp[cur][:, h, :], Up[cur][:, h, :], True, True,
                                [("Up", cur), ("Lp", cur)], [kU2])
                self.cp(Lp[nx], pL, [kL], [("Lp", nx)], eng="act")
                if step < 5:
                    self.cp(Up[nx], pU2, [kU2], [("Up", nx)])
                pX, kX = self.pslot()
                for h in range(H):
                    self.mm(pX[:, h, :], Lp[nx][:, h, :], Xp[xc][:, h, :], True, True, [("Lp", nx), ("Xp", xc)], [kX])
                self.tt(Xp[1 - xc], pX, Xp[xc], ALU.add, [kX, ("Xp", xc)], [("Xp", 1 - xc)])
                xc = 1 - xc
                cur = nx
            X = Xp[xc]
            kXk = ("Xp", xc)
            pKt, kKt = self.pslot()
            for h in range(H):
                self.tr(pKt[:, h, :], kT[sl][:, h, :], self.ident_f[:], [("kT", sl)], [kKt])
            self.tt(kbg, pKt, bc(bg), ALU.mult, [kKt, "sc"], ["kbg"])
            self.tt(kdec, pKt, bc(edl), ALU.mult, [kKt, "sc"], ["kdec"])
            pVt, kVt = self.pslot()
            for h in range(H):
                self.tr(pVt[:, h, :], vT[sl][:, h, :], self.ident_f[:], [("vT", sl)], [kVt])
            self.tt(vb, pVt, bc(beta), ALU.mult, [kVt, ("sm", sl)], ["vb"])
            pu, ku = self.pslot()
            for h in range(H):
                self.mm(pu[:, h, :], X[:, h, :], vb[:, h, :], True, True, [kXk, "vb"], [ku])
            self.cp(u, pu, [ku], ["u"], eng="act")
            pw, kw = self.pslot()
            for h in range(H):
                self.mm(pw[:, h, :], kbg[:, h, :], X[:, h, :], True, True, [kXk, "kbg"], [kw])
            self.cp(wTb, pw, [kw], ["wTb"], eng="act")
            pWS, kWS = self.pslot()
            for h in range(H):
                self.mm(pWS[:, h, :], wTb[:, h, :], St[:, h, :], True, True, ["wTb", "St"], [kWS])
            self.tt(vnew, u, pWS, ALU.subtract, ["u", kWS], ["vnew"])
            pQS, kQS = self.pslot()
            for h in range(H):
                self.mm(pQS[:, h, :], qTb[:, h, :], Sb[:, h, :], True, True, ["qTb", "Sb"], [kQS])
            pAV, kAV = self.pslot()
            for h in range(H):
                self.mm(pAV[:, h, :], AtT[:, h, :], vnew[:, h, :], True, True, ["AtT", "vnew"], [kAV])
            self.tt(o, pQS, bc(egc), ALU.mult, [kQS, "sc"], ["o"])
            self.tt(o, o, pAV, ALU.add, ["o", kAV], ["o"])
            pKV, kKV = self.pslot()
            for h in range(H):
                self.mm(pKV[:, h, :], kdec[:, h, :], vnew[:, h, :], True, True, ["kdec", "vnew"], [kKV])
            self.tt(St, St, bc(egl), ALU.mult, ["St", "sc"], ["St"], eng="pool")
            self.tt(St, St, pKV, ALU.add, ["St", kKV], ["St"])
            self.cp(Sb, St, ["St"], ["Sb"], eng="act")
            self.act(osq, o, AF.Square, ["o"], ["osq"])
            self.op("dve", "tensor_reduce", ss, osq, AX.X, ALU.add, r=["osq"], w=["ss"])
            self.act(ss, ss, AF.Sqrt, ["ss"], ["ss"], bias=self.eps_t[:, 0:1], scale=1.0 / 128)
            self.op("dve", "reciprocal", ss, ss, r=["ss"], w=["ss"])
            self.tt(o, o, bc(ss), ALU.mult, ["o", "ss"], ["o"])
            pO, kO = self.pslot()
            for h in range(H):
                self.tr(pO[:, h, :], o[:, h, :], self.ident_f[:], ["o"], [kO])
            self.stt(ya[sl], pO, onw, zg[sl], ALU.mult, ALU.mult, [kO, ("zg", sl)], [("ya", sl)])
            self.store(yv[:, 0:8, t0:t0 + 128], ya[sl], ("ya", sl), [("ya", sl)], [("G", "yT", tix)])

    def nsa_stage(self, l):
        S, T = self.S, self.T
        self.stage_begin("nsa%d" % l)
        c = self.carve
        Dm = self.D
        I = self.I
        NQB = S // 128
        NCMP = S // 16 - 1
        NCP = NCMP + 1
        NCC = (NCP + 127) // 128
        NSEL = S // 64
        sh4 = [128, 4, 128]
        cmask = c([128, 16, 128], BF16)
        cmf = c([128, 128], F32)
        for a in range(16):
            self.memset(cmf, 0.0, ["cmf"])
            self.asel(cmf, cmf, [[1, 128]], ALU.is_ge, -30000.0, 128 * a - 15, -16, ["cmf"], ["cmf"])
            self.cp(cmask[:, a, :], cmf, ["cmf"], ["cmask"], eng="pool")
        caus = c([128, 128], BF16)
        self.memset(cmf, 0.0, ["cmf"])
        self.asel(cmf, cmf, [[1, 128]], ALU.is_ge, -30000.0, 0, -1, ["cmf"], ["cmf"])
        self.cp(caus, cmf, ["cmf"], ["caus"], eng="pool")
        wlow = c([128, 128], BF16)
        self.memset(cmf, 0.0, ["cmf"])
        self.asel(cmf, cmf, [[-1, 128]], ALU.is_gt, -30000.0, 0, 1, ["cmf"], ["cmf"])
        self.cp(wlow, cmf, ["cmf"], ["wlow"], eng="pool")
        Ebig = c([128, S], BF16)
        ebf = c([128, 512], F32)
        for k0 in range(0, S, 512):
            self.memset(ebf, 1.0, ["ebf"])
            self.asel(ebf, ebf, [[1, 512]], ALU.is_ge, 0.0, k0, -64, ["ebf"], ["ebf"])
            self.asel(ebf, ebf, [[-1, 512]], ALU.is_ge, 0.0, 63 - k0, 64, ["ebf"], ["ebf"])
            self.cp(Ebig[:, k0:k0 + 512], ebf, ["ebf"], ["Ebig"], eng="pool")
        ovl = c([128, NCC, 128], F32)
        ov2 = c([128, NCC, 128], F32)
        for (t, base) in ((ovl, 0), (ov2, -1)):
            nm = "ovl" if base == 0 else "ov2"
            self.memset(t, 1.0, [nm])
            self.asel(t, t, [[128, NCC], [-4, 128]], ALU.is_ge, 0.0, base, 1, [nm], [nm])
            self.asel(t, t, [[-128, NCC], [4, 128]], ALU.is_ge, 0.0, 3 - base, -1, [nm], [nm])
        self.tt(ovl, ovl, ov2, ALU.add, ["ovl", "ov2"], ["ovl"], eng="pool")
        self.memset(ovl[0:1, 0, :], 0.0, ["ovl"])
        slr = c([1, 8], F32)
        for h in range(8):
            self.memset(slr[:, h:h + 1], SLOPES[h], ["slr"])
        NM = NQB + 16
        tabf = c([1, NM, 8], F32)
        self.sch.op("pool", lambda e: e.iota(tabf, pattern=[[128, NM], [0, 8]], base=-128 * (NQB - 1),
                                             channel_multiplier=0, allow_small_or_imprecise_dtypes=True),
                    (), self.Bs(["tabf"]))
        self.tt(tabf, tabf, slr.unsqueeze(1).to_broadcast([1, NM, 8]), ALU.mult, ["tabf", "slr"], ["tabf"], eng="pool")
        tabm = c([1, NM, 8], BF16)
        self.cp(tabm, tabf, ["tabf"], ["tabm"], eng="pool")
        ones1 = c([1, 128], BF16)
        self.memset(ones1, 1.0, ["ones1"])
        a2f = c([2, 128], F32)
        aLc = c([2, 128], BF16)
        aLs = c([2, 128], BF16)
        self.memset(a2f, 1.0, ["a2f"])
        self.sch.op("pool", lambda e: e.iota(a2f[0:1, :], pattern=[[16, 128]], base=0, channel_multiplier=0,
                                             allow_small_or_imprecise_dtypes=True), (), self.Bs(["a2f"]))
        self.cp(aLc, a2f, ["a2f"], ["aLc"], eng="pool")
        self.sch.op("pool", lambda e: e.iota(a2f[0:1, :], pattern=[[1, 128]], base=0, channel_multiplier=0,
                                             allow_small_or_imprecise_dtypes=True), self.Bs(["a2f"]), self.Bs(["a2f"]))
        self.cp(aLs, a2f, ["a2f"], ["aLs"], eng="pool")
        r2f = c([2, 8, 128], F32)
        aRc = c([2, 8, 128], BF16)
        aRs = c([2, 8, 128], BF16)
        for (dst, nm, b0, st) in ((aRc, "aRc", 15, -1), (aRs, "aRs", 0, -1)):
            self.sch.op("pool", lambda e, b0=b0, st=st: e.iota(r2f, pattern=[[0, 8], [st, 128]], base=b0,
                                                               channel_multiplier=0, allow_small_or_imprecise_dtypes=True),
                        self.Bs(["r2f"]), self.Bs(["r2f"]))
            self.memset(r2f[0:1, :, :], 1.0, ["r2f"])
            self.tt(r2f, r2f, self.slr2(slr, c), ALU.mult, ["r2f", "slr2"], ["r2f"], eng="pool")
            self.cp(dst, r2f, ["r2f"], [nm], eng="pool")
        kcT = c([128, 2, NCC * 128], BF16)
        vca = c([128, NCC, 2, 257], BF16)
        self.memset(kcT, 0.0, ["kcT"])
        self.memset(vca, 0.0, ["vca"])
        for g in range(2):
            self.cp(vca[:, :, g, 129:257], ovl, ["ovl"], ["vca"], eng="pool")
            self.memset(vca[:, :, g, 128:129], 1.0, ["vca"])
            self.memset(vca[0:1, 0, g, 128:129], 0.0, ["vca"])
        amark = self.aoff
        xc = c([128, S], F32)
        xpb = c([128, 32, 512], BF16)
        w1 = c([128, 32, 256], BF16)
        w2 = c([128, 2, 128], BF16)
        hb = c([128, 2, 512], BF16)
        vcf = c([128, NCC * 128], F32)
        cst = c([128, 512], F32)
        csq = c([128, 512], BF16)
        crs = c([128, 512], F32)
        xcv = xc.rearrange("p (i r) -> p i r", r=16)
        for kv in range(2):
            self.load(w1, I["cmp_w1_t"][l, kv], "w1", (), ["w1"], queue="pool")
            self.load(w2, I["cmp_w2_t"][l, kv], "w2", (), ["w2"], queue="pool")
            for g in range(2):
                row = kv * 2 + g
                self.load(xc, Dm["ns_kvc"][row * 128:(row + 1) * 128, :], "xc",
                          [("G", "ns_kvc", i) for i in range(S // T)], ["xc"])
                for p in range(32):
                    src = xcv[:, 0:NCMP, p] if p < 16 else xcv[:, 1:NCMP + 1, p - 16]
                    self.ts(xpb[:, p, 0:NCMP], src, self.n128[:, l, 8 + 32 * kv + p:9 + 32 * kv + p], None, ALU.add, None,
                            ["xc"], ["xpb"], eng=("dve" if p % 2 == 0 else "pool"))
                for hc in range(2):
                    pH = self.bank(hc)
                    for p in range(32):
                        self.mm(pH[:, 0:NCMP], w1[:, p, hc * 128:(hc + 1) * 128], xpb[:, p, 0:NCMP], p == 0, p == 31,
                                ["w1", "xpb"], [("pb", hc)])
                    self.act(hb[:, hc, 0:NCMP], pH[:, 0:NCMP], AF.Silu, [("pb", hc)], ["hb"])
                pO = self.bank(2)
                for hc in range(2):
                    self.mm(pO[:, 0:NCMP], w2[:, hc, :], hb[:, hc, 0:NCMP], hc == 0, hc == 1, ["w2", "hb"], [("pb", 2)])
                if kv == 0:
                    self.act(csq[:, 0:NCMP], pO[:, 0:NCMP], AF.Square, [("pb", 2)], ["csq"])
                    pS = self.bank(3)
                    self.mm(pS[:, 0:NCMP], self.ones_bf[:], csq[:, 0:NCMP], True, True, ["csq"], [("pb", 3)])
                    self.act(crs[:, 0:NCMP], pS[:, 0:NCMP], AF.Sqrt, [("pb", 3)], ["crs"], bias=self.eps_t[:, 0:1],
                             scale=1.0 / 128)
                    self.op("dve", "reciprocal", crs[:, 0:NCMP], crs[:, 0:NCMP], r=["crs"], w=["crs"])
                    self.stt(kcT[:, g, 1:NCP], pO[:, 0:NCMP], self.n128[:, l, 2:3], crs[:, 0:NCMP], ALU.mult, ALU.mult,
                             [("pb", 2), "crs"], ["kcT"])
                else:
                    self.memset(vcf, 0.0, ["vcf"])
                    self.cp(vcf[:, 1:NCP], pO[:, 0:NCMP], [("pb", 2)], ["vcf"], eng="act")
                    pT = self.bank(3)
                    for k in range(NCC):
                        self.tr(pT[:, k * 128:(k + 1) * 128], vcf[:, k * 128:(k + 1) * 128], self.ident_f[:],
                                ["vcf"], [("pb", 3)])
                    self.cp(vca[:, :, g, 0:128], pT[:, 0:NCC * 128].rearrange("p (k d) -> p k d", d=128),
                            [("pb", 3)], ["vca"])
        self.sch.barrier()
        self.aoff = amark
        ksT = c([128, 2, S], BF16)
        vsa = c([128, NQB, 2, 129], BF16)
        for g in range(2):
            self.load(ksT[:, g, :], Dm["ns_k"][g * 128:(g + 1) * 128, :], "ksT",
                      [("G", "ns_k", i) for i in range(S // T)], ["ksT"])
            self.load(vsa[:, :, g, 0:128], Dm["ns_v"][g].rearrange("(n p) d -> p n d", p=128), "vsa",
                      [("G", "ns_v", i) for i in range(S // T)], ["vsa"])
            self.memset(vsa[:, :, g, 128:129], 1.0, ["vsa"])
        kwT = [c([128, 2, 640], BF16) for _ in range(2)]
        vwa = [c([128, 5, 2, 129], BF16) for _ in range(2)]
        for i in range(2):
            self.memset(vwa[i][:, :, :, 128:129], 1.0, [("vwa", i)])
        qT = [c([128, 8, 128], BF16) for _ in range(2)]
        gq = [c([128, 64], F32) for _ in range(2)]
        eT = [c([128, 512], BF16) for _ in range(3)]
        ev = c([128, 4, 257], F32)
        evs = c([128, 4, 129], F32)
        rs = c([128, 8], F32)
        wv = c([128, 8], F32)
        tmp4 = c(sh4, F32)
        imp = c([128, 128], F32)
        imp2 = c([128, 128], F32)
        m8 = c([128, 16], F32)
        nsel = c([128, 128], F32)
        nselT = c([128, 4, 128], BF16)
        onsa = c([128, 8, 128], F32)
        ybT = [c([128, 8, 128], BF16) for _ in range(2)]
        qv = Dm["ns_q"].rearrange("(h d) s -> d h s", h=8)
        yv = Dm["yT"].rearrange("(k p) s -> p k s", p=128)
        pSb = [self.bank(0), self.bank(1)]
        pS3 = [b.rearrange("p (h q) -> p h q", h=4) for b in pSb]
        acc = [self.bank(2 + i) for i in range(4)]

        def exp_chunk(kS, ps):
            e = self.nxt("eT", 3)
            self.act(eT[e], ps, AF.Exp, [kS], [("eT", e)])
            return e

        for qb in range(NQB):
            t0 = qb * 128
            tix = t0 // T
            sl = qb % 2
            self.load(qT[sl], qv[:, :, t0:t0 + 128], ("qT", sl), [("G", "ns_q", tix)], [("qT", sl)])
            self.load(gq[sl], Dm["sm_tok"][t0:t0 + 128, :], ("gq", sl), [("G", "sm_tok", tix)], [("gq", sl)])
            w0 = max(0, qb - 4)
            nw = qb - w0 + 1
            self.load(kwT[sl][:, :, 0:nw * 128],
                      Dm["ns_k"][256:512, w0 * 128:(qb + 1) * 128].rearrange("(g d) s -> d g s", g=2),
                      ("kwT", sl), [("G", "ns_k", i) for i in range(w0 * 128 // T, tix + 1)], [("kwT", sl)])
            for g in range(2):
                self.load(vwa[sl][:, 0:nw, g, 0:128],
                          Dm["ns_v"][2 + g, w0 * 128:(qb + 1) * 128, :].rearrange("(n p) d -> p n d", p=128),
                          ("vwa", sl), [("G", "ns_v", i) for i in range(w0 * 128 // T, tix + 1)], [("vwa", sl)])
            gates = gq[sl][:, 16:40].rearrange("p (b h) -> p b h", b=3)
            for g in range(2):
                q4 = qT[sl][:, 4 * g:4 * g + 4, :]
                aRc4 = aRc[:, 4 * g:4 * g + 4, :]
                aRs4 = aRs[:, 4 * g:4 * g + 4, :]
                ncc = (8 * qb + 8 + 127) // 128
                for k in range(ncc):
                    s = self.nxt("pS")
                    kS = ("pS", s)
                    m = 16 * k - qb
                    self.mm(pS3[s], kcT[:, g, k * 128:(k + 1) * 128], q4, True, False, ["kcT", ("qT", sl)], [kS])
                    self.mm(pS3[s], aLc, aRc4, False, False, ["aLc", "aRc"], [kS])
                    if k == ncc - 1:
                        self.mm(pS3[s], self.ident_b[:], cmask[:, qb % 16, :].unsqueeze(1).to_broadcast(sh4), False, False,
                                ["cmask"], [kS])
                    self.mm(pS3[s], ones1, tabm[:, m + NQB - 1, 4 * g:4 * g + 4].unsqueeze(2).to_broadcast([1, 4, 128]),
                            False, True, ["ones1", "tabm"], [kS])
                    e = exp_chunk(kS, pSb[s])
                    for h in range(4):
                        self.mm(acc[h][:, 0:257], eT[e][:, h * 128:(h + 1) * 128], vca[:, k, g, :], k == 0, k == ncc - 1,
                                [("eT", e), "vca"], [("acc", h)])
                for h in range(4):
                    self.cp(ev[:, h, :], acc[h][:, 0:257], [("acc", h)], ["ev"], eng="act")
                self.ts(rs[:, 0:4], ev[:, :, 128], 1e-30, None, ALU.max, None, ["ev"], ["rs"])
                self.op("dve", "reciprocal", rs[:, 0:4], rs[:, 0:4], r=["rs"], w=["rs"])
                self.tt(wv[:, 0:4], rs[:, 0:4], gates[:, 0, 4 * g:4 * g + 4], ALU.mult, ["rs", ("gq", sl)], ["wv"])
                self.tt(onsa[:, 4 * g:4 * g + 4, :], ev[:, :, 0:128], wv[:, 0:4].unsqueeze(2).to_broadcast(sh4), ALU.mult,
                        ["ev", "wv"], ["onsa"])
                self.tt(tmp4, ev[:, :, 129:257], rs[:, 0:4].unsqueeze(2).to_broadcast(sh4), ALU.mult, ["ev", "rs"], ["tmp4"])
                self.tt(imp, tmp4[:, 0, :], tmp4[:, 1, :], ALU.add, ["tmp4"], ["imp"])
                self.tt(imp2, tmp4[:, 2, :], tmp4[:, 3, :], ALU.add, ["tmp4"], ["imp2"])
                self.tt(imp, imp, imp2, ALU.add, ["imp", "imp2"], ["imp"])
                j0 = 2 * qb
                if j0 + 2 < NSEL:
                    self.memset(imp[:, j0 + 2:NSEL], -1e6, ["imp"], eng="dve")
                self.memset(imp[0:64, j0 + 1:j0 + 2], -1e6, ["imp"], eng="dve")
                if j0 >= 1:
                    self.memset(imp[0:64, j0 - 1:j0 + 1], 1e6, ["imp"], eng="dve")
                else:
                    self.memset(imp[0:64, 0:1], 1e6, ["imp"], eng="dve")
                self.memset(imp[64:128, j0:j0 + 2], 1e6, ["imp"], eng="dve")
                self.memset(imp[:, 0:1], 1e6, ["imp"], eng="dve")
                if NSEL < 128:
                    self.memset(imp[:, NSEL:128], -1e6, ["imp"], eng="dve")
                self.op("dve", "max", out=m8[:, 0:8], in_=imp, r=["imp"], w=["m8"])
                self.op("dve", "match_replace", out=imp2, in_to_replace=m8[:, 0:8], in_values=imp, imm_value=-2e6,
                        r=["imp", "m8"], w=["imp2"])
                self.op("dve", "max", out=m8[:, 8:16], in_=imp2, r=["imp2"], w=["m8"])
                self.ts(nsel, imp, m8[:, 15:16], -32768.0, ALU.is_lt, ALU.mult, ["imp", "m8"], ["nsel"])
                pT = acc[3]
                self.tr(pT[:, 0:128], nsel, self.ident_f[:], ["nsel"], [("acc", 3)])
                self.cp(nselT, pT[:, 0:128].unsqueeze(1).to_broadcast(sh4), [("acc", 3)], ["nselT"])
                for (br, k_lo) in ((1, 0), (2, w0)):
                    for kc in range(k_lo, qb + 1):
                        s = self.nxt("pS")
                        kS = ("pS", s)
                        m = kc - qb
                        if br == 1:
                            self.mm(pS3[s], ksT[:, g, kc * 128:(kc + 1) * 128], q4, True, False, ["ksT", ("qT", sl)], [kS])
                            self.mm(pS3[s], Ebig[:, kc * 128:(kc + 1) * 128], nselT, False, False, ["Ebig", "nselT"], [kS])
                        else:
                            self.mm(pS3[s], kwT[sl][:, g, (kc - w0) * 128:(kc - w0 + 1) * 128], q4, True, False,
                                    [("kwT", sl), ("qT", sl)], [kS])
                        self.mm(pS3[s], aLs, aRs4, False, False, ["aLs", "aRs"], [kS])
                        if kc == qb:
                            self.mm(pS3[s], self.ident_b[:], caus.unsqueeze(1).to_broadcast(sh4), False, False, ["caus"], [kS])
                        if br == 2 and kc == qb - 4:
                            self.mm(pS3[s], self.ident_b[:], wlow.unsqueeze(1).to_broadcast(sh4), False, False, ["wlow"], [kS])
                        self.mm(pS3[s], ones1,
                                tabm[:, m + NQB - 1, 4 * g:4 * g + 4].unsqueeze(2).to_broadcast([1, 4, 128]),
                                False, True, ["ones1", "tabm"], [kS])
                        e = exp_chunk(kS, pSb[s])
                        for h in range(4):
                            ai = (0 if br == 1 else 2) + h // 2
                            rhs = vsa[:, kc, g, :] if br == 1 else vwa[sl][:, kc - w0, g, :]
                            self.mm(acc[ai][:, (h % 2) * 129:(h % 2) * 129 + 129], eT[e][:, h * 128:(h + 1) * 128], rhs,
                                    kc == k_lo and h % 2 == 0, kc == qb, [("eT", e), "vsa" if br == 1 else ("vwa", sl)],
                                    [("acc", ai)], skip=True)
                    a0 = 0 if br == 1 else 2
                    for i in range(2):
                        self.cp(evs[:, 2 * i:2 * i + 2, :], acc[a0 + i][:, 0:258].rearrange("p (h c) -> p h c", h=2),
                                [("acc", a0 + i)], ["evs"], eng="act")
                    self.op("dve", "reciprocal", rs[:, 4:8], evs[:, :, 128], r=["evs"], w=["rs"])
                    self.tt(wv[:, 4:8], rs[:, 4:8], gates[:, br, 4 * g:4 * g + 4], ALU.mult, ["rs", ("gq", sl)], ["wv"])
                    self.tt(tmp4, evs[:, :, 0:128], wv[:, 4:8].unsqueeze(2).to_broadcast(sh4), ALU.mult, ["evs", "wv"], ["tmp4"])
                    self.tt(onsa[:, 4 * g:4 * g + 4, :], onsa[:, 4 * g:4 * g + 4, :], tmp4, ALU.add, ["onsa", "tmp4"], ["onsa"])
            pY = self.bank(2, 2).rearrange("p (h c) -> p h c", h=8)
            for h in range(8):
                self.tr(pY[:, h, :], onsa[:, h, :], self.ident_f[:], ["onsa"], [("acc", 0), ("acc", 1)])
            self.cp(ybT[sl], pY, [("acc", 0), ("acc", 1)], [("ybT", sl)], eng="act")
            self.store(yv[:, 8:16, t0:t0 + 128], ybT[sl], ("ybT", sl), [("ybT", sl)], [("G", "yT", tix)])

    def slr2(self, slr, c):
        if not hasattr(self, "_slr2") or self._slr2_stage != self.stage:
            t = c([2, 8, 128], F32)
            for h in range(8):
                self.memset(t[:, h, :], SLOPES[h], ["slr2"])
            self._slr2 = t
            self._slr2_stage = self.stage
        return self._slr2

    def build(self):
        S, NL = self.S, self.NL
        I = {}
        I["xT_in"] = self.dram_in("xT_in", [D, S])
        I["gam_in"] = self.dram_in("gam_in", [128, 3 * NL, KC])
        I["n128"] = self.dram_in("n128", [NL, 128, 72])
        for f in ("ffn1", "ffn2"):
            I[f + "_gate_t"] = self.dram_in(f + "_gate_t", [NL, FC, 128, KC, 128])
            I[f + "_up_t"] = self.dram_in(f + "_up_t", [NL, FC, 128, KC, 128])
            I[f + "_down_t"] = self.dram_in(f + "_down_t", [NL, KC, 128, FC, 128])
        I["w_in_t"] = self.dram_in("w_in_t", [NL, NWIN, 128, KC, 128])
        I["w_sm_t"] = self.dram_in("w_sm_t", [NL, 128, KC, 64])
        I["w_br_t"] = self.dram_in("w_br_t", [NL, 3, KC, 128, 8, 128])
        I["w_out_t"] = self.dram_in("w_out_t", [NL, KC, 128, KC, 128])
        I["dn_conv_t"] = self.dram_in("dn_conv_t", [NL, 128, 24, 4])
        I["dn_vec"] = self.dram_in("dn_vec", [NL, 16])
        I["cmp_w1_t"] = self.dram_in("cmp_w1_t", [NL, 2, 128, 32, 256])
        I["cmp_w2_t"] = self.dram_in("cmp_w2_t", [NL, 2, 128, 2, 128])
        I["sgu_nw"] = self.dram_in("sgu_nw", [NL, 128, 16])
        I["sgu_wT"] = self.dram_in("sgu_wT", [NL, 128, 8, 128])
        I["sgu_b"] = self.dram_in("sgu_b", [NL, 8, 128])
        self.I = I
        out = self.dram_out("yT_out", [D, S])
        Dm = {}
        Dm["xTd"] = self.dram_tmp("xTd", [D, S])
        Dm["dn_qkv"] = self.dram_tmp("dn_qkv", [3072, S])
        Dm["dn_zg"] = self.dram_tmp("dn_zg", [1024, S])
        Dm["sm_tok"] = self.dram_tmp("sm_tok", [S, 64])
        Dm["ns_q"] = self.dram_tmp("ns_q", [1024, S], BF16)
        Dm["ns_kvc"] = self.dram_tmp("ns_kvc", [512, S])
        Dm["ns_k"] = self.dram_tmp("ns_k", [512, S], BF16)
        Dm["ns_v"] = self.dram_tmp("ns_v", [4, S, 128], BF16)
        Dm["gates"] = self.dram_tmp("gates", [6144, S])
        Dm["yT"] = self.dram_tmp("yT", [3072, S], BF16)
        self.D = Dm
        self.setup_consts()
        st = self.stages
        mix = lambda l: [f for f in ("dn", "nsa") if st is None or f in st]
        last = NL - 1
        if st is not None and "rl0only" in st:
            self.rl_stage("rl0", None, None, 0, 0, I["xT_in"], out, "out")
            return self.sch.finalize()
        self.rl_stage("rl0", None, None, 0, 0, I["xT_in"], Dm["xTd"], "xTd")
        for l in range(NL):
            if "dn" in mix(l):
                self.dn_stage(l)
            if "nsa" in mix(l):
                self.nsa_stage(l)
            if st is not None and "nomerge" in st:
                continue
            if l < last:
                self.rl_stage("rl%d" % (l + 1), l, l, l + 1, l + 1, Dm["xTd"], Dm["xTd"], "xTd")
            else:
                self.rl_stage("rl%d" % (l + 1), l, l, None, None, Dm["xTd"], out, "out")
        return self.sch.finalize()


IN_SIZES = (1024, 1024, 1024, 1024, 8, 8, 1024, 256, 256, 256, 256, 256, 256, 24, 1024, 1024, 2048, 2048, 2048)


def tile_w(w):
    K, N = w.shape
    return np.ascontiguousarray(w.reshape(K // 128, 128, N // 128, 128).transpose(2, 1, 0, 3))


def prep_weights(inp, NL):
    f32 = np.float32
    o = {}
    gam = np.zeros((128, 3 * NL, KC), f32)
    for l in range(NL):
        for i, nm in enumerate(("ffn1_norm", "mix_norm", "ffn2_norm")):
            gam[:, 3 * l + i, :] = np.asarray(inp[nm][l], f32).reshape(KC, 128).T
    o["gam_in"] = gam
    n128 = np.zeros((NL, 128, 72), f32)
    for l in range(NL):
        n128[l, :, 0] = inp["dn_out_norm"][l]
        n128[l, :, 1] = inp["nsa_q_norm"][l]
        n128[l, :, 2:5] = np.asarray(inp["nsa_k_norm"][l]).T
        n128[l, :, 8:40] = np.asarray(inp["cmpk_pos"][l]).T
        n128[l, :, 40:72] = np.asarray(inp["cmpv_pos"][l]).T
    o["n128"] = n128
    for f in ("ffn1", "ffn2"):
        o[f + "_gate_t"] = np.stack([tile_w(np.asarray(inp[f + "_gate"][l], f32)) for l in range(NL)])
        o[f + "_up_t"] = np.stack([tile_w(np.asarray(inp[f + "_up"][l], f32)) for l in range(NL)])
        o[f + "_down_t"] = np.stack([tile_w(np.asarray(inp[f + "_down"][l], f32)) for l in range(NL)])
    offs = np.concatenate([[0], np.cumsum(IN_SIZES)])
    big = np.concatenate([np.arange(0, 4096), np.arange(4112, 6672), np.arange(6696, 14888)])
    small = np.concatenate([np.arange(4096, 4112), np.arange(6672, 6696)])
    assert big.size == NWIN * 128 and offs[-1] == 14888
    w_in_t, w_sm_t = [], []
    for l in range(NL):
        w = np.asarray(inp["w_in"][l], f32)
        w_in_t.append(tile_w(w[:, big]))
        ws = np.zeros((D, 64), f32)
        ws[:, 0:40] = w[:, small]
        w_sm_t.append(np.ascontiguousarray(ws.reshape(KC, 128, 64).transpose(1, 0, 2)))
    o["w_in_t"] = np.stack(w_in_t)
    o["w_sm_t"] = np.stack(w_sm_t)
    o["w_br_t"] = np.stack([np.stack([tile_w(np.asarray(inp[nm][l], f32))
                                      for nm in ("w_branch_a", "w_branch_b", "w_branch_c")]) for l in range(NL)])
    o["w_out_t"] = np.stack([tile_w(np.asarray(inp["w_out"][l], f32)) for l in range(NL)])
    o["dn_conv_t"] = np.ascontiguousarray(np.asarray(inp["dn_conv"], f32).reshape(NL, 4, 24, 128).transpose(0, 3, 2, 1))
    o["dn_vec"] = np.concatenate([np.asarray(inp["dn_a_log"], f32), np.asarray(inp["dn_dt_bias"], f32)], axis=1)
    w1 = np.stack([np.asarray(inp["cmpk_w1"], f32), np.asarray(inp["cmpv_w1"], f32)], axis=1)
    o["cmp_w1_t"] = np.ascontiguousarray(w1.reshape(NL, 2, 32, 128, 256).transpose(0, 1, 3, 2, 4))
    w2 = np.stack([np.asarray(inp["cmpk_w2"], f32), np.asarray(inp["cmpv_w2"], f32)], axis=1)
    o["cmp_w2_t"] = np.ascontiguousarray(w2.reshape(NL, 2, 2, 128, 128).transpose(0, 1, 3, 2, 4))
    nw = np.zeros((NL, 128, 16), f32)
    for l in range(NL):
        nw[l, :, 0:8] = np.asarray(inp["sgu_norm_w"][l], f32).reshape(8, 128).T
        nw[l, :, 8:16] = np.asarray(inp["sgu_norm_b"][l], f32).reshape(8, 128).T
    o["sgu_nw"] = nw
    o["sgu_wT"] = np.ascontiguousarray(np.asarray(inp["sgu_w"], f32).transpose(0, 3, 1, 2))
    o["sgu_b"] = np.ascontiguousarray(np.asarray(inp["sgu_b"], f32))
    return o


_CACHE = {}


def kernel(**inputs):
    x = np.asarray(inputs["x"], np.float32)
    Bsz, S, _ = x.shape
    NL = int(np.asarray(inputs["ffn1_norm"]).shape[0])
    key = (S, NL)
    if key not in _CACHE:
        mk = MK(S, NL)
        mk.build()
        _CACHE[key] = mk
    mk = _CACHE[key]
    w = prep_weights(inputs, NL)
    n_cores = 8
    in_maps = []
    for c in range(n_cores):
        b = c % Bsz
        m = dict(w)
        m["xT_in"] = np.ascontiguousarray(x[b].T)
        in_maps.append(m)
    res = run_bass_kernel_spmd(mk.nc, in_maps, core_ids=list(range(n_cores)))
    out = np.empty_like(x)
    for b in range(Bsz):
        out[b] = res.results[b]["yT_out"].T
    return out
```

```python
import numpy as np
from contextlib import ExitStack
import concourse.bass as bass
import concourse.mybir as mybir
from concourse.bass_utils import run_bass_kernel_spmd

F32 = mybir.dt.float32
BF16 = mybir.dt.bfloat16
ALU = mybir.AluOpType
AF = mybir.ActivationFunctionType
AX = mybir.AxisListType

D = 2048
KC = D // 128
FF = 5504
FC = FF // 128
EPS = 1e-6


class Buf:
    __slots__ = ("name", "w", "r")

    def __init__(self, name=""):
        self.name = name
        self.w = None
        self.r = {}


class Ins:
    __slots__ = ("eng", "fn", "deps", "tick", "dma", "sem", "val", "needs_inc")

    def __init__(self, eng, fn):
        self.eng = eng
        self.fn = fn
        self.deps = ()
        self.tick = 0
        self.dma = False
        self.sem = None
        self.val = 0
        self.needs_inc = False


class Sched:
    ENG = ("pe", "act", "dve", "pool", "sp")

    def __init__(self, nc, stack):
        self.nc = nc
        self.stack = stack
        self.q = {e: [] for e in self.ENG}
        self.engsem = {e: stack.enter_context(nc.semaphore("es_" + e)) for e in self.ENG}
        self.dmasem = {}
        self.last_dma = {}
        self.pending = {}
        self.freesem = []
        self.nsem = 0

    def _track(self, ins, reads, writes):
        deps = {}
        for b in reads:
            d = b.w
            if d is not None:
                deps[id(d)] = d
        for b in writes:
            d = b.w
            if d is not None:
                deps[id(d)] = d
            for d in b.r.values():
                deps[id(d)] = d
        extra = self.pending.pop(ins.eng, None)
        if extra:
            for d in extra:
                deps[id(d)] = d
        deps.pop(id(ins), None)
        dl = list(deps.values())
        if ins.eng == "pe":
            dl = [d for d in dl if d.dma or d.eng != "pe"]
        ins.deps = dl
        key = ("d", id(ins.sem)) if ins.dma else ins.eng
        for b in reads:
            b.r[key] = ins
        for b in writes:
            b.w = ins
            b.r = {}
        self.q[ins.eng].append(ins)

    def op(self, eng, fn, reads=(), writes=()):
        ins = Ins(eng, fn)
        self._track(ins, reads, writes)
        return ins

    def dma(self, queue, fn, semkey, reads=(), writes=()):
        ins = Ins(queue, fn)
        ins.dma = True
        ent = self.dmasem.get(semkey)
        if ent is None:
            if self.freesem:
                ent = self.freesem.pop()
            else:
                ent = [self.stack.enter_context(self.nc.semaphore("ds%d" % self.nsem)), 0]
                self.nsem += 1
            self.dmasem[semkey] = ent
        ent[1] += 16
        ins.sem = ent[0]
        ins.val = ent[1]
        self._track(ins, reads, writes)
        self.last_dma[id(ins.sem)] = ins
        return ins

    def barrier(self):
        deps = list(self.last_dma.values())
        for e in self.ENG:
            if self.q[e]:
                for ins in reversed(self.q[e]):
                    if not ins.dma and ins.fn is not None:
                        deps.append(ins)
                        break
        for e in self.ENG:
            self.pending[e] = list(deps) + list(self.pending.get(e, []))
        self.freesem.extend(self.dmasem.values())
        self.dmasem = {}

    def finalize(self, final_eng="pool"):
        fin = Ins(final_eng, None)
        fin.deps = list(self.last_dma.values())
        self.q[final_eng].append(fin)
        for e in self.ENG:
            for ins in self.q[e]:
                for d in ins.deps:
                    if not d.dma:
                        d.needs_inc = True
        for e in self.ENG:
            t = 0
            for ins in self.q[e]:
                if ins.needs_inc and not ins.dma:
                    t += 1
                    ins.tick = t
        nc = self.nc
        names = {"pe": "tensor", "act": "scalar", "dve": "vector", "pool": "gpsimd", "sp": "sync"}
        stats = {}
        engsem = self.engsem
        with nc.Block() as block:
            for e in self.ENG:
                def body(eng, q=self.q[e], e=e):
                    obs = {}
                    nw = 0
                    for ins in q:
                        for d in ins.deps:
                            if d.dma:
                                sem, val = d.sem, d.val
                            else:
                                sem, val = engsem[d.eng], d.tick
                            k = id(sem)
                            if obs.get(k, 0) < val:
                                eng.wait_ge(sem, val)
                                obs[k] = val
                                nw += 1
                        if ins.fn is None:
                            continue
                        r = ins.fn(eng)
                        if ins.dma:
                            r.then_inc(ins.sem, 16)
                        elif ins.needs_inc:
                            r.then_inc(engsem[e], 1)
                    stats[e] = (len(q), nw)
                getattr(block, names[e])(body)
        return stats


NWIN = 116
AW = 50000
SLOPES = [2.0 ** (-(h + 1)) for h in range(8)]


class MK:
    def __init__(self, S, NL, T=512, taps=(), stages=None):
        self.S = S
        self.NL = NL
        self.T = T
        self.taps = set(taps)
        self.stages = stages
        self.nc = bass.Bass("TRN2", target_bir_lowering=False)
        self.stack = ExitStack()
        self.sch = Sched(self.nc, self.stack)
        self.bufs = {}
        self.cnt = {}
        self.stage = "init"
        self.arena = self.sb("arena", [128, AW], F32)
        self.aoff = 0
        self.psall = self.ps("psall", [128, 4096], F32)

    def dram_in(self, name, shape, dt=F32):
        return self.nc.dram_tensor(name, list(shape), dt, kind="ExternalInput").ap()

    def dram_out(self, name, shape, dt=F32):
        return self.nc.dram_tensor(name, list(shape), dt, kind="ExternalOutput").ap()

    def dram_tmp(self, name, shape, dt=F32):
        kind = "ExternalOutput" if name in self.taps else "Internal"
        return self.nc.dram_tensor(name, list(shape), dt, kind=kind).ap()

    def sb(self, name, shape, dt=F32):
        return self.stack.enter_context(self.nc.sbuf_tensor("sb_" + name, list(shape), dt))

    def ps(self, name, shape, dt=F32):
        return self.stack.enter_context(self.nc.psum_tensor("ps_" + name, list(shape), dt))

    def carve(self, shape, dt=F32):
        n = 1
        for d in shape[1:]:
            n *= d
        words = (n + 1) // 2 if dt == BF16 else n
        words = (words + 7) // 8 * 8
        assert self.aoff + words <= AW, ("arena overflow", self.stage, self.aoff, words)
        v = self.arena[0:shape[0], self.aoff:self.aoff + words]
        self.aoff += words
        if dt == BF16:
            v = v.bitcast(BF16)
        v = v[:, 0:n]
        if len(shape) == 3:
            v = v.rearrange("p (a b) -> p a b", a=shape[1])
        elif len(shape) == 4:
            v = v.rearrange("p (a b c) -> p a b c", a=shape[1], b=shape[2])
        return v

    def bank(self, i, n=1):
        return self.psall[:, i * 512:(i + n) * 512]

    def stage_begin(self, name):
        self.sch.barrier()
        self.stage = name
        self.aoff = 0
        self.cnt = {}

    def B(self, key):
        if not (isinstance(key, tuple) and key[0] == "G"):
            key = (self.stage, key)
        b = self.bufs.get(key)
        if b is None:
            b = Buf(str(key))
            self.bufs[key] = b
        return b

    def Bs(self, keys):
        return [k if isinstance(k, Buf) else self.B(k) for k in keys]

    def nxt(self, key, n=2):
        v = self.cnt.get(key, 0)
        self.cnt[key] = v + 1
        return v % n

    def op(self, eng, method, *args, r=(), w=(), **kw):
        self.sch.op(eng, lambda e: getattr(e, method)(*args, **kw), self.Bs(r), self.Bs(w))

    def mm(self, out, lhsT, rhs, start, stop, r, w, skip=False):
        if skip:
            self.sch.op("pe", lambda e: e.matmul(out, lhsT, rhs, start=start, stop=stop, skip_group_check=True),
                        self.Bs(r), self.Bs(w))
        else:
            self.sch.op("pe", lambda e: e.matmul(out, lhsT, rhs, start=start, stop=stop), self.Bs(r), self.Bs(w))

    def tr(self, out, in_, ident, r, w):
        self.sch.op("pe", lambda e: e.transpose(out, in_, ident), self.Bs(r), self.Bs(w))

    def act(self, out, in_, func, r, w, bias=None, scale=1.0):
        if bias is None:
            self.sch.op("act", lambda e: e.activation(out, in_, func, scale=scale), self.Bs(r), self.Bs(w))
        else:
            self.sch.op("act", lambda e: e.activation(out, in_, func, bias=bias, scale=scale),
                        self.Bs(r), self.Bs(w))

    def tt(self, out, in0, in1, op, r, w, eng="dve"):
        self.sch.op(eng, lambda e: e.tensor_tensor(out, in0, in1, op), self.Bs(r), self.Bs(w))

    def ts(self, out, in0, s1, s2, op0, op1, r, w, eng="dve"):
        if op1 is None:
            self.sch.op(eng, lambda e: e.tensor_scalar(out, in0, s1, None, op0), self.Bs(r), self.Bs(w))
        else:
            self.sch.op(eng, lambda e: e.tensor_scalar(out, in0, s1, s2, op0, op1), self.Bs(r), self.Bs(w))

    def stt(self, out, in0, scalar, in1, op0, op1, r, w, eng="dve"):
        self.sch.op(eng, lambda e: e.scalar_tensor_tensor(out=out, in0=in0, scalar=scalar, in1=in1,
                                                          op0=op0, op1=op1), self.Bs(r), self.Bs(w))

    def cp(self, out, in_, r, w, eng="dve"):
        if eng == "act":
            self.sch.op("act", lambda e: e.copy(out, in_), self.Bs(r), self.Bs(w))
        else:
            self.sch.op(eng, lambda e: e.tensor_copy(out, in_), self.Bs(r), self.Bs(w))

    def memset(self, out, val, w, eng="pool"):
        self.sch.op(eng, lambda e: e.memset(out, val), (), self.Bs(w))

    def asel(self, out, in_, pattern, cmp, fill, base, cm, r, w):
        self.sch.op("pool", lambda e: e.affine_select(out=out, in_=in_, pattern=pattern, compare_op=cmp,
                                                      fill=fill, base=base, channel_multiplier=cm),
                    self.Bs(r), self.Bs(w))

    def load(self, out, in_, semkey, r, w, queue="pool"):
        self.sch.dma(queue, lambda e: e.dma_start(out=out, in_=in_), (self.stage, semkey), self.Bs(r), self.Bs(w))

    def store(self, out, in_, semkey, r, w, queue="pool"):
        self.sch.dma(queue, lambda e: e.dma_start(out=out, in_=in_), (self.stage, semkey), self.Bs(r), self.Bs(w))

    def setup_consts(self):
        G = lambda n: ("G", n)
        self.ones_bf = self.sb("ones_bf", [128, 128], BF16)
        self.memset(self.ones_bf[:], 1.0, [G("ones_bf")])
        self.ones_f = self.sb("ones_f", [128, 128], F32)
        self.memset(self.ones_f[:], 1.0, [G("ones_f")])
        self.eps_t = self.sb("eps_t", [128, 1], F32)
        self.memset(self.eps_t[:], EPS, [G("eps")])
        self.ident_f = self.sb("ident_f", [128, 128], F32)
        self.memset(self.ident_f[:], 1.0, [G("ident_f")])
        self.asel(self.ident_f[:], self.ident_f[:], [[-1, 128]], ALU.is_equal, 0.0, 0, 1, [G("ident_f")], [G("ident_f")])
        self.ident_b = self.sb("ident_b", [128, 128], BF16)
        self.cp(self.ident_b[:], self.ident_f[:], [G("ident_f")], [G("ident_b")], eng="pool")
        NG = 3 * self.NL
        self.gam = self.sb("gam", [128, NG, KC], F32)
        self.load(self.gam[:], self.I["gam_in"], "gam", (), [G("gam")])
        self.n128 = self.sb("n128", [128, self.NL, 72], F32)
        self.load(self.n128[:], self.I["n128"].rearrange("l p c -> p l c"), "n128", (), [G("n128")])
        self.qnw = self.sb("qnw", [128, self.NL], F32)
        for l in range(self.NL):
            self.ts(self.qnw[:, l:l + 1], self.n128[:, l, 1:2], 128.0 ** -0.5, None, ALU.mult, None,
                    [G("n128")], [G("qnw")])

    def alloc_rl(self):
        T = self.T
        c = self.carve
        self.xT = c([128, KC, T], F32)
        self.hT = c([128, KC, T], BF16)
        self.actflat = self.arena[:, self.aoff:self.aoff + FC * T // 2]
        self.actT = c([128, FC, T], BF16)
        self.uvT = self.actflat[:, 0:16 * T].rearrange("p (a b) -> p a b", a=16)
        self.wA = [c([128, KC, 128], BF16) for _ in range(2)]
        self.wB = [c([128, KC, 128], BF16) for _ in range(2)]
        self.wD = [c([128, FC, 128], BF16) for _ in range(2)]
        self.sq = [c([128, T], BF16) for _ in range(2)]
        self.sg = [c([128, T], F32) for _ in range(2)]
        self.rstd = c([128, T], F32)
        self.gt = [c([128, 3, T], F32) for _ in range(2)]
        self.ost = [c([128, T], F32) for _ in range(4)]
        self.cv = [c([128, T + 3], F32) for _ in range(2)]
        self.halo = c([128, 24, 3], F32)
        self.cvt = [c([128, T], F32) for _ in range(2)]
        self.convw = c([128, 24, 4], F32)
        self.wsm = c([128, KC, 64], BF16)
        self.smo = [c([128, 4, 64], F32) for _ in range(2)]
        self.smt = c([128, 32], F32)
        self.dnv = c([128, 16], F32)
        self.nA = c([128, 8], F32)
        self.sgnw = c([128, 16], F32)
        self.wsT = c([128, 8, 128], BF16)
        self.bsb = c([128, 8, 128], F32)
        self.lnst = c([128, 4, T], F32)
        self.wsTf = self.lnst[:, 0:2, :].rearrange("p a (b c) -> p (a b) c", c=128)
        self.vtok = [c([128, 4, 128], BF16) for _ in range(2)]
        self.ycT = c([128, 8, T], BF16)
        self.psM = [self.bank(i) for i in range(4)]
        self.psD = [self.bank(4), self.bank(5)]
        self.psS = self.bank(6)
        self.psX = self.bank(7)

    def xb(self):
        return [("xT", k) for k in range(KC)]

    def rmsnorm(self, gi):
        for kc in range(KC):
            s = self.nxt("sq")
            self.act(self.sq[s], self.xT[:, kc, :], AF.Square, [("xT", kc)], [("sq", s)])
            self.mm(self.psS, self.ones_bf[:], self.sq[s], kc == 0, kc == KC - 1, [("sq", s)], ["psS"])
        self.act(self.rstd, self.psS, AF.Sqrt, ["psS"], ["rstd"], bias=self.eps_t[:, 0:1], scale=1.0 / D)
        self.op("dve", "reciprocal", self.rstd, self.rstd, r=["rstd"], w=["rstd"])
        for kc in range(KC):
            self.stt(self.hT[:, kc, :], self.xT[:, kc, :], self.gam[:, gi, kc:kc + 1], self.rstd,
                     ALU.mult, ALU.mult, [("xT", kc), "rstd"], ["hT"])

    def ffn(self, gi, f, l):
        wg, wu, wd = self.Wb[f + "_gate_t"][l], self.Wb[f + "_up_t"][l], self.Wb[f + "_down_t"][l]
        dg, du, dd = [("G", "wbf", f + "_gate_t", l)], [("G", "wbf", f + "_up_t", l)], [("G", "wbf", f + "_down_t", l)]
        self.rmsnorm(gi)
        for fc in range(FC):
            s = self.nxt("wAB")
            self.load(self.wA[s], wg[fc], ("wA", s), dg, [("wA", s)], queue="sp")
            self.load(self.wB[s], wu[fc], ("wB", s), du, [("wB", s)], queue="sp")
            p = self.nxt("psM", 2)
            pa, pb = self.psM[2 * p], self.psM[2 * p + 1]
            ka, kb = ("psM", 2 * p), ("psM", 2 * p + 1)
            for kc in range(KC):
                self.mm(pa, self.wA[s][:, kc, :], self.hT[:, kc, :], kc == 0, kc == KC - 1, [("wA", s), "hT"], [ka])
            for kc in range(KC):
                self.mm(pb, self.wB[s][:, kc, :], self.hT[:, kc, :], kc == 0, kc == KC - 1, [("wB", s), "hT"], [kb])
            g = self.nxt("sg")
            self.act(self.sg[g], pa, AF.Silu, [ka], [("sg", g)])
            self.tt(self.actT[:, fc, :], self.sg[g], pb, ALU.mult, [("sg", g), kb], [("actT", fc)])
        for dc in range(KC):
            s = self.nxt("wD")
            self.load(self.wD[s], wd[dc], ("wD", s), dd, [("wD", s)], queue="sp")
            p = self.nxt("psD")
            for fc in range(FC):
                self.mm(self.psD[p], self.wD[s][:, fc, :], self.actT[:, fc, :], fc == 0, fc == FC - 1,
                        [("wD", s), ("actT", fc)], [("psD", p)])
            self.stt(self.xT[:, dc, :], self.psD[p], 0.5, self.xT[:, dc, :], ALU.mult, ALU.add,
                     [("psD", p), ("xT", dc)], [("xT", dc)])

    def merge(self, l, t0):
        T = self.T
        yT = self.actT
        yv = self.D["yT"].rearrange("(k p) s -> p k s", p=128)
        self.load(yT[:, 0:24, :], yv[:, :, t0:t0 + T], "yT", [("G", "yT", t0 // T)], [("actT", k) for k in range(24)])
        gv = self.D["gates"].rearrange("(b k p) s -> p b k s", b=3, p=128)
        wbr = self.Wb["w_br_t"]
        for dc in range(KC):
            gs = self.nxt("gt")
            self.load(self.gt[gs], gv[:, :, dc, t0:t0 + T], ("gt", gs), [("G", "gates", t0 // T)], [("gt", gs)])
            pks = []
            for br in range(3):
                if br < 2:
                    s = self.nxt("wAB")
                    wt = (self.wA, self.wB)[br][s]
                    wk = (("wA", s), ("wB", s))[br]
                else:
                    s = self.nxt("wAB")
                    wt = self.wA[s]
                    wk = ("wA", s)
                self.load(wt[:, 0:8, :], wbr[l, br, dc], wk, [("G", "wbf", "w_br_t", l)], [wk], queue="sp")
                p = self.nxt("psM4", 4)
                for kc in range(8):
                    self.mm(self.psM[p], wt[:, kc, :], yT[:, br * 8 + kc, :], kc == 0, kc == 7,
                            [wk, ("actT", br * 8 + kc)], [("psM", p)])
                pks.append(p)
            a = self.nxt("ost", 4)
            b = self.nxt("ost", 4)
            self.tt(self.ost[a], self.psM[pks[0]], self.gt[gs][:, 0, :], ALU.mult, [("psM", pks[0]), ("gt", gs)], [("ost", a)])
            self.tt(self.ost[b], self.psM[pks[1]], self.gt[gs][:, 1, :], ALU.mult, [("psM", pks[1]), ("gt", gs)], [("ost", b)])
            self.tt(self.ost[a], self.ost[a], self.ost[b], ALU.add, [("ost", a), ("ost", b)], [("ost", a)])
            self.tt(self.ost[b], self.psM[pks[2]], self.gt[gs][:, 2, :], ALU.mult, [("psM", pks[2]), ("gt", gs)], [("ost", b)])
            self.tt(self.hT[:, dc, :], self.ost[a], self.ost[b], ALU.add, [("ost", a), ("ost", b)], ["hT"])
        wo = self.Wb["w_out_t"]
        for dc in range(KC):
            s = self.nxt("wAB")
            self.load(self.wA[s], wo[l, dc], ("wA", s), [("G", "wbf", "w_out_t", l)], [("wA", s)], queue="sp")
            p = self.nxt("psD")
            for kc in range(KC):
                self.mm(self.psD[p], self.wA[s][:, kc, :], self.hT[:, kc, :], kc == 0, kc == KC - 1,
                        [("wA", s), "hT"], [("psD", p)])
            self.tt(self.xT[:, dc, :], self.psD[p], self.xT[:, dc, :], ALU.add, [("psD", p), ("xT", dc)], [("xT", dc)])

    def win_setup(self, l):
        self.load(self.convw, self.I["dn_conv_t"][l], "convw", (), ["convw"])
        self.load(self.wsm, self.I["w_sm_t"][l], "wsm", (), ["wsm"], queue="pool")
        self.load(self.dnv, self.I["dn_vec"][l].partition_broadcast(128), "dnv", (), ["dnv"])
        self.act(self.nA, self.dnv[:, 0:8], AF.Exp, ["dnv"], ["nA"])
        self.ts(self.nA, self.nA, -1.0, None, ALU.mult, None, ["nA"], ["nA"])
        self.load(self.sgnw, self.I["sgu_nw"][l], "sgnw", (), ["sgnw"])
        self.load(self.wsTf, self.I["sgu_wT"][l], "wsTf", (), ["wsTf", "ln_mu", "ln_var"])
        for g in range(8):
            self.asel(self.wsTf[:, g, :], self.wsTf[:, g, :], [[1, 128]], ALU.is_ge, 0.0, 0, -1, ["wsTf"], ["wsTf", "ln_mu", "ln_var"])
        self.cp(self.wsT, self.wsTf, ["wsTf", "ln_mu", "ln_var"], ["wsT"], eng="pool")
        self.load(self.bsb, self.I["sgu_b"][l].rearrange("g i -> (g i)").partition_broadcast(128)
                  .rearrange("p (g i) -> p g i", g=8), "bsb", (), ["bsb"])
        self.memset(self.halo, 0.0, ["halo"])

    def ostage_store(self, dst, src_key_slot, tix, dkey):
        a = src_key_slot
        self.store(dst, self.ost[a], ("ost", a), [("ost", a)], [("G", dkey, tix)])

    def win(self, l, t0):
        T = self.T
        tix = t0 // T
        Dm = self.D
        self.rmsnorm(3 * l + 1)
        so = self.nxt("smo")
        for sub in range(4):
            pX = self.psX[:, sub * 64:(sub + 1) * 64]
            for kc in range(KC):
                self.mm(pX, self.hT[:, kc, sub * 128:(sub + 1) * 128], self.wsm[:, kc, :], kc == 0, kc == KC - 1,
                        ["hT", "wsm"], ["psX"])
        pv = self.psX[:, 0:256].rearrange("p (s c) -> p s c", s=4)
        smt = self.smt.rearrange("p (s c) -> p s c", s=4)
        self.tt(smt, pv[:, :, 0:8], self.dnv[:, 8:16].unsqueeze(1).to_broadcast([128, 4, 8]), ALU.add,
                ["psX", "dnv"], ["smt"])
        self.act(smt, smt, AF.Exp, ["smt"], ["smt"])
        self.act(smt, smt, AF.Ln, ["smt"], ["smt"], bias=1.0)
        self.tt(self.smo[so][:, :, 0:8], smt, self.nA.unsqueeze(1).to_broadcast([128, 4, 8]), ALU.mult,
                ["smt", "nA"], [("smo", so)])
        self.act(self.smo[so][:, :, 8:40], pv[:, :, 8:40], AF.Sigmoid, ["psX"], [("smo", so)])
        self.store(Dm["sm_tok"][t0:t0 + T, :].rearrange("(s p) c -> p s c", p=128), self.smo[so],
                   ("smo", so), [("smo", so)], [("G", "sm_tok", tix)])
        win_t = self.Wb["w_in_t"]
        uT = self.uvT[:, 0:8, :]
        vfT = self.uvT[:, 8:16, :]
        uvk = lambda j: [("actT", 2 * j), ("actT", 2 * j + 1)]
        for c in range(NWIN):
            s = self.nxt("wAB")
            self.load(self.wA[s], win_t[l, c], ("wA", s), [("G", "wbf", "w_in_t", l)], [("wA", s)], queue="sp")
            p = self.nxt("psM4", 4)
            ps, pk = self.psM[p], ("psM", p)
            for kc in range(KC):
                self.mm(ps, self.wA[s][:, kc, :], self.hT[:, kc, :], kc == 0, kc == KC - 1, [("wA", s), "hT"], [pk])
            if c < 24:
                v = self.nxt("cv")
                cv = self.cv[v]
                self.cp(cv[:, 3:T + 3], ps, [pk], [("cv", v)], eng="act")
                self.cp(cv[:, 0:3], self.halo[:, c, :], ["halo"], [("cv", v)], eng="pool")
                t = self.nxt("cvt")
                ct = self.cvt[t]
                self.ts(ct, cv[:, 0:T], self.convw[:, c, 0:1], None, ALU.mult, None, [("cv", v), "convw"], [("cvt", t)])
                for j in range(1, 4):
                    self.stt(ct, cv[:, j:T + j], self.convw[:, c, j:j + 1], ct, ALU.mult, ALU.add,
                             [("cv", v), "convw", ("cvt", t)], [("cvt", t)])
                self.cp(self.halo[:, c, :], cv[:, T:T + 3], [("cv", v)], ["halo"], eng="pool")
                a = self.nxt("ost", 4)
                if c < 16:
                    self.act(ct, ct, AF.Silu, [("cvt", t)], [("cvt", t)])
                    q = self.nxt("sq")
                    self.act(self.sq[q], ct, AF.Square, [("cvt", t)], [("sq", q)])
                    self.mm(self.psS, self.ones_bf[:], self.sq[q], True, True, [("sq", q)], ["psS"])
                    self.act(self.rstd, self.psS, AF.Sqrt, ["psS"], ["rstd"], bias=self.eps_t[:, 0:1])
                    self.op("dve", "reciprocal", self.rstd, self.rstd, r=["rstd"], w=["rstd"])
                    self.stt(self.ost[a], ct, (128.0 ** -0.5) if c < 8 else 1.0, self.rstd, ALU.mult, ALU.mult,
                             [("cvt", t), "rstd"], [("ost", a)])
                else:
                    self.act(self.ost[a], ct, AF.Silu, [("cvt", t)], [("ost", a)])
                self.store(Dm["dn_qkv"][c * 128:(c + 1) * 128, t0:t0 + T], self.ost[a], ("ost", a),
                           [("ost", a)], [("G", "dn_qkv", tix)])
            elif c < 32:
                a = self.nxt("ost", 4)
                self.act(self.ost[a], ps, AF.Silu, [pk], [("ost", a)])
                self.store(Dm["dn_zg"][(c - 24) * 128:(c - 23) * 128, t0:t0 + T], self.ost[a], ("ost", a),
                           [("ost", a)], [("G", "dn_zg", tix)])
            elif c < 40 or c in (44, 45, 48, 49):
                q = self.nxt("sq")
                self.act(self.sq[q], ps, AF.Square, [pk], [("sq", q)])
                self.mm(self.psS, self.ones_bf[:], self.sq[q], True, True, [("sq", q)], ["psS"])
                self.act(self.rstd, self.psS, AF.Sqrt, ["psS"], ["rstd"], bias=self.eps_t[:, 0:1], scale=1.0 / 128)
                self.op("dve", "reciprocal", self.rstd, self.rstd, r=["rstd"], w=["rstd"])
                a = self.nxt("ost", 4)
                ob = self.ost[a].bitcast(BF16)[:, 0:T]
                if c < 40:
                    gcol = self.qnw[:, l:l + 1]
                    dst = Dm["ns_q"][(c - 32) * 128:(c - 31) * 128, t0:t0 + T]
                    dk = "ns_q"
                else:
                    kn = 3 if c < 48 else 4
                    gcol = self.n128[:, l, kn:kn + 1]
                    row = {44: 0, 45: 1, 48: 2, 49: 3}[c]
                    dst = Dm["ns_k"][row * 128:(row + 1) * 128, t0:t0 + T]
                    dk = "ns_k"
                self.stt(ob, ps, gcol, self.rstd, ALU.mult, ALU.mult, [pk, "rstd", ("G", "n128"), ("G", "qnw")], [("ost", a)])
                self.store(dst, ob, ("ost", a), [("ost", a)], [("G", dk, tix)])
            elif c < 44:
                a = self.nxt("ost", 4)
                self.cp(self.ost[a], ps, [pk], [("ost", a)], eng="act")
                self.store(Dm["ns_kvc"][(c - 40) * 128:(c - 39) * 128, t0:t0 + T], self.ost[a], ("ost", a),
                           [("ost", a)], [("G", "ns_kvc", tix)])
            elif c < 52:
                a = self.nxt("ost", 4)
                self.cp(self.ost[a], ps, [pk], [("ost", a)], eng="act")
                for sub in range(4):
                    self.tr(self.psX[:, sub * 128:(sub + 1) * 128], self.ost[a][:, sub * 128:(sub + 1) * 128],
                            self.ident_f[:], [("ost", a)], ["psX"])
                vt = self.nxt("vtok")
                self.cp(self.vtok[vt], self.psX.rearrange("p (s c) -> p s c", s=4), ["psX"], [("vtok", vt)])
                row = {46: 0, 47: 1, 50: 2, 51: 3}[c]
                self.store(Dm["ns_v"][row, t0:t0 + T, :].rearrange("(s p) d -> p s d", p=128), self.vtok[vt],
                           ("vtok", vt), [("vtok", vt)], [("G", "ns_v", tix)])
            elif c < 60:
                g = c - 52
                self.act(uT[:, g, :], ps, AF.Gelu_apprx_tanh, [pk], uvk(g))
            elif c < 68:
                g = c - 60
                self.act(vfT[:, g, :], ps, AF.Gelu_apprx_tanh, [pk], uvk(8 + g))
                q = self.nxt("sq")
                self.cp(self.sq[q], vfT[:, g, :], uvk(8 + g), [("sq", q)])
                self.mm(self.psD[0], self.ones_bf[:], self.sq[q], g == 0, g == 7, [("sq", q)], [("psD", 0)])
                q = self.nxt("sq")
                self.act(self.sq[q], vfT[:, g, :], AF.Square, uvk(8 + g), [("sq", q)])
                self.mm(self.psD[1], self.ones_bf[:], self.sq[q], g == 0, g == 7, [("sq", q)], [("psD", 1)])
                if g == 7:
                    self.sgu_finish(l, t0)
            else:
                a = self.nxt("ost", 4)
                self.act(self.ost[a], ps, AF.Sigmoid, [pk], [("ost", a)])
                self.store(Dm["gates"][(c - 68) * 128:(c - 67) * 128, t0:t0 + T], self.ost[a], ("ost", a),
                           [("ost", a)], [("G", "gates", tix)])

    def sgu_finish(self, l, t0):
        T = self.T
        tix = t0 // T
        uT = self.uvT[:, 0:8, :]
        vfT = self.uvT[:, 8:16, :]
        uvk = lambda j: [("actT", 2 * j), ("actT", 2 * j + 1)]
        mu, var, rs, tmp = (self.lnst[:, i, :] for i in range(4))
        self.ts(mu, self.psD[0], 1.0 / 1024, None, ALU.mult, None, [("psD", 0)], ["ln_mu"])
        self.tt(var, mu, mu, ALU.mult, ["ln_mu"], ["ln_var"])
        self.stt(var, self.psD[1], 1.0 / 1024, var, ALU.mult, ALU.subtract, [("psD", 1), "ln_var"], ["ln_var"])
        self.act(rs, var, AF.Sqrt, ["ln_var"], ["ln_rs"], bias=self.eps_t[:, 0:1])
        self.op("dve", "reciprocal", rs, rs, r=["ln_rs"], w=["ln_rs"])
        for g in range(8):
            v = self.nxt("cvt")
            vn = self.cvt[v]
            self.tt(vn, vfT[:, g, :], mu, ALU.subtract, uvk(8 + g) + ["ln_mu"], [("cvt", v)])
            self.tt(vn, vn, rs, ALU.mult, [("cvt", v), "ln_rs"], [("cvt", v)])
            self.ts(vn, vn, self.sgnw[:, g:g + 1], self.sgnw[:, 8 + g:9 + g], ALU.mult, ALU.add,
                    [("cvt", v), "sgnw"], [("cvt", v)])
            for sub in range(4):
                self.tr(self.psX[:, sub * 128:(sub + 1) * 128], vn[:, sub * 128:(sub + 1) * 128], self.ident_f[:],
                        [("cvt", v)], ["psX"])
            vt = self.nxt("vtok")
            self.cp(self.vtok[vt], self.psX.rearrange("p (s c) -> p s c", s=4), ["psX"], [("vtok", vt)], eng="act")
            p = self.nxt("psD")
            for sub in range(4):
                self.mm(self.psD[p][:, sub * 128:(sub + 1) * 128], self.vtok[vt][:, sub, :], self.wsT[:, g, :],
                        True, True, [("vtok", vt), "wsT"], [("psD", p)])
            y = self.nxt("sg")
            self.tt(self.sg[y].rearrange("p (s c) -> p s c", s=4), self.psD[p].rearrange("p (s c) -> p s c", s=4),
                    self.bsb[:, g, :].unsqueeze(1).to_broadcast([128, 4, 128]), ALU.add,
                    [("psD", p), "bsb"], [("sg", y)])
            self.tt(self.ycT[:, g, :], self.sg[y], uT[:, g, :], ALU.mult, [("sg", y)] + uvk(g), ["ycT"])
        yv = self.D["yT"].rearrange("(k p) s -> p k s", p=128)
        self.store(yv[:, 16:24, t0:t0 + T], self.ycT, "ycT", ["ycT"], [("G", "yT", tix)])

    def rl_stage(self, name, l_merge, l_ffn2, l_ffn1, l_win, src, dst, dst_key):
        S, T = self.S, self.T
        self.stage_begin(name)
        self.alloc_rl()
        if l_win is not None:
            self.win_setup(l_win)
        I = self.I
        srcv = src.rearrange("(k p) s -> p k s", p=128)
        dstv = dst.rearrange("(k p) s -> p k s", p=128)
        for ti in range(S // T):
            t0 = ti * T
            self.load(self.xT, srcv[:, :, t0:t0 + T], "xT", [("G", "xTd", ti)], self.xb())
            if l_merge is not None:
                self.merge(l_merge, t0)
            if l_ffn2 is not None:
                self.ffn(3 * l_ffn2 + 2, "ffn2", l_ffn2)
            if l_ffn1 is not None:
                self.ffn(3 * l_ffn1 + 0, "ffn1", l_ffn1)
            self.store(dstv[:, :, t0:t0 + T], self.xT, "xT", self.xb(), [("G", dst_key, ti)])
            if l_win is not None:
                self.win(l_win, t0)

    def pslot(self):
        i = self.nxt("pslot", 4)
        v = self.bank(2 * i, 2).rearrange("p (h c) -> p h c", h=8)
        return v, ("pslot", i)

    def pslot_bf(self):
        i = self.nxt("pslot", 4)
        v = self.bank(2 * i, 1).bitcast(BF16).rearrange("p (h c) -> p h c", h=8)
        return v, ("pslot", i)

    def dn_stage(self, l):
        S, T = self.S, self.T
        self.stage_begin("dn%d" % l)
        c = self.carve
        Dm = self.D
        H = 8
        sh = [128, H, 128]
        triC = c([128, 128], F32)
        pmask = c([128, 128], F32)
        sl01 = c([128, 128], F32)
        self.memset(triC, 1.0, ["triC"])
        self.asel(triC, triC, [[1, 128]], ALU.is_ge, 0.0, 0, -1, ["triC"], ["triC"])
        self.memset(pmask, 0.0, ["pmask"])
        self.asel(pmask, pmask, [[-1, 128]], ALU.is_ge, 30000.0, 0, 1, ["pmask"], ["pmask"])
        self.memset(sl01, 1.0, ["sl01"])
        self.asel(sl01, sl01, [[-1, 128]], ALU.is_gt, 0.0, 0, 1, ["sl01"], ["sl01"])
        St = c(sh, F32)
        Sb = c(sh, BF16)
        self.memset(St, 0.0, ["St"])
        self.memset(Sb, 0.0, ["Sb"])
        qT = [c(sh, F32) for _ in range(2)]
        kT = [c(sh, F32) for _ in range(2)]
        vT = [c(sh, F32) for _ in range(2)]
        zg = [c(sh, F32) for _ in range(2)]
        sm = [c([128, 64], F32) for _ in range(2)]
        gcs = c([128, 16], F32)
        sc = c([128, 40], F32)
        Gbc = c(sh, F32)
        qTb = c(sh, BF16)
        kTb = c(sh, BF16)
        dec = c(sh, F32)
        decs = c(sh, F32)
        t1 = c(sh, F32)
        Lp = [c(sh, F32) for _ in range(2)]
        Up = [c(sh, F32) for _ in range(2)]
        Xp = [c(sh, F32) for _ in range(2)]
        At = c(sh, BF16)
        AtT = c(sh, BF16)
        kbg = c(sh, F32)
        kdec = c(sh, BF16)
        vb = c(sh, F32)
        u = c(sh, F32)
        wTb = c(sh, F32)
        vnew = c(sh, BF16)
        o = c(sh, F32)
        osq = c(sh, F32)
        ss = c([128, 8], F32)
        ya = [c(sh, BF16) for _ in range(2)]
        onw = self.n128[:, l, 0:1]
        qv = Dm["dn_qkv"].rearrange("(t h d) s -> t d h s", t=3, h=H)
        zv = Dm["dn_zg"].rearrange("(h d) s -> d h s", h=H)
        yv = Dm["yT"].rearrange("(k p) s -> p k s", p=128)

        def bc(col):
            return col.unsqueeze(2).to_broadcast(sh)

        for n in range(S // 128):
            t0 = n * 128
            tix = t0 // T
            sl = n % 2
            self.load(qT[sl], qv[0][:, :, t0:t0 + 128], ("qT", sl), [("G", "dn_qkv", tix)], [("qT", sl)])
            self.load(kT[sl], qv[1][:, :, t0:t0 + 128], ("kT", sl), [("G", "dn_qkv", tix)], [("kT", sl)])
            self.load(vT[sl], qv[2][:, :, t0:t0 + 128], ("vT", sl), [("G", "dn_qkv", tix)], [("vT", sl)])
            self.load(zg[sl], zv[:, :, t0:t0 + 128], ("zg", sl), [("G", "dn_zg", tix)], [("zg", sl)])
            self.load(sm[sl], Dm["sm_tok"][t0:t0 + 128, :], ("sm", sl), [("G", "sm_tok", tix)], [("sm", sl)])
            g8 = sm[sl][:, 0:8]
            beta = sm[sl][:, 8:16]
            pG, kG = self.pslot()
            pg = pG[:, 0, 0:16]
            self.mm(pg[:, 0:8], triC, g8, True, True, ["triC", ("sm", sl)], [kG])
            self.mm(pg[:, 8:16], self.ones_f[:], g8, True, True, [("sm", sl)], [kG])
            self.cp(gcs, pg, [kG], ["gcs"])
            gc = gcs[:, 0:8]
            glb = gcs[:, 8:16]
            self.act(sc[:, 0:8], gc, AF.Exp, ["gcs"], ["sc"])
            self.tt(sc[:, 8:16], glb, gc, ALU.subtract, ["gcs"], ["sc"])
            self.act(sc[:, 8:16], sc[:, 8:16], AF.Exp, ["sc"], ["sc"])
            self.act(sc[:, 16:24], glb, AF.Exp, ["gcs"], ["sc"])
            self.tt(sc[:, 24:32], beta, sc[:, 0:8], ALU.mult, [("sm", sl), "sc"], ["sc"])
            egc, edl, egl, bg = sc[:, 0:8], sc[:, 8:16], sc[:, 16:24], sc[:, 24:32]
            self.cp(Gbc, bc(g8), [("sm", sl)], ["Gbc"], eng="pool")
            self.cp(qTb, qT[sl], [("qT", sl)], ["qTb"], eng="pool")
            self.cp(kTb, kT[sl], [("kT", sl)], ["kTb"])
            pR, kR = self.pslot()
            for h in range(H):
                self.mm(pR[:, h, :], Gbc[:, h, :], triC, True, False, ["Gbc", "triC"], [kR])
                self.mm(pR[:, h, :], self.ident_f[:], pmask, False, True, ["pmask"], [kR])
            self.tt(t1, pR, bc(gc), ALU.subtract, [kR, "gcs"], ["t1"])
            self.act(dec, t1, AF.Exp, ["t1"], ["dec"], scale=-1.0)
            self.tt(decs, dec, sl01.unsqueeze(1).to_broadcast(sh), ALU.mult, ["dec", "sl01"], ["decs"], eng="pool")
            pKK, kKK = self.pslot()
            for h in range(H):
                self.mm(pKK[:, h, :], kT[sl][:, h, :], kT[sl][:, h, :], True, True, [("kT", sl)], [kKK])
            pQK, kQK = self.pslot()
            for h in range(H):
                self.mm(pQK[:, h, :], qTb[:, h, :], kTb[:, h, :], True, True, ["qTb", "kTb"], [kQK])
            self.tt(t1, pKK, decs, ALU.mult, [kKK, "decs"], ["t1"])
            self.tt(Lp[0], t1, bc(beta), ALU.mult, ["t1", ("sm", sl)], [("Lp", 0)])
            self.tt(At, pQK, dec, ALU.mult, [kQK, "dec"], ["At"])
            pU, kU = self.pslot()
            for h in range(H):
                self.tr(pU[:, h, :], Lp[0][:, h, :], self.ident_f[:], [("Lp", 0)], [kU])
            self.cp(Up[0], pU, [kU], [("Up", 0)], eng="act")
            pA, kA = self.pslot_bf()
            for h in range(H):
                self.tr(pA[:, h, :], At[:, h, :], self.ident_b[:], ["At"], [kA])
            self.cp(AtT, pA, [kA], ["AtT"], eng="act")
            self.tt(Xp[0], self.ident_f[:].unsqueeze(1).to_broadcast(sh), Up[0], ALU.subtract, [("Up", 0)], [("Xp", 0)])
            cur = 0
            xc = 0
            for step in range(6):
                nx = 1 - cur
                pL, kL = self.pslot()
                for h in range(H):
                    self.mm(pL[:, h, :], Up[cur][:, h, :], Lp[cur][:, h, :], True, True, [("Up", cur), ("Lp", cur)], [kL])
                if step < 5:
                    pU2, kU2 = self.pslot()
                    for h in range(H):
                        self.mm(pU2[:, h, :], Lp[cur][:, h, :], Up[cur][:, h, :], True, True,
                                [("Up", cur), ("Lp", cur)], [kU2])
                self.cp(Lp[nx], pL, [kL], [("Lp", nx)], eng="act")
                if step < 5:
                    self.cp(Up[nx], pU2, [kU2], [("Up", nx)])
                pX, kX = self.pslot()
                for h in range(H):
                    self.mm(pX[:, h, :], Lp[nx][:, h, :], Xp[xc][:, h, :], True, True, [("Lp", nx), ("Xp", xc)], [kX])
                self.tt(Xp[1 - xc], pX, Xp[xc], ALU.add, [kX, ("Xp", xc)], [("Xp", 1 - xc)])
                xc = 1 - xc
                cur = nx
            X = Xp[xc]
            kXk = ("Xp", xc)
            pKt, kKt = self.pslot()
            for h in range(H):
                self.tr(pKt[:, h, :], kT[sl][:, h, :], self.ident_f[:], [("kT", sl)], [kKt])
            self.tt(kbg, pKt, bc(bg), ALU.mult, [kKt, "sc"], ["kbg"])
            self.tt(kdec, pKt, bc(edl), ALU.mult, [kKt, "sc"], ["kdec"])
            pVt, kVt = self.pslot()
            for h in range(H):
                self.tr(pVt[:, h, :], vT[sl][:, h, :], self.ident_f[:], [("vT", sl)], [kVt])
            self.tt(vb, pVt, bc(beta), ALU.mult, [kVt, ("sm", sl)], ["vb"])
            pu, ku = self.pslot()
            for h in range(H):
                self.mm(pu[:, h, :], X[:, h, :], vb[:, h, :], True, True, [kXk, "vb"], [ku])
            self.cp(u, pu, [ku], ["u"], eng="act")
            pw, kw = self.pslot()
            for h in range(H):
                self.mm(pw[:, h, :], kbg[:, h, :], X[:, h, :], True, True, [kXk, "kbg"], [kw])
            self.cp(wTb, pw, [kw], ["wTb"], eng="act")
            pWS, kWS = self.pslot()
            for h in range(H):
                self.mm(pWS[:, h, :], wTb[:, h, :], St[:, h, :], True, True, ["wTb", "St"], [kWS])
            self.tt(vnew, u, pWS, ALU.subtract, ["u", kWS], ["vnew"])
            pQS, kQS = self.pslot()
            for h in range(H):
                self.mm(pQS[:, h, :], qTb[:, h, :], Sb[:, h, :], True, True, ["qTb", "Sb"], [kQS])
            pAV, kAV = self.pslot()
            for h in range(H):
                self.mm(pAV[:, h, :], AtT[:, h, :], vnew[:, h, :], True, True, ["AtT", "vnew"], [kAV])
            self.tt(o, pQS, bc(egc), ALU.mult, [kQS, "sc"], ["o"])
            self.tt(o, o, pAV, ALU.add, ["o", kAV], ["o"])
            pKV, kKV = self.pslot()
            for h in range(H):
                self.mm(pKV[:, h, :], kdec[:, h, :], vnew[:, h, :], True, True, ["kdec", "vnew"], [kKV])
            self.tt(St, St, bc(egl), ALU.mult, ["St", "sc"], ["St"], eng="pool")
            self.tt(St, St, pKV, ALU.add, ["St", kKV], ["St"])
            self.cp(Sb, St, ["St"], ["Sb"], eng="act")
            self.act(osq, o, AF.Square, ["o"], ["osq"])
            self.op("dve", "tensor_reduce", ss, osq, AX.X, ALU.add, r=["osq"], w=["ss"])
            self.act(ss, ss, AF.Sqrt, ["ss"], ["ss"], bias=self.eps_t[:, 0:1], scale=1.0 / 128)
            self.op("dve", "reciprocal", ss, ss, r=["ss"], w=["ss"])
            self.tt(o, o, bc(ss), ALU.mult, ["o", "ss"], ["o"])
            pO, kO = self.pslot()
            for h in range(H):
                self.tr(pO[:, h, :], o[:, h, :], self.ident_f[:], ["o"], [kO])
            self.stt(ya[sl], pO, onw, zg[sl], ALU.mult, ALU.mult, [kO, ("zg", sl)], [("ya", sl)])
            self.store(yv[:, 0:8, t0:t0 + 128], ya[sl], ("ya", sl), [("ya", sl)], [("G", "yT", tix)])

    def nsa_stage(self, l):
        S, T = self.S, self.T
        self.stage_begin("nsa%d" % l)
        c = self.carve
        Dm = self.D
        I = self.I
        NQB = S // 128
        NCMP = S // 16 - 1
        NCP = NCMP + 1
        NCC = (NCP + 127) // 128
        NSEL = S // 64
        sh4 = [128, 4, 128]
        cmask = c([128, 16, 128], BF16)
        cmf = c([128, 128], F32)
        for a in range(16):
            self.memset(cmf, 0.0, ["cmf"])
            self.asel(cmf, cmf, [[1, 128]], ALU.is_ge, -30000.0, 128 * a - 15, -16, ["cmf"], ["cmf"])
            self.cp(cmask[:, a, :], cmf, ["cmf"], ["cmask"], eng="pool")
        caus = c([128, 128], BF16)
        self.memset(cmf, 0.0, ["cmf"])
        self.asel(cmf, cmf, [[1, 128]], ALU.is_ge, -30000.0, 0, -1, ["cmf"], ["cmf"])
        self.cp(caus, cmf, ["cmf"], ["caus"], eng="pool")
        wlow = c([128, 128], BF16)
        self.memset(cmf, 0.0, ["cmf"])
        self.asel(cmf, cmf, [[-1, 128]], ALU.is_gt, -30000.0, 0, 1, ["cmf"], ["cmf"])
        self.cp(wlow, cmf, ["cmf"], ["wlow"], eng="pool")
        Ebig = c([128, S], BF16)
        ebf = c([128, 512], F32)
        for k0 in range(0, S, 512):
            self.memset(ebf, 1.0, ["ebf"])
            self.asel(ebf, ebf, [[1, 512]], ALU.is_ge, 0.0, k0, -64, ["ebf"], ["ebf"])
            self.asel(ebf, ebf, [[-1, 512]], ALU.is_ge, 0.0, 63 - k0, 64, ["ebf"], ["ebf"])
            self.cp(Ebig[:, k0:k0 + 512], ebf, ["ebf"], ["Ebig"], eng="pool")
        ovl = c([128, NCC, 128], F32)
        ov2 = c([128, NCC, 128], F32)
        for (t, base) in ((ovl, 0), (ov2, -1)):
            nm = "ovl" if base == 0 else "ov2"
            self.memset(t, 1.0, [nm])
            self.asel(t, t, [[128, NCC], [-4, 128]], ALU.is_ge, 0.0, base, 1, [nm], [nm])
            self.asel(t, t, [[-128, NCC], [4, 128]], ALU.is_ge, 0.0, 3 - base, -1, [nm], [nm])
        self.tt(ovl, ovl, ov2, ALU.add, ["ovl", "ov2"], ["ovl"], eng="pool")
        self.memset(ovl[0:1, 0, :], 0.0, ["ovl"])
        slr = c([1, 8], F32)
        for h in range(8):
            self.memset(slr[:, h:h + 1], SLOPES[h], ["slr"])
        NM = NQB + 16
        tabf = c([1, NM, 8], F32)
        self.sch.op("pool", lambda e: e.iota(tabf, pattern=[[128, NM], [0, 8]], base=-128 * (NQB - 1),
                                             channel_multiplier=0, allow_small_or_imprecise_dtypes=True),
                    (), self.Bs(["tabf"]))
        self.tt(tabf, tabf, slr.unsqueeze(1).to_broadcast([1, NM, 8]), ALU.mult, ["tabf", "slr"], ["tabf"], eng="pool")
        tabm = c([1, NM, 8], BF16)
        self.cp(tabm, tabf, ["tabf"], ["tabm"], eng="pool")
        ones1 = c([1, 128], BF16)
        self.memset(ones1, 1.0, ["ones1"])
        a2f = c([2, 128], F32)
        aLc = c([2, 128], BF16)
        aLs = c([2, 128], BF16)
        self.memset(a2f, 1.0, ["a2f"])
        self.sch.op("pool", lambda e: e.iota(a2f[0:1, :], pattern=[[16, 128]], base=0, channel_multiplier=0,
                                             allow_small_or_imprecise_dtypes=True), (), self.Bs(["a2f"]))
        self.cp(aLc, a2f, ["a2f"], ["aLc"], eng="pool")
        self.sch.op("pool", lambda e: e.iota(a2f[0:1, :], pattern=[[1, 128]], base=0, channel_multiplier=0,
                                             allow_small_or_imprecise_dtypes=True), self.Bs(["a2f"]), self.Bs(["a2f"]))
        self.cp(aLs, a2f, ["a2f"], ["aLs"], eng="pool")
        r2f = c([2, 8, 128], F32)
        aRc = c([2, 8, 128], BF16)
        aRs = c([2, 8, 128], BF16)
        for (dst, nm, b0, st) in ((aRc, "aRc", 15, -1), (aRs, "aRs", 0, -1)):
            self.sch.op("pool", lambda e, b0=b0, st=st: e.iota(r2f, pattern=[[0, 8], [st, 128]], base=b0,
                                                               channel_multiplier=0, allow_small_or_imprecise_dtypes=True),
                        self.Bs(["r2f"]), self.Bs(["r2f"]))
            self.memset(r2f[0:1, :, :], 1.0, ["r2f"])
            self.tt(r2f, r2f, self.slr2(slr, c), ALU.mult, ["r2f", "slr2"], ["r2f"], eng="pool")
            self.cp(dst, r2f, ["r2f"], [nm], eng="pool")
        kcT = c([128, 2, NCC * 128], BF16)
        vca = c([128, NCC, 2, 257], BF16)
        self.memset(kcT, 0.0, ["kcT"])
        self.memset(vca, 0.0, ["vca"])
        for g in range(2):
            self.cp(vca[:, :, g, 129:257], ovl, ["ovl"], ["vca"], eng="pool")
            self.memset(vca[:, :, g, 128:129], 1.0, ["vca"])
            self.memset(vca[0:1, 0, g, 128:129], 0.0, ["vca"])
        amark = self.aoff
        xc = c([128, S], F32)
        xpb = c([128, 32, 512], BF16)
        w1 = c([128, 32, 256], BF16)
        w2 = c([128, 2, 128], BF16)
        hb = c([128, 2, 512], BF16)
        vcf = c([128, NCC * 128], F32)
        cst = c([128, 512], F32)
        csq = c([128, 512], BF16)
        crs = c([128, 512], F32)
        xcv = xc.rearrange("p (i r) -> p i r", r=16)
        for kv in range(2):
            self.load(w1, I["cmp_w1_t"][l, kv], "w1", (), ["w1"], queue="pool")
            self.load(w2, I["cmp_w2_t"][l, kv], "w2", (), ["w2"], queue="pool")
            for g in range(2):
                row = kv * 2 + g
                self.load(xc, Dm["ns_kvc"][row * 128:(row + 1) * 128, :], "xc",
                          [("G", "ns_kvc", i) for i in range(S // T)], ["xc"])
                for p in range(32):
                    src = xcv[:, 0:NCMP, p] if p < 16 else xcv[:, 1:NCMP + 1, p - 16]
                    self.ts(xpb[:, p, 0:NCMP], src, self.n128[:, l, 8 + 32 * kv + p:9 + 32 * kv + p], None, ALU.add, None,
                            ["xc"], ["xpb"], eng=("dve" if p % 2 == 0 else "pool"))
                for hc in range(2):
                    pH = self.bank(hc)
                    for p in range(32):
                        self.mm(pH[:, 0:NCMP], w1[:, p, hc * 128:(hc + 1) * 128], xpb[:, p, 0:NCMP], p == 0, p == 31,
                                ["w1", "xpb"], [("pb", hc)])
                    self.act(hb[:, hc, 0:NCMP], pH[:, 0:NCMP], AF.Silu, [("pb", hc)], ["hb"])
                pO = self.bank(2)
                for hc in range(2):
                    self.mm(pO[:, 0:NCMP], w2[:, hc, :], hb[:, hc, 0:NCMP], hc == 0, hc == 1, ["w2", "hb"], [("pb", 2)])
                if kv == 0:
                    self.act(csq[:, 0:NCMP], pO[:, 0:NCMP], AF.Square, [("pb", 2)], ["csq"])
                    pS = self.bank(3)
                    self.mm(pS[:, 0:NCMP], self.ones_bf[:], csq[:, 0:NCMP], True, True, ["csq"], [("pb", 3)])
                    self.act(crs[:, 0:NCMP], pS[:, 0:NCMP], AF.Sqrt, [("pb", 3)], ["crs"], bias=self.eps_t[:, 0:1],
                             scale=1.0 / 128)
                    self.op("dve", "reciprocal", crs[:, 0:NCMP], crs[:, 0:NCMP], r=["crs"], w=["crs"])
                    self.stt(kcT[:, g, 1:NCP], pO[:, 0:NCMP], self.n128[:, l, 2:3], crs[:, 0:NCMP], ALU.mult, ALU.mult,
                             [("pb", 2), "crs"], ["kcT"])
                else:
                    self.memset(vcf, 0.0, ["vcf"])
                    self.cp(vcf[:, 1:NCP], pO[:, 0:NCMP], [("pb", 2)], ["vcf"], eng="act")
                    pT = self.bank(3)
                    for k in range(NCC):
                        self.tr(pT[:, k * 128:(k + 1) * 128], vcf[:, k * 128:(k + 1) * 128], self.ident_f[:],
                                ["vcf"], [("pb", 3)])
                    self.cp(vca[:, :, g, 0:128], pT[:, 0:NCC * 128].rearrange("p (k d) -> p k d", d=128),
                            [("pb", 3)], ["vca"])
        self.sch.barrier()
        self.aoff = amark
        ksT = c([128, 2, S], BF16)
        vsa = c([128, NQB, 2, 129], BF16)
        for g in range(2):
            self.load(ksT[:, g, :], Dm["ns_k"][g * 128:(g + 1) * 128, :], "ksT",
                      [("G", "ns_k", i) for i in range(S // T)], ["ksT"])
            self.load(vsa[:, :, g, 0:128], Dm["ns_v"][g].rearrange("(n p) d -> p n d", p=128), "vsa",
                      [("G", "ns_v", i) for i in range(S // T)], ["vsa"])
            self.memset(vsa[:, :, g, 128:129], 1.0, ["vsa"])
        kwT = [c([128, 2, 640], BF16) for _ in range(2)]
        vwa = [c([128, 5, 2, 129], BF16) for _ in range(2)]
        for i in range(2):
            self.memset(vwa[i][:, :, :, 128:129], 1.0, [("vwa", i)])
        qT = [c([128, 8, 128], BF16) for _ in range(2)]
        gq = [c([128, 64], F32) for _ in range(2)]
        eT = [c([128, 512], BF16) for _ in range(3)]
        ev = c([128, 4, 257], F32)
        evs = c([128, 4, 129], F32)
        rs = c([128, 8], F32)
        wv = c([128, 8], F32)
        tmp4 = c(sh4, F32)
        imp = c([128, 128], F32)
        imp2 = c([128, 128], F32)
        m8 = c([128, 16], F32)
        nsel = c([128, 128], F32)
        nselT = c([128, 4, 128], BF16)
        onsa = c([128, 8, 128], F32)
        ybT = [c([128, 8, 128], BF16) for _ in range(2)]
        qv = Dm["ns_q"].rearrange("(h d) s -> d h s", h=8)
        yv = Dm["yT"].rearrange("(k p) s -> p k s", p=128)
        pSb = [self.bank(0), self.bank(1)]
        pS3 = [b.rearrange("p (h q) -> p h q", h=4) for b in pSb]
        acc = [self.bank(2 + i) for i in range(4)]

        def exp_chunk(kS, ps):
            e = self.nxt("eT", 3)
            self.act(eT[e], ps, AF.Exp, [kS], [("eT", e)])
            return e

        for qb in range(NQB):
            t0 = qb * 128
            tix = t0 // T
            sl = qb % 2
            self.load(qT[sl], qv[:, :, t0:t0 + 128], ("qT", sl), [("G", "ns_q", tix)], [("qT", sl)])
            self.load(gq[sl], Dm["sm_tok"][t0:t0 + 128, :], ("gq", sl), [("G", "sm_tok", tix)], [("gq", sl)])
            w0 = max(0, qb - 4)
            nw = qb - w0 + 1
            self.load(kwT[sl][:, :, 0:nw * 128],
                      Dm["ns_k"][256:512, w0 * 128:(qb + 1) * 128].rearrange("(g d) s -> d g s", g=2),
                      ("kwT", sl), [("G", "ns_k", i) for i in range(w0 * 128 // T, tix + 1)], [("kwT", sl)])
            for g in range(2):
                self.load(vwa[sl][:, 0:nw, g, 0:128],
                          Dm["ns_v"][2 + g, w0 * 128:(qb + 1) * 128, :].rearrange("(n p) d -> p n d", p=128),
                          ("vwa", sl), [("G", "ns_v", i) for i in range(w0 * 128 // T, tix + 1)], [("vwa", sl)])
            gates = gq[sl][:, 16:40].rearrange("p (b h) -> p b h", b=3)
            for g in range(2):
                q4 = qT[sl][:, 4 * g:4 * g + 4, :]
                aRc4 = aRc[:, 4 * g:4 * g + 4, :]
                aRs4 = aRs[:, 4 * g:4 * g + 4, :]
                ncc = (8 * qb + 8 + 127) // 128
                for k in range(ncc):
                    s = self.nxt("pS")
                    kS = ("pS", s)
                    m = 16 * k - qb
                    self.mm(pS3[s], kcT[:, g, k * 128:(k + 1) * 128], q4, True, False, ["kcT", ("qT", sl)], [kS])
                    self.mm(pS3[s], aLc, aRc4, False, False, ["aLc", "aRc"], [kS])
                    if k == ncc - 1:
                        self.mm(pS3[s], self.ident_b[:], cmask[:, qb % 16, :].unsqueeze(1).to_broadcast(sh4), False, False,
                                ["cmask"], [kS])
                    self.mm(pS3[s], ones1, tabm[:, m + NQB - 1, 4 * g:4 * g + 4].unsqueeze(2).to_broadcast([1, 4, 128]),
                            False, True, ["ones1", "tabm"], [kS])
                    e = exp_chunk(kS, pSb[s])
                    for h in range(4):
                        self.mm(acc[h][:, 0:257], eT[e][:, h * 128:(h + 1) * 128], vca[:, k, g, :], k == 0, k == ncc - 1,
                                [("eT", e), "vca"], [("acc", h)])
                for h in range(4):
                    self.cp(ev[:, h, :], acc[h][:, 0:257], [("acc", h)], ["ev"], eng="act")
                self.ts(rs[:, 0:4], ev[:, :, 128], 1e-30, None, ALU.max, None, ["ev"], ["rs"])
                self.op("dve", "reciprocal", rs[:, 0:4], rs[:, 0:4], r=["rs"], w=["rs"])
                self.tt(wv[:, 0:4], rs[:, 0:4], gates[:, 0, 4 * g:4 * g + 4], ALU.mult, ["rs", ("gq", sl)], ["wv"])
                self.tt(onsa[:, 4 * g:4 * g + 4, :], ev[:, :, 0:128], wv[:, 0:4].unsqueeze(2).to_broadcast(sh4), ALU.mult,
                        ["ev", "wv"], ["onsa"])
                self.tt(tmp4, ev[:, :, 129:257], rs[:, 0:4].unsqueeze(2).to_broadcast(sh4), ALU.mult, ["ev", "rs"], ["tmp4"])
                self.tt(imp, tmp4[:, 0, :], tmp4[:, 1, :], ALU.add, ["tmp4"], ["imp"])
                self.tt(imp2, tmp4[:, 2, :], tmp4[:, 3, :], ALU.add, ["tmp4"], ["imp2"])
                self.tt(imp, imp, imp2, ALU.add, ["imp", "imp2"], ["imp"])
                j0 = 2 * qb
                if j0 + 2 < NSEL:
                    self.memset(imp[:, j0 + 2:NSEL], -1e6, ["imp"], eng="dve")
                self.memset(imp[0:64, j0 + 1:j0 + 2], -1e6, ["imp"], eng="dve")
                if j0 >= 1:
                    self.memset(imp[0:64, j0 - 1:j0 + 1], 1e6, ["imp"], eng="dve")
                else:
                    self.memset(imp[0:64, 0:1], 1e6, ["imp"], eng="dve")
                self.memset(imp[64:128, j0:j0 + 2], 1e6, ["imp"], eng="dve")
                self.memset(imp[:, 0:1], 1e6, ["imp"], eng="dve")
                if NSEL < 128:
                    self.memset(imp[:, NSEL:128], -1e6, ["imp"], eng="dve")
                self.op("dve", "max", out=m8[:, 0:8], in_=imp, r=["imp"], w=["m8"])
                self.op("dve", "match_replace", out=imp2, in_to_replace=m8[:, 0:8], in_values=imp, imm_value=-2e6,
                        r=["imp", "m8"], w=["imp2"])
                self.op("dve", "max", out=m8[:, 8:16], in_=imp2, r=["imp2"], w=["m8"])
                self.ts(nsel, imp, m8[:, 15:16], -32768.0, ALU.is_lt, ALU.mult, ["imp", "m8"], ["nsel"])
                pT = acc[3]
                self.tr(pT[:, 0:128], nsel, self.ident_f[:], ["nsel"], [("acc", 3)])
                self.cp(nselT, pT[:, 0:128].unsqueeze(1).to_broadcast(sh4), [("acc", 3)], ["nselT"])
                for (br, k_lo) in ((1, 0), (2, w0)):
                    for kc in range(k_lo, qb + 1):
                        s = self.nxt("pS")
                        kS = ("pS", s)
                        m = kc - qb
                        if br == 1:
                            self.mm(pS3[s], ksT[:, g, kc * 128:(kc + 1) * 128], q4, True, False, ["ksT", ("qT", sl)], [kS])
                            self.mm(pS3[s], Ebig[:, kc * 128:(kc + 1) * 128], nselT, False, False, ["Ebig", "nselT"], [kS])
                        else:
                            self.mm(pS3[s], kwT[sl][:, g, (kc - w0) * 128:(kc - w0 + 1) * 128], q4, True, False,
                                    [("kwT", sl), ("qT", sl)], [kS])
                        self.mm(pS3[s], aLs, aRs4, False, False, ["aLs", "aRs"], [kS])
                        if kc == qb:
                            self.mm(pS3[s], self.ident_b[:], caus.unsqueeze(1).to_broadcast(sh4), False, False, ["caus"], [kS])
                        if br == 2 and kc == qb - 4:
                            self.mm(pS3[s], self.ident_b[:], wlow.unsqueeze(1).to_broadcast(sh4), False, False, ["wlow"], [kS])
                        self.mm(pS3[s], ones1,
                                tabm[:, m + NQB - 1, 4 * g:4 * g + 4].unsqueeze(2).to_broadcast([1, 4, 128]),
                                False, True, ["ones1", "tabm"], [kS])
                        e = exp_chunk(kS, pSb[s])
                        for h in range(4):
                            ai = (0 if br == 1 else 2) + h // 2
                            rhs = vsa[:, kc, g, :] if br == 1 else vwa[sl][:, kc - w0, g, :]
                            self.mm(acc[ai][:, (h % 2) * 129:(h % 2) * 129 + 129], eT[e][:, h * 128:(h + 1) * 128], rhs,
                                    kc == k_lo and h % 2 == 0, kc == qb, [("eT", e), "vsa" if br == 1 else ("vwa", sl)],
                                    [("acc", ai)], skip=True)
                    a0 = 0 if br == 1 else 2
                    for i in range(2):
                        self.cp(evs[:, 2 * i:2 * i + 2, :], acc[a0 + i][:, 0:258].rearrange("p (h c) -> p h c", h=2),
                                [("acc", a0 + i)], ["evs"], eng="act")
                    self.op("dve", "reciprocal", rs[:, 4:8], evs[:, :, 128], r=["evs"], w=["rs"])
                    self.tt(wv[:, 4:8], rs[:, 4:8], gates[:, br, 4 * g:4 * g + 4], ALU.mult, ["rs", ("gq", sl)], ["wv"])
                    self.tt(tmp4, evs[:, :, 0:128], wv[:, 4:8].unsqueeze(2).to_broadcast(sh4), ALU.mult, ["evs", "wv"], ["tmp4"])
                    self.tt(onsa[:, 4 * g:4 * g + 4, :], onsa[:, 4 * g:4 * g + 4, :], tmp4, ALU.add, ["onsa", "tmp4"], ["onsa"])
            pY = self.bank(2, 2).rearrange("p (h c) -> p h c", h=8)
            for h in range(8):
                self.tr(pY[:, h, :], onsa[:, h, :], self.ident_f[:], ["onsa"], [("acc", 0), ("acc", 1)])
            self.cp(ybT[sl], pY, [("acc", 0), ("acc", 1)], [("ybT", sl)], eng="act")
            self.store(yv[:, 8:16, t0:t0 + 128], ybT[sl], ("ybT", sl), [("ybT", sl)], [("G", "yT", tix)])

    def slr2(self, slr, c):
        if not hasattr(self, "_slr2") or self._slr2_stage != self.stage:
            t = c([2, 8, 128], F32)
            for h in range(8):
                self.memset(t[:, h, :], SLOPES[h], ["slr2"])
            self._slr2 = t
            self._slr2_stage = self.stage
        return self._slr2

    def convert_weights(self):
        self.Wb = {}
        order = ["ffn1_gate_t", "ffn1_up_t", "ffn1_down_t", "w_in_t", "w_br_t", "w_out_t",
                 "ffn2_gate_t", "ffn2_up_t", "ffn2_down_t"]
        views = {}
        for nm in order:
            src = self.I[nm]
            shp = list(src.shape)
            dst = self.nc.dram_tensor(nm + "_bf", shp, BF16, kind="Internal").ap()
            self.Wb[nm] = dst
            if nm == "w_br_t":
                views[nm] = (src.rearrange("l b c p k n -> l (b c) p (k n)"), dst.rearrange("l b c p k n -> l (b c) p (k n)"))
            else:
                views[nm] = (src.rearrange("l c p k n -> l c p (k n)"), dst.rearrange("l c p k n -> l c p (k n)"))
        for l in range(self.NL):
            for nm in order:
                sv, dv = views[nm]
                nch = sv.shape[1]
                per = sv.shape[3] * 128 * 4
                grp = max(1, (4 << 20) // per)
                for c0 in range(0, nch, grp):
                    c1 = min(nch, c0 + grp)
                    self.load(dv[l, c0:c1], sv[l, c0:c1], ("cvt", nm), (), [("G", "wbf", nm, l)], queue="pool")

    def build(self):
        S, NL = self.S, self.NL
        I = {}
        I["xT_in"] = self.dram_in("xT_in", [D, S])
        I["gam_in"] = self.dram_in("gam_in", [128, 3 * NL, KC])
        I["n128"] = self.dram_in("n128", [NL, 128, 72])
        for f in ("ffn1", "ffn2"):
            I[f + "_gate_t"] = self.dram_in(f + "_gate_t", [NL, FC, 128, KC, 128])
            I[f + "_up_t"] = self.dram_in(f + "_up_t", [NL, FC, 128, KC, 128])
            I[f + "_down_t"] = self.dram_in(f + "_down_t", [NL, KC, 128, FC, 128])
        I["w_in_t"] = self.dram_in("w_in_t", [NL, NWIN, 128, KC, 128])
        I["w_sm_t"] = self.dram_in("w_sm_t", [NL, 128, KC, 64])
        I["w_br_t"] = self.dram_in("w_br_t", [NL, 3, KC, 128, 8, 128])
        I["w_out_t"] = self.dram_in("w_out_t", [NL, KC, 128, KC, 128])
        I["dn_conv_t"] = self.dram_in("dn_conv_t", [NL, 128, 24, 4])
        I["dn_vec"] = self.dram_in("dn_vec", [NL, 16])
        I["cmp_w1_t"] = self.dram_in("cmp_w1_t", [NL, 2, 128, 32, 256])
        I["cmp_w2_t"] = self.dram_in("cmp_w2_t", [NL, 2, 128, 2, 128])
        I["sgu_nw"] = self.dram_in("sgu_nw", [NL, 128, 16])
        I["sgu_wT"] = self.dram_in("sgu_wT", [NL, 128, 8, 128])
        I["sgu_b"] = self.dram_in("sgu_b", [NL, 8, 128])
        self.I = I
        out = self.dram_out("yT_out", [D, S])
        Dm = {}
        Dm["xTd"] = self.dram_tmp("xTd", [D, S])
        Dm["dn_qkv"] = self.dram_tmp("dn_qkv", [3072, S])
        Dm["dn_zg"] = self.dram_tmp("dn_zg", [1024, S])
        Dm["sm_tok"] = self.dram_tmp("sm_tok", [S, 64])
        Dm["ns_q"] = self.dram_tmp("ns_q", [1024, S], BF16)
        Dm["ns_kvc"] = self.dram_tmp("ns_kvc", [512, S])
        Dm["ns_k"] = self.dram_tmp("ns_k", [512, S], BF16)
        Dm["ns_v"] = self.dram_tmp("ns_v", [4, S, 128], BF16)
        Dm["gates"] = self.dram_tmp("gates", [6144, S])
        Dm["yT"] = self.dram_tmp("yT", [3072, S], BF16)
        self.D = Dm
        self.setup_consts()
        self.convert_weights()
        st = self.stages
        mix = lambda l: [f for f in ("dn", "nsa") if st is None or f in st]
        last = NL - 1
        if st is not None and "rl0only" in st:
            self.rl_stage("rl0", None, None, 0, 0, I["xT_in"], out, "out")
            return self.sch.finalize()
        self.rl_stage("rl0", None, None, 0, 0, I["xT_in"], Dm["xTd"], "xTd")
        for l in range(NL):
            if "dn" in mix(l):
                self.dn_stage(l)
            if "nsa" in mix(l):
                self.nsa_stage(l)
            if st is not None and "nomerge" in st:
                continue
            if l < last:
                self.rl_stage("rl%d" % (l + 1), l, l, l + 1, l + 1, Dm["xTd"], Dm["xTd"], "xTd")
            else:
                self.rl_stage("rl%d" % (l + 1), l, l, None, None, Dm["xTd"], out, "out")
        return self.sch.finalize()


IN_SIZES = (1024, 1024, 1024, 1024, 8, 8, 1024, 256, 256, 256, 256, 256, 256, 24, 1024, 1024, 2048, 2048, 2048)


def tile_w(w):
    K, N = w.shape
    return np.ascontiguousarray(w.reshape(K // 128, 128, N // 128, 128).transpose(2, 1, 0, 3))


def prep_weights(inp, NL):
    f32 = np.float32
    o = {}
    gam = np.zeros((128, 3 * NL, KC), f32)
    for l in range(NL):
        for i, nm in enumerate(("ffn1_norm", "mix_norm", "ffn2_norm")):
            gam[:, 3 * l + i, :] = np.asarray(inp[nm][l], f32).reshape(KC, 128).T
    o["gam_in"] = gam
    n128 = np.zeros((NL, 128, 72), f32)
    for l in range(NL):
        n128[l, :, 0] = inp["dn_out_norm"][l]
        n128[l, :, 1] = inp["nsa_q_norm"][l]
        n128[l, :, 2:5] = np.asarray(inp["nsa_k_norm"][l]).T
        n128[l, :, 8:40] = np.asarray(inp["cmpk_pos"][l]).T
        n128[l, :, 40:72] = np.asarray(inp["cmpv_pos"][l]).T
    o["n128"] = n128
    for f in ("ffn1", "ffn2"):
        o[f + "_gate_t"] = np.stack([tile_w(np.asarray(inp[f + "_gate"][l], f32)) for l in range(NL)])
        o[f + "_up_t"] = np.stack([tile_w(np.asarray(inp[f + "_up"][l], f32)) for l in range(NL)])
        o[f + "_down_t"] = np.stack([tile_w(np.asarray(inp[f + "_down"][l], f32)) for l in range(NL)])
    offs = np.concatenate([[0], np.cumsum(IN_SIZES)])
    big = np.concatenate([np.arange(0, 4096), np.arange(4112, 6672), np.arange(6696, 14888)])
    small = np.concatenate([np.arange(4096, 4112), np.arange(6672, 6696)])
    assert big.size == NWIN * 128 and offs[-1] == 14888
    w_in_t, w_sm_t = [], []
    for l in range(NL):
        w = np.asarray(inp["w_in"][l], f32)
        w_in_t.append(tile_w(w[:, big]))
        ws = np.zeros((D, 64), f32)
        ws[:, 0:40] = w[:, small]
        w_sm_t.append(np.ascontiguousarray(ws.reshape(KC, 128, 64).transpose(1, 0, 2)))
    o["w_in_t"] = np.stack(w_in_t)
    o["w_sm_t"] = np.stack(w_sm_t)
    o["w_br_t"] = np.stack([np.stack([tile_w(np.asarray(inp[nm][l], f32))
                                      for nm in ("w_branch_a", "w_branch_b", "w_branch_c")]) for l in range(NL)])
    o["w_out_t"] = np.stack([tile_w(np.asarray(inp["w_out"][l], f32)) for l in range(NL)])
    o["dn_conv_t"] = np.ascontiguousarray(np.asarray(inp["dn_conv"], f32).reshape(NL, 4, 24, 128).transpose(0, 3, 2, 1))
    o["dn_vec"] = np.concatenate([np.asarray(inp["dn_a_log"], f32), np.asarray(inp["dn_dt_bias"], f32)], axis=1)
    w1 = np.stack([np.asarray(inp["cmpk_w1"], f32), np.asarray(inp["cmpv_w1"], f32)], axis=1)
    o["cmp_w1_t"] = np.ascontiguousarray(w1.reshape(NL, 2, 32, 128, 256).transpose(0, 1, 3, 2, 4))
    w2 = np.stack([np.asarray(inp["cmpk_w2"], f32), np.asarray(inp["cmpv_w2"], f32)], axis=1)
    o["cmp_w2_t"] = np.ascontiguousarray(w2.reshape(NL, 2, 2, 128, 128).transpose(0, 1, 3, 2, 4))
    nw = np.zeros((NL, 128, 16), f32)
    for l in range(NL):
        nw[l, :, 0:8] = np.asarray(inp["sgu_norm_w"][l], f32).reshape(8, 128).T
        nw[l, :, 8:16] = np.asarray(inp["sgu_norm_b"][l], f32).reshape(8, 128).T
    o["sgu_nw"] = nw
    o["sgu_wT"] = np.ascontiguousarray(np.asarray(inp["sgu_w"], f32).transpose(0, 3, 1, 2))
    o["sgu_b"] = np.ascontiguousarray(np.asarray(inp["sgu_b"], f32))
    return o


_CACHE = {}


def kernel(**inputs):
    x = np.asarray(inputs["x"], np.float32)
    Bsz, S, _ = x.shape
    NL = int(np.asarray(inputs["ffn1_norm"]).shape[0])
    key = (S, NL)
    if key not in _CACHE:
        mk = MK(S, NL)
        mk.build()
        _CACHE[key] = mk
    mk = _CACHE[key]
    w = prep_weights(inputs, NL)
    n_cores = 8
    in_maps = []
    for c in range(n_cores):
        b = c % Bsz
        m = dict(w)
        m["xT_in"] = np.ascontiguousarray(x[b].T)
        in_maps.append(m)
    res = run_bass_kernel_spmd(mk.nc, in_maps, core_ids=list(range(n_cores)))
    out = np.empty_like(x)
    for b in range(Bsz):
        out[b] = res.results[b]["yT_out"].T
    return out
```

```python
import numpy as np
from contextlib import ExitStack
import concourse.bass as bass
import concourse.mybir as mybir
from concourse.bass_utils import run_bass_kernel_spmd

F32 = mybir.dt.float32
BF16 = mybir.dt.bfloat16
ALU = mybir.AluOpType
AF = mybir.ActivationFunctionType
AX = mybir.AxisListType

D = 2048
KC = D // 128
FF = 5504
FC = FF // 128
EPS = 1e-6


class Buf:
    __slots__ = ("name", "w", "r")

    def __init__(self, name=""):
        self.name = name
        self.w = None
        self.r = {}


class Ins:
    __slots__ = ("eng", "fn", "deps", "tick", "dma", "sem", "val", "needs_inc")

    def __init__(self, eng, fn):
        self.eng = eng
        self.fn = fn
        self.deps = ()
        self.tick = 0
        self.dma = False
        self.sem = None
        self.val = 0
        self.needs_inc = False


class Sched:
    ENG = ("pe", "act", "dve", "pool", "sp")

    def __init__(self, nc, stack):
        self.nc = nc
        self.stack = stack
        self.q = {e: [] for e in self.ENG}
        self.engsem = {e: stack.enter_context(nc.semaphore("es_" + e)) for e in self.ENG}
        self.dmasem = {}
        self.last_dma = {}
        self.pending = {}
        self.freesem = []
        self.nsem = 0

    def _track(self, ins, reads, writes):
        deps = {}
        for b in reads:
            d = b.w
            if d is not None:
                deps[id(d)] = d
        for b in writes:
            d = b.w
            if d is not None:
                deps[id(d)] = d
            for d in b.r.values():
                deps[id(d)] = d
        extra = self.pending.pop(ins.eng, None)
        if extra:
            for d in extra:
                deps[id(d)] = d
        deps.pop(id(ins), None)
        dl = list(deps.values())
        if ins.eng == "pe":
            dl = [d for d in dl if d.dma or d.eng != "pe"]
        ins.deps = dl
        key = ("d", id(ins.sem)) if ins.dma else ins.eng
        for b in reads:
            b.r[key] = ins
        for b in writes:
            b.w = ins
            b.r = {}
        self.q[ins.eng].append(ins)

    def op(self, eng, fn, reads=(), writes=()):
        ins = Ins(eng, fn)
        self._track(ins, reads, writes)
        return ins

    def dma(self, queue, fn, semkey, reads=(), writes=()):
        ins = Ins(queue, fn)
        ins.dma = True
        ent = self.dmasem.get(semkey)
        if ent is None:
            if self.freesem:
                ent = self.freesem.pop()
            else:
                ent = [self.stack.enter_context(self.nc.semaphore("ds%d" % self.nsem)), 0]
                self.nsem += 1
            self.dmasem[semkey] = ent
        ent[1] += 16
        ins.sem = ent[0]
        ins.val = ent[1]
        self._track(ins, reads, writes)
        self.last_dma[id(ins.sem)] = ins
        return ins

    def barrier(self):
        deps = list(self.last_dma.values())
        for e in self.ENG:
            if self.q[e]:
                for ins in reversed(self.q[e]):
                    if not ins.dma and ins.fn is not None:
                        deps.append(ins)
                        break
        for e in self.ENG:
            self.pending[e] = list(deps) + list(self.pending.get(e, []))
        self.freesem.extend(self.dmasem.values())
        self.dmasem = {}

    def finalize(self, final_eng="pool"):
        fin = Ins(final_eng, None)
        fin.deps = list(self.last_dma.values())
        self.q[final_eng].append(fin)
        for e in self.ENG:
            for ins in self.q[e]:
                for d in ins.deps:
                    if not d.dma:
                        d.needs_inc = True
        for e in self.ENG:
            t = 0
            for ins in self.q[e]:
                if ins.needs_inc and not ins.dma:
                    t += 1
                    ins.tick = t
        nc = self.nc
        names = {"pe": "tensor", "act": "scalar", "dve": "vector", "pool": "gpsimd", "sp": "sync"}
        stats = {}
        engsem = self.engsem
        with nc.Block() as block:
            for e in self.ENG:
                def body(eng, q=self.q[e], e=e):
                    obs = {}
                    nw = 0
                    for ins in q:
                        for d in ins.deps:
                            if d.dma:
                                sem, val = d.sem, d.val
                            else:
                                sem, val = engsem[d.eng], d.tick
                            k = id(sem)
                            if obs.get(k, 0) < val:
                                eng.wait_ge(sem, val)
                                obs[k] = val
                                nw += 1
                        if ins.fn is None:
                            continue
                        r = ins.fn(eng)
                        if ins.dma:
                            r.then_inc(ins.sem, 16)
                        elif ins.needs_inc:
                            r.then_inc(engsem[e], 1)
                    stats[e] = (len(q), nw)
                getattr(block, names[e])(body)
        return stats


NWIN = 116
AW = 52000
SLOPES = [2.0 ** (-(h + 1)) for h in range(8)]


class MK:
    def __init__(self, S, NL, T=512, taps=(), stages=None):
        self.S = S
        self.NL = NL
        self.T = T
        self.taps = set(taps)
        self.stages = stages
        self.nc = bass.Bass("TRN2", target_bir_lowering=False)
        self.stack = ExitStack()
        self.sch = Sched(self.nc, self.stack)
        self.bufs = {}
        self.cnt = {}
        self.stage = "init"
        self.arena = self.sb("arena", [128, AW], F32)
        self.aoff = 0
        self.psall = self.ps("psall", [128, 4096], F32)

    def dram_in(self, name, shape, dt=F32):
        return self.nc.dram_tensor(name, list(shape), dt, kind="ExternalInput").ap()

    def dram_out(self, name, shape, dt=F32):
        return self.nc.dram_tensor(name, list(shape), dt, kind="ExternalOutput").ap()

    def dram_tmp(self, name, shape, dt=F32):
        kind = "ExternalOutput" if name in self.taps else "Internal"
        return self.nc.dram_tensor(name, list(shape), dt, kind=kind).ap()

    def sb(self, name, shape, dt=F32):
        return self.stack.enter_context(self.nc.sbuf_tensor("sb_" + name, list(shape), dt))

    def ps(self, name, shape, dt=F32):
        return self.stack.enter_context(self.nc.psum_tensor("ps_" + name, list(shape), dt))

    def carve(self, shape, dt=F32):
        n = 1
        for d in shape[1:]:
            n *= d
        words = (n + 1) // 2 if dt == BF16 else n
        words = (words + 7) // 8 * 8
        assert self.aoff + words <= AW, ("arena overflow", self.stage, self.aoff, words)
        v = self.arena[0:shape[0], self.aoff:self.aoff + words]
        self.aoff += words
        if dt == BF16:
            v = v.bitcast(BF16)
        v = v[:, 0:n]
        if len(shape) == 3:
            v = v.rearrange("p (a b) -> p a b", a=shape[1])
        elif len(shape) == 4:
            v = v.rearrange("p (a b c) -> p a b c", a=shape[1], b=shape[2])
        return v

    def bank(self, i, n=1):
        return self.psall[:, i * 512:(i + n) * 512]

    def stage_begin(self, name):
        self.sch.barrier()
        self.stage = name
        self.aoff = 0
        self.cnt = {}

    def B(self, key):
        if not (isinstance(key, tuple) and key[0] == "G"):
            key = (self.stage, key)
        b = self.bufs.get(key)
        if b is None:
            b = Buf(str(key))
            self.bufs[key] = b
        return b

    def Bs(self, keys):
        return [k if isinstance(k, Buf) else self.B(k) for k in keys]

    def nxt(self, key, n=2):
        v = self.cnt.get(key, 0)
        self.cnt[key] = v + 1
        return v % n

    def op(self, eng, method, *args, r=(), w=(), **kw):
        self.sch.op(eng, lambda e: getattr(e, method)(*args, **kw), self.Bs(r), self.Bs(w))

    def mm(self, out, lhsT, rhs, start, stop, r, w, skip=False):
        if skip:
            self.sch.op("pe", lambda e: e.matmul(out, lhsT, rhs, start=start, stop=stop, skip_group_check=True),
                        self.Bs(r), self.Bs(w))
        else:
            self.sch.op("pe", lambda e: e.matmul(out, lhsT, rhs, start=start, stop=stop), self.Bs(r), self.Bs(w))

    def tr(self, out, in_, ident, r, w):
        self.sch.op("pe", lambda e: e.transpose(out, in_, ident), self.Bs(r), self.Bs(w))

    def act(self, out, in_, func, r, w, bias=None, scale=1.0):
        if bias is None:
            self.sch.op("act", lambda e: e.activation(out, in_, func, scale=scale), self.Bs(r), self.Bs(w))
        else:
            self.sch.op("act", lambda e: e.activation(out, in_, func, bias=bias, scale=scale),
                        self.Bs(r), self.Bs(w))

    def tt(self, out, in0, in1, op, r, w, eng="dve"):
        self.sch.op(eng, lambda e: e.tensor_tensor(out, in0, in1, op), self.Bs(r), self.Bs(w))

    def ts(self, out, in0, s1, s2, op0, op1, r, w, eng="dve"):
        if op1 is None:
            self.sch.op(eng, lambda e: e.tensor_scalar(out, in0, s1, None, op0), self.Bs(r), self.Bs(w))
        else:
            self.sch.op(eng, lambda e: e.tensor_scalar(out, in0, s1, s2, op0, op1), self.Bs(r), self.Bs(w))

    def stt(self, out, in0, scalar, in1, op0, op1, r, w, eng="dve"):
        self.sch.op(eng, lambda e: e.scalar_tensor_tensor(out=out, in0=in0, scalar=scalar, in1=in1,
                                                          op0=op0, op1=op1), self.Bs(r), self.Bs(w))

    def cp(self, out, in_, r, w, eng="dve"):
        if eng == "act":
            self.sch.op("act", lambda e: e.copy(out, in_), self.Bs(r), self.Bs(w))
        else:
            self.sch.op(eng, lambda e: e.tensor_copy(out, in_), self.Bs(r), self.Bs(w))

    def memset(self, out, val, w, eng="pool"):
        self.sch.op(eng, lambda e: e.memset(out, val), (), self.Bs(w))

    def asel(self, out, in_, pattern, cmp, fill, base, cm, r, w):
        self.sch.op("pool", lambda e: e.affine_select(out=out, in_=in_, pattern=pattern, compare_op=cmp,
                                                      fill=fill, base=base, channel_multiplier=cm),
                    self.Bs(r), self.Bs(w))

    def load(self, out, in_, semkey, r, w, queue="pool"):
        self.sch.dma(queue, lambda e: e.dma_start(out=out, in_=in_), (self.stage, semkey), self.Bs(r), self.Bs(w))

    def store(self, out, in_, semkey, r, w, queue="pool"):
        self.sch.dma(queue, lambda e: e.dma_start(out=out, in_=in_), (self.stage, semkey), self.Bs(r), self.Bs(w))

    def setup_consts(self):
        G = lambda n: ("G", n)
        self.ones_bf = self.sb("ones_bf", [128, 128], BF16)
        self.memset(self.ones_bf[:], 1.0, [G("ones_bf")])
        self.ones_f = self.sb("ones_f", [128, 128], F32)
        self.memset(self.ones_f[:], 1.0, [G("ones_f")])
        self.eps_t = self.sb("eps_t", [128, 1], F32)
        self.memset(self.eps_t[:], EPS, [G("eps")])
        self.ident_f = self.sb("ident_f", [128, 128], F32)
        self.memset(self.ident_f[:], 1.0, [G("ident_f")])
        self.asel(self.ident_f[:], self.ident_f[:], [[-1, 128]], ALU.is_equal, 0.0, 0, 1, [G("ident_f")], [G("ident_f")])
        self.ident_b = self.sb("ident_b", [128, 128], BF16)
        self.cp(self.ident_b[:], self.ident_f[:], [G("ident_f")], [G("ident_b")], eng="pool")
        NG = 3 * self.NL
        self.gam = self.sb("gam", [128, NG, KC], F32)
        self.load(self.gam[:], self.I["gam_in"], "gam", (), [G("gam")])
        self.n128 = self.sb("n128", [128, self.NL, 72], F32)
        self.load(self.n128[:], self.I["n128"].rearrange("l p c -> p l c"), "n128", (), [G("n128")])
        self.qnw = self.sb("qnw", [128, self.NL], F32)
        for l in range(self.NL):
            self.ts(self.qnw[:, l:l + 1], self.n128[:, l, 1:2], 128.0 ** -0.5, None, ALU.mult, None,
                    [G("n128")], [G("qnw")])

    def alloc_rl(self):
        T = self.T
        c = self.carve
        self.xT = c([128, KC, T], F32)
        self.hT = c([128, KC, T], BF16)
        self.actflat = self.arena[:, self.aoff:self.aoff + FC * T // 2]
        self.actT = c([128, FC, T], BF16)
        self.uvT = self.actflat[:, 0:16 * T].rearrange("p (a b) -> p a b", a=16)
        self.wA = [c([128, KC, 128], BF16) for _ in range(3)]
        self.wB = [c([128, KC, 128], BF16) for _ in range(3)]
        self.wD = [c([128, FC, 128], BF16) for _ in range(2)]
        self.sq = [c([128, T], BF16) for _ in range(2)]
        self.sg = [c([128, T], F32) for _ in range(2)]
        self.rstd = c([128, T], F32)
        self.gt = [c([128, 3, T], F32) for _ in range(2)]
        self.ost = [c([128, T], F32) for _ in range(4)]
        self.cv = [c([128, T + 3], F32) for _ in range(2)]
        self.halo = c([128, 24, 3], F32)
        self.cvt = [c([128, T], F32) for _ in range(2)]
        self.convw = c([128, 24, 4], F32)
        self.wsm = c([128, KC, 64], BF16)
        self.smo = [c([128, 4, 64], F32) for _ in range(2)]
        self.smt = c([128, 32], F32)
        self.dnv = c([128, 16], F32)
        self.nA = c([128, 8], F32)
        self.sgnw = c([128, 16], F32)
        self.wsT = c([128, 8, 128], BF16)
        self.bsb = c([128, 8, 128], F32)
        self.lnst = c([128, 4, T], F32)
        self.wsTf = self.lnst[:, 0:2, :].rearrange("p a (b c) -> p (a b) c", c=128)
        self.vtok = [c([128, 4, 128], BF16) for _ in range(2)]
        self.ycT = c([128, 8, T], BF16)
        self.psM = [self.bank(i) for i in range(4)]
        self.psD = [self.bank(4), self.bank(5)]
        self.psS = self.bank(6)
        self.psX = self.bank(7)

    def xb(self):
        return [("xT", k) for k in range(KC)]

    def rmsnorm(self, gi):
        for kc in range(KC):
            s = self.nxt("sq")
            self.act(self.sq[s], self.xT[:, kc, :], AF.Square, [("xT", kc)], [("sq", s)])
            self.mm(self.psS, self.ones_bf[:], self.sq[s], kc == 0, kc == KC - 1, [("sq", s)], ["psS"])
        self.act(self.rstd, self.psS, AF.Sqrt, ["psS"], ["rstd"], bias=self.eps_t[:, 0:1], scale=1.0 / D)
        self.op("dve", "reciprocal", self.rstd, self.rstd, r=["rstd"], w=["rstd"])
        for kc in range(KC):
            self.stt(self.hT[:, kc, :], self.xT[:, kc, :], self.gam[:, gi, kc:kc + 1], self.rstd,
                     ALU.mult, ALU.mult, [("xT", kc), "rstd"], ["hT"])

    def ffn(self, gi, f, l):
        wg, wu, wd = self.Wb[f + "_gate_t"][l], self.Wb[f + "_up_t"][l], self.Wb[f + "_down_t"][l]
        dg, du, dd = [("G", "wbf", f + "_gate_t", l)], [("G", "wbf", f + "_up_t", l)], [("G", "wbf", f + "_down_t", l)]
        self.rmsnorm(gi)
        for fc in range(FC):
            s = self.nxt("wAB", 3)
            self.load(self.wA[s], wg[fc], ("wA", s), dg, [("wA", s)], queue="sp")
            self.load(self.wB[s], wu[fc], ("wB", s), du, [("wB", s)], queue="sp")
            p = self.nxt("psM", 2)
            pa, pb = self.psM[2 * p], self.psM[2 * p + 1]
            ka, kb = ("psM", 2 * p), ("psM", 2 * p + 1)
            for kc in range(KC):
                self.mm(pa, self.wA[s][:, kc, :], self.hT[:, kc, :], kc == 0, kc == KC - 1, [("wA", s), "hT"], [ka])
            for kc in range(KC):
                self.mm(pb, self.wB[s][:, kc, :], self.hT[:, kc, :], kc == 0, kc == KC - 1, [("wB", s), "hT"], [kb])
            g = self.nxt("sg")
            self.act(self.sg[g], pa, AF.Silu, [ka], [("sg", g)])
            self.tt(self.actT[:, fc, :], self.sg[g], pb, ALU.mult, [("sg", g), kb], [("actT", fc)])
        for dc in range(KC):
            s = self.nxt("wD")
            self.load(self.wD[s], wd[dc], ("wD", s), dd, [("wD", s)], queue="sp")
            p = self.nxt("psD")
            for fc in range(FC):
                self.mm(self.psD[p], self.wD[s][:, fc, :], self.actT[:, fc, :], fc == 0, fc == FC - 1,
                        [("wD", s), ("actT", fc)], [("psD", p)])
            self.stt(self.xT[:, dc, :], self.psD[p], 0.5, self.xT[:, dc, :], ALU.mult, ALU.add,
                     [("psD", p), ("xT", dc)], [("xT", dc)])

    def merge(self, l, t0):
        T = self.T
        yT = self.actT
        yv = self.D["yT"].rearrange("(k p) s -> p k s", p=128)
        self.load(yT[:, 0:24, :], yv[:, :, t0:t0 + T], "yT", [("G", "yT", t0 // T)], [("actT", k) for k in range(24)])
        gv = self.D["gates"].rearrange("(b k p) s -> p b k s", b=3, p=128)
        wbr = self.Wb["w_br_t"]
        for dc in range(KC):
            gs = self.nxt("gt")
            self.load(self.gt[gs], gv[:, :, dc, t0:t0 + T], ("gt", gs), [("G", "gates", t0 // T)], [("gt", gs)])
            pks = []
            for br in range(3):
                if br < 2:
                    s = self.nxt("wAB")
                    wt = (self.wA, self.wB)[br][s]
                    wk = (("wA", s), ("wB", s))[br]
                else:
                    s = self.nxt("wAB")
                    wt = self.wA[s]
                    wk = ("wA", s)
                self.load(wt[:, 0:8, :], wbr[l, br, dc], wk, [("G", "wbf", "w_br_t", l)], [wk], queue="sp")
                p = self.nxt("psM4", 4)
                for kc in range(8):
                    self.mm(self.psM[p], wt[:, kc, :], yT[:, br * 8 + kc, :], kc == 0, kc == 7,
                            [wk, ("actT", br * 8 + kc)], [("psM", p)])
                pks.append(p)
            a = self.nxt("ost", 4)
            b = self.nxt("ost", 4)
            self.tt(self.ost[a], self.psM[pks[0]], self.gt[gs][:, 0, :], ALU.mult, [("psM", pks[0]), ("gt", gs)], [("ost", a)])
            self.tt(self.ost[b], self.psM[pks[1]], self.gt[gs][:, 1, :], ALU.mult, [("psM", pks[1]), ("gt", gs)], [("ost", b)])
            self.tt(self.ost[a], self.ost[a], self.ost[b], ALU.add, [("ost", a), ("ost", b)], [("ost", a)])
            self.tt(self.ost[b], self.psM[pks[2]], self.gt[gs][:, 2, :], ALU.mult, [("psM", pks[2]), ("gt", gs)], [("ost", b)])
            self.tt(self.hT[:, dc, :], self.ost[a], self.ost[b], ALU.add, [("ost", a), ("ost", b)], ["hT"])
        wo = self.Wb["w_out_t"]
        for dc in range(KC):
            s = self.nxt("wAB")
            self.load(self.wA[s], wo[l, dc], ("wA", s), [("G", "wbf", "w_out_t", l)], [("wA", s)], queue="sp")
            p = self.nxt("psD")
            for kc in range(KC):
                self.mm(self.psD[p], self.wA[s][:, kc, :], self.hT[:, kc, :], kc == 0, kc == KC - 1,
                        [("wA", s), "hT"], [("psD", p)])
            self.tt(self.xT[:, dc, :], self.psD[p], self.xT[:, dc, :], ALU.add, [("psD", p), ("xT", dc)], [("xT", dc)])

    def win_setup(self, l):
        self.load(self.convw, self.I["dn_conv_t"][l], "convw", (), ["convw"])
        self.load(self.wsm, self.I["w_sm_t"][l], "wsm", (), ["wsm"], queue="pool")
        self.load(self.dnv, self.I["dn_vec"][l].partition_broadcast(128), "dnv", (), ["dnv"])
        self.act(self.nA, self.dnv[:, 0:8], AF.Exp, ["dnv"], ["nA"])
        self.ts(self.nA, self.nA, -1.0, None, ALU.mult, None, ["nA"], ["nA"])
        self.load(self.sgnw, self.I["sgu_nw"][l], "sgnw", (), ["sgnw"])
        self.load(self.wsTf, self.I["sgu_wT"][l], "wsTf", (), ["wsTf", "ln_mu", "ln_var"])
        for g in range(8):
            self.asel(self.wsTf[:, g, :], self.wsTf[:, g, :], [[1, 128]], ALU.is_ge, 0.0, 0, -1, ["wsTf"], ["wsTf", "ln_mu", "ln_var"])
        self.cp(self.wsT, self.wsTf, ["wsTf", "ln_mu", "ln_var"], ["wsT"], eng="pool")
        self.load(self.bsb, self.I["sgu_b"][l].rearrange("g i -> (g i)").partition_broadcast(128)
                  .rearrange("p (g i) -> p g i", g=8), "bsb", (), ["bsb"])
        self.memset(self.halo, 0.0, ["halo"])

    def ostage_store(self, dst, src_key_slot, tix, dkey):
        a = src_key_slot
        self.store(dst, self.ost[a], ("ost", a), [("ost", a)], [("G", dkey, tix)])

    def win(self, l, t0):
        T = self.T
        tix = t0 // T
        Dm = self.D
        self.rmsnorm(3 * l + 1)
        so = self.nxt("smo")
        for sub in range(4):
            pX = self.psX[:, sub * 64:(sub + 1) * 64]
            for kc in range(KC):
                self.mm(pX, self.hT[:, kc, sub * 128:(sub + 1) * 128], self.wsm[:, kc, :], kc == 0, kc == KC - 1,
                        ["hT", "wsm"], ["psX"])
        pv = self.psX[:, 0:256].rearrange("p (s c) -> p s c", s=4)
        smt = self.smt.rearrange("p (s c) -> p s c", s=4)
        self.tt(smt, pv[:, :, 0:8], self.dnv[:, 8:16].unsqueeze(1).to_broadcast([128, 4, 8]), ALU.add,
                ["psX", "dnv"], ["smt"])
        self.act(smt, smt, AF.Exp, ["smt"], ["smt"])
        self.act(smt, smt, AF.Ln, ["smt"], ["smt"], bias=1.0)
        self.tt(self.smo[so][:, :, 0:8], smt, self.nA.unsqueeze(1).to_broadcast([128, 4, 8]), ALU.mult,
                ["smt", "nA"], [("smo", so)])
        self.act(self.smo[so][:, :, 8:40], pv[:, :, 8:40], AF.Sigmoid, ["psX"], [("smo", so)])
        self.store(Dm["sm_tok"][t0:t0 + T, :].rearrange("(s p) c -> p s c", p=128), self.smo[so],
                   ("smo", so), [("smo", so)], [("G", "sm_tok", tix)])
        win_t = self.Wb["w_in_t"]
        uT = self.uvT[:, 0:8, :]
        vfT = self.uvT[:, 8:16, :]
        uvk = lambda j: [("actT", 2 * j), ("actT", 2 * j + 1)]
        for c in range(NWIN):
            s = self.nxt("wAB", 3)
            self.load(self.wA[s], win_t[l, c], ("wA", s), [("G", "wbf", "w_in_t", l)], [("wA", s)], queue="sp")
            p = self.nxt("psM4", 4)
            ps, pk = self.psM[p], ("psM", p)
            for kc in range(KC):
                self.mm(ps, self.wA[s][:, kc, :], self.hT[:, kc, :], kc == 0, kc == KC - 1, [("wA", s), "hT"], [pk])
            if c < 24:
                v = self.nxt("cv")
                cv = self.cv[v]
                self.cp(cv[:, 3:T + 3], ps, [pk], [("cv", v)], eng="act")
                self.cp(cv[:, 0:3], self.halo[:, c, :], ["halo"], [("cv", v)], eng="pool")
                t = self.nxt("cvt")
                ct = self.cvt[t]
                self.ts(ct, cv[:, 0:T], self.convw[:, c, 0:1], None, ALU.mult, None, [("cv", v), "convw"], [("cvt", t)])
                for j in range(1, 4):
                    self.stt(ct, cv[:, j:T + j], self.convw[:, c, j:j + 1], ct, ALU.mult, ALU.add,
                             [("cv", v), "convw", ("cvt", t)], [("cvt", t)])
                self.cp(self.halo[:, c, :], cv[:, T:T + 3], [("cv", v)], ["halo"], eng="pool")
                a = self.nxt("ost", 4)
                if c < 16:
                    self.act(ct, ct, AF.Silu, [("cvt", t)], [("cvt", t)])
                    q = self.nxt("sq")
                    self.act(self.sq[q], ct, AF.Square, [("cvt", t)], [("sq", q)])
                    self.mm(self.psS, self.ones_bf[:], self.sq[q], True, True, [("sq", q)], ["psS"])
                    self.act(self.rstd, self.psS, AF.Sqrt, ["psS"], ["rstd"], bias=self.eps_t[:, 0:1])
                    self.op("dve", "reciprocal", self.rstd, self.rstd, r=["rstd"], w=["rstd"])
                    self.stt(self.ost[a], ct, (128.0 ** -0.5) if c < 8 else 1.0, self.rstd, ALU.mult, ALU.mult,
                             [("cvt", t), "rstd"], [("ost", a)])
                else:
                    self.act(self.ost[a], ct, AF.Silu, [("cvt", t)], [("ost", a)])
                self.store(Dm["dn_qkv"][c * 128:(c + 1) * 128, t0:t0 + T], self.ost[a], ("ost", a),
                           [("ost", a)], [("G", "dn_qkv", tix)])
            elif c < 32:
                a = self.nxt("ost", 4)
                self.act(self.ost[a], ps, AF.Silu, [pk], [("ost", a)])
                self.store(Dm["dn_zg"][(c - 24) * 128:(c - 23) * 128, t0:t0 + T], self.ost[a], ("ost", a),
                           [("ost", a)], [("G", "dn_zg", tix)])
            elif c < 40 or c in (44, 45, 48, 49):
                q = self.nxt("sq")
                self.act(self.sq[q], ps, AF.Square, [pk], [("sq", q)])
                self.mm(self.psS, self.ones_bf[:], self.sq[q], True, True, [("sq", q)], ["psS"])
                self.act(self.rstd, self.psS, AF.Sqrt, ["psS"], ["rstd"], bias=self.eps_t[:, 0:1], scale=1.0 / 128)
                self.op("dve", "reciprocal", self.rstd, self.rstd, r=["rstd"], w=["rstd"])
                a = self.nxt("ost", 4)
                ob = self.ost[a].bitcast(BF16)[:, 0:T]
                if c < 40:
                    gcol = self.qnw[:, l:l + 1]
                    dst = Dm["ns_q"][(c - 32) * 128:(c - 31) * 128, t0:t0 + T]
                    dk = "ns_q"
                else:
                    kn = 3 if c < 48 else 4
                    gcol = self.n128[:, l, kn:kn + 1]
                    row = {44: 0, 45: 1, 48: 2, 49: 3}[c]
                    dst = Dm["ns_k"][row * 128:(row + 1) * 128, t0:t0 + T]
                    dk = "ns_k"
                self.stt(ob, ps, gcol, self.rstd, ALU.mult, ALU.mult, [pk, "rstd", ("G", "n128"), ("G", "qnw")], [("ost", a)])
                self.store(dst, ob, ("ost", a), [("ost", a)], [("G", dk, tix)])
            elif c < 44:
                a = self.nxt("ost", 4)
                self.cp(self.ost[a], ps, [pk], [("ost", a)], eng="act")
                self.store(Dm["ns_kvc"][(c - 40) * 128:(c - 39) * 128, t0:t0 + T], self.ost[a], ("ost", a),
                           [("ost", a)], [("G", "ns_kvc", tix)])
            elif c < 52:
                a = self.nxt("ost", 4)
                self.cp(self.ost[a], ps, [pk], [("ost", a)], eng="act")
                for sub in range(4):
                    self.tr(self.psX[:, sub * 128:(sub + 1) * 128], self.ost[a][:, sub * 128:(sub + 1) * 128],
                            self.ident_f[:], [("ost", a)], ["psX"])
                vt = self.nxt("vtok")
                self.cp(self.vtok[vt], self.psX.rearrange("p (s c) -> p s c", s=4), ["psX"], [("vtok", vt)])
                row = {46: 0, 47: 1, 50: 2, 51: 3}[c]
                self.store(Dm["ns_v"][row, t0:t0 + T, :].rearrange("(s p) d -> p s d", p=128), self.vtok[vt],
                           ("vtok", vt), [("vtok", vt)], [("G", "ns_v", tix)])
            elif c < 60:
                g = c - 52
                self.act(uT[:, g, :], ps, AF.Gelu_apprx_tanh, [pk], uvk(g))
            elif c < 68:
                g = c - 60
                self.act(vfT[:, g, :], ps, AF.Gelu_apprx_tanh, [pk], uvk(8 + g))
                q = self.nxt("sq")
                self.cp(self.sq[q], vfT[:, g, :], uvk(8 + g), [("sq", q)])
                self.mm(self.psD[0], self.ones_bf[:], self.sq[q], g == 0, g == 7, [("sq", q)], [("psD", 0)])
                q = self.nxt("sq")
                self.act(self.sq[q], vfT[:, g, :], AF.Square, uvk(8 + g), [("sq", q)])
                self.mm(self.psD[1], self.ones_bf[:], self.sq[q], g == 0, g == 7, [("sq", q)], [("psD", 1)])
                if g == 7:
                    self.sgu_finish(l, t0)
            else:
                a = self.nxt("ost", 4)
                self.act(self.ost[a], ps, AF.Sigmoid, [pk], [("ost", a)])
                self.store(Dm["gates"][(c - 68) * 128:(c - 67) * 128, t0:t0 + T], self.ost[a], ("ost", a),
                           [("ost", a)], [("G", "gates", tix)])

    def sgu_finish(self, l, t0):
        T = self.T
        tix = t0 // T
        uT = self.uvT[:, 0:8, :]
        vfT = self.uvT[:, 8:16, :]
        uvk = lambda j: [("actT", 2 * j), ("actT", 2 * j + 1)]
        mu, var, rs, tmp = (self.lnst[:, i, :] for i in range(4))
        self.ts(mu, self.psD[0], 1.0 / 1024, None, ALU.mult, None, [("psD", 0)], ["ln_mu"])
        self.tt(var, mu, mu, ALU.mult, ["ln_mu"], ["ln_var"])
        self.stt(var, self.psD[1], 1.0 / 1024, var, ALU.mult, ALU.subtract, [("psD", 1), "ln_var"], ["ln_var"])
        self.act(rs, var, AF.Sqrt, ["ln_var"], ["ln_rs"], bias=self.eps_t[:, 0:1])
        self.op("dve", "reciprocal", rs, rs, r=["ln_rs"], w=["ln_rs"])
        for g in range(8):
            v = self.nxt("cvt")
            vn = self.cvt[v]
            self.tt(vn, vfT[:, g, :], mu, ALU.subtract, uvk(8 + g) + ["ln_mu"], [("cvt", v)])
            self.tt(vn, vn, rs, ALU.mult, [("cvt", v), "ln_rs"], [("cvt", v)])
            self.ts(vn, vn, self.sgnw[:, g:g + 1], self.sgnw[:, 8 + g:9 + g], ALU.mult, ALU.add,
                    [("cvt", v), "sgnw"], [("cvt", v)])
            for sub in range(4):
                self.tr(self.psX[:, sub * 128:(sub + 1) * 128], vn[:, sub * 128:(sub + 1) * 128], self.ident_f[:],
                        [("cvt", v)], ["psX"])
            vt = self.nxt("vtok")
            self.cp(self.vtok[vt], self.psX.rearrange("p (s c) -> p s c", s=4), ["psX"], [("vtok", vt)], eng="act")
            p = self.nxt("psD")
            for sub in range(4):
                self.mm(self.psD[p][:, sub * 128:(sub + 1) * 128], self.vtok[vt][:, sub, :], self.wsT[:, g, :],
                        True, True, [("vtok", vt), "wsT"], [("psD", p)])
            y = self.nxt("sg")
            self.tt(self.sg[y].rearrange("p (s c) -> p s c", s=4), self.psD[p].rearrange("p (s c) -> p s c", s=4),
                    self.bsb[:, g, :].unsqueeze(1).to_broadcast([128, 4, 128]), ALU.add,
                    [("psD", p), "bsb"], [("sg", y)])
            self.tt(self.ycT[:, g, :], self.sg[y], uT[:, g, :], ALU.mult, [("sg", y)] + uvk(g), ["ycT"])
        yv = self.D["yT"].rearrange("(k p) s -> p k s", p=128)
        self.store(yv[:, 16:24, t0:t0 + T], self.ycT, "ycT", ["ycT"], [("G", "yT", tix)])

    def rl_stage(self, name, l_merge, l_ffn2, l_ffn1, l_win, src, dst, dst_key):
        S, T = self.S, self.T
        self.stage_begin(name)
        self.alloc_rl()
        if l_win is not None:
            self.win_setup(l_win)
        I = self.I
        srcv = src.rearrange("(k p) s -> p k s", p=128)
        dstv = dst.rearrange("(k p) s -> p k s", p=128)
        for ti in range(S // T):
            t0 = ti * T
            self.load(self.xT, srcv[:, :, t0:t0 + T], "xT", [("G", "xTd", ti)], self.xb())
            if l_merge is not None:
                self.merge(l_merge, t0)
            if l_ffn2 is not None:
                self.ffn(3 * l_ffn2 + 2, "ffn2", l_ffn2)
            if l_ffn1 is not None:
                self.ffn(3 * l_ffn1 + 0, "ffn1", l_ffn1)
            self.store(dstv[:, :, t0:t0 + T], self.xT, "xT", self.xb(), [("G", dst_key, ti)])
            if l_win is not None:
                self.win(l_win, t0)

    def pslot(self):
        i = self.nxt("pslot", 4)
        v = self.bank(2 * i, 2).rearrange("p (h c) -> p h c", h=8)
        return v, ("pslot", i)

    def pslot_bf(self):
        i = self.nxt("pslot", 4)
        v = self.bank(2 * i, 1).bitcast(BF16).rearrange("p (h c) -> p h c", h=8)
        return v, ("pslot", i)

    def dn_stage(self, l):
        S, T = self.S, self.T
        self.stage_begin("dn%d" % l)
        c = self.carve
        Dm = self.D
        H = 8
        sh = [128, H, 128]
        triC = c([128, 128], F32)
        pmask = c([128, 128], F32)
        sl01 = c([128, 128], F32)
        self.memset(triC, 1.0, ["triC"])
        self.asel(triC, triC, [[1, 128]], ALU.is_ge, 0.0, 0, -1, ["triC"], ["triC"])
        self.memset(pmask, 0.0, ["pmask"])
        self.asel(pmask, pmask, [[-1, 128]], ALU.is_ge, 30000.0, 0, 1, ["pmask"], ["pmask"])
        self.memset(sl01, 1.0, ["sl01"])
        self.asel(sl01, sl01, [[-1, 128]], ALU.is_gt, 0.0, 0, 1, ["sl01"], ["sl01"])
        St = c(sh, F32)
        Sb = c(sh, BF16)
        self.memset(St, 0.0, ["St"])
        self.memset(Sb, 0.0, ["Sb"])
        qT = [c(sh, F32) for _ in range(2)]
        kT = [c(sh, F32) for _ in range(2)]
        vT = [c(sh, F32) for _ in range(2)]
        zg = [c(sh, F32) for _ in range(2)]
        sm = [c([128, 64], F32) for _ in range(2)]
        gcs = c([128, 16], F32)
        sc = c([128, 40], F32)
        Gbc = c(sh, F32)
        qTb = c(sh, BF16)
        kTb = c(sh, BF16)
        dec = c(sh, F32)
        decs = c(sh, F32)
        t1 = c(sh, F32)
        Lp = [c(sh, F32) for _ in range(2)]
        Up = [c(sh, F32) for _ in range(2)]
        Xp = [c(sh, F32) for _ in range(2)]
        At = c(sh, BF16)
        AtT = c(sh, BF16)
        kbg = c(sh, F32)
        kdec = c(sh, BF16)
        vb = c(sh, F32)
        u = c(sh, F32)
        wTb = c(sh, F32)
        vnew = c(sh, BF16)
        o = c(sh, F32)
        osq = c(sh, F32)
        ss = c([128, 8], F32)
        ya = [c(sh, BF16) for _ in range(2)]
        onw = self.n128[:, l, 0:1]
        qv = Dm["dn_qkv"].rearrange("(t h d) s -> t d h s", t=3, h=H)
        zv = Dm["dn_zg"].rearrange("(h d) s -> d h s", h=H)
        yv = Dm["yT"].rearrange("(k p) s -> p k s", p=128)

        def bc(col):
            return col.unsqueeze(2).to_broadcast(sh)

        for n in range(S // 128):
            t0 = n * 128
            tix = t0 // T
            sl = n % 2
            self.load(qT[sl], qv[0][:, :, t0:t0 + 128], ("qT", sl), [("G", "dn_qkv", tix)], [("qT", sl)])
            self.load(kT[sl], qv[1][:, :, t0:t0 + 128], ("kT", sl), [("G", "dn_qkv", tix)], [("kT", sl)])
            self.load(vT[sl], qv[2][:, :, t0:t0 + 128], ("vT", sl), [("G", "dn_qkv", tix)], [("vT", sl)])
            self.load(zg[sl], zv[:, :, t0:t0 + 128], ("zg", sl), [("G", "dn_zg", tix)], [("zg", sl)])
            self.load(sm[sl], Dm["sm_tok"][t0:t0 + 128, :], ("sm", sl), [("G", "sm_tok", tix)], [("sm", sl)])
            g8 = sm[sl][:, 0:8]
            beta = sm[sl][:, 8:16]
            pG, kG = self.pslot()
            pg = pG[:, 0, 0:16]
            self.mm(pg[:, 0:8], triC, g8, True, True, ["triC", ("sm", sl)], [kG])
            self.mm(pg[:, 8:16], self.ones_f[:], g8, True, True, [("sm", sl)], [kG])
            self.cp(gcs, pg, [kG], ["gcs"])
            gc = gcs[:, 0:8]
            glb = gcs[:, 8:16]
            self.act(sc[:, 0:8], gc, AF.Exp, ["gcs"], ["sc"])
            self.tt(sc[:, 8:16], glb, gc, ALU.subtract, ["gcs"], ["sc"])
            self.act(sc[:, 8:16], sc[:, 8:16], AF.Exp, ["sc"], ["sc"])
            self.act(sc[:, 16:24], glb, AF.Exp, ["gcs"], ["sc"])
            self.tt(sc[:, 24:32], beta, sc[:, 0:8], ALU.mult, [("sm", sl), "sc"], ["sc"])
            egc, edl, egl, bg = sc[:, 0:8], sc[:, 8:16], sc[:, 16:24], sc[:, 24:32]
            self.cp(Gbc, bc(g8), [("sm", sl)], ["Gbc"], eng="pool")
            self.cp(qTb, qT[sl], [("qT", sl)], ["qTb"], eng="pool")
            self.cp(kTb, kT[sl], [("kT", sl)], ["kTb"])
            pR, kR = self.pslot()
            for h in range(H):
                self.mm(pR[:, h, :], Gbc[:, h, :], triC, True, False, ["Gbc", "triC"], [kR])
                self.mm(pR[:, h, :], self.ident_f[:], pmask, False, True, ["pmask"], [kR])
            self.tt(t1, pR, bc(gc), ALU.subtract, [kR, "gcs"], ["t1"])
            self.act(dec, t1, AF.Exp, ["t1"], ["dec"], scale=-1.0)
            self.tt(decs, dec, sl01.unsqueeze(1).to_broadcast(sh), ALU.mult, ["dec", "sl01"], ["decs"], eng="pool")
            pKK, kKK = self.pslot()
            for h in range(H):
                self.mm(pKK[:, h, :], kT[sl][:, h, :], kT[sl][:, h, :], True, True, [("kT", sl)], [kKK])
            pQK, kQK = self.pslot()
            for h in range(H):
                self.mm(pQK[:, h, :], qTb[:, h, :], kTb[:, h, :], True, True, ["qTb", "kTb"], [kQK])
            self.tt(t1, pKK, decs, ALU.mult, [kKK, "decs"], ["t1"])
            self.tt(Lp[0], t1, bc(beta), ALU.mult, ["t1", ("sm", sl)], [("Lp", 0)])
            self.tt(At, pQK, dec, ALU.mult, [kQK, "dec"], ["At"])
            pU, kU = self.pslot()
            for h in range(H):
                self.tr(pU[:, h, :], Lp[0][:, h, :], self.ident_f[:], [("Lp", 0)], [kU])
            self.cp(Up[0], pU, [kU], [("Up", 0)], eng="act")
            pA, kA = self.pslot_bf()
            for h in range(H):
                self.tr(pA[:, h, :], At[:, h, :], self.ident_b[:], ["At"], [kA])
            self.cp(AtT, pA, [kA], ["AtT"], eng="act")
            self.tt(Xp[0], self.ident_f[:].unsqueeze(1).to_broadcast(sh), Up[0], ALU.subtract, [("Up", 0)], [("Xp", 0)])
            cur = 0
            xc = 0
            for step in range(6):
                nx = 1 - cur
                pL, kL = self.pslot()
                for h in range(H):
                    self.mm(pL[:, h, :], Up[cur][:, h, :], Lp[cur][:, h, :], True, True, [("Up", cur), ("Lp", cur)], [kL])
                if step < 5:
                    pU2, kU2 = self.pslot()
                    for h in range(H):
                        self.mm(pU2[:, h, :], Lp[cur][:, h, :], Up[cur][:, h, :], True, True,
                                [("Up", cur), ("Lp", cur)], [kU2])
                self.cp(Lp[nx], pL, [kL], [("Lp", nx)], eng="act")
                if step < 5:
                    self.cp(Up[nx], pU2, [kU2], [("Up", nx)])
                pX, kX = self.pslot()
                for h in range(H):
                    self.mm(pX[:, h, :], Lp[nx][:, h, :], Xp[xc][:, h, :], True, True, [("Lp", nx), ("Xp", xc)], [kX])
                self.tt(Xp[1 - xc], pX, Xp[xc], ALU.add, [kX, ("Xp", xc)], [("Xp", 1 - xc)])
                xc = 1 - xc
                cur = nx
            X = Xp[xc]
            kXk = ("Xp", xc)
            pKt, kKt = self.pslot()
            for h in range(H):
                self.tr(pKt[:, h, :], kT[sl][:, h, :], self.ident_f[:], [("kT", sl)], [kKt])
            self.tt(kbg, pKt, bc(bg), ALU.mult, [kKt, "sc"], ["kbg"])
            self.tt(kdec, pKt, bc(edl), ALU.mult, [kKt, "sc"], ["kdec"])
            pVt, kVt = self.pslot()
            for h in range(H):
                self.tr(pVt[:, h, :], vT[sl][:, h, :], self.ident_f[:], [("vT", sl)], [kVt])
            self.tt(vb, pVt, bc(beta), ALU.mult, [kVt, ("sm", sl)], ["vb"])
            pu, ku = self.pslot()
            for h in range(H):
                self.mm(pu[:, h, :], X[:, h, :], vb[:, h, :], True, True, [kXk, "vb"], [ku])
            self.cp(u, pu, [ku], ["u"], eng="act")
            pw, kw = self.pslot()
            for h in range(H):
                self.mm(pw[:, h, :], kbg[:, h, :], X[:, h, :], True, True, [kXk, "kbg"], [kw])
            self.cp(wTb, pw, [kw], ["wTb"], eng="act")
            pWS, kWS = self.pslot()
            for h in range(H):
                self.mm(pWS[:, h, :], wTb[:, h, :], St[:, h, :], True, True, ["wTb", "St"], [kWS])
            self.tt(vnew, u, pWS, ALU.subtract, ["u", kWS], ["vnew"])
            pQS, kQS = self.pslot()
            for h in range(H):
                self.mm(pQS[:, h, :], qTb[:, h, :], Sb[:, h, :], True, True, ["qTb", "Sb"], [kQS])
            pAV, kAV = self.pslot()
            for h in range(H):
                self.mm(pAV[:, h, :], AtT[:, h, :], vnew[:, h, :], True, True, ["AtT", "vnew"], [kAV])
            self.tt(o, pQS, bc(egc), ALU.mult, [kQS, "sc"], ["o"])
            self.tt(o, o, pAV, ALU.add, ["o", kAV], ["o"])
            pKV, kKV = self.pslot()
            for h in range(H):
                self.mm(pKV[:, h, :], kdec[:, h, :], vnew[:, h, :], True, True, ["kdec", "vnew"], [kKV])
            self.tt(St, St, bc(egl), ALU.mult, ["St", "sc"], ["St"], eng="pool")
            self.tt(St, St, pKV, ALU.add, ["St", kKV], ["St"])
            self.cp(Sb, St, ["St"], ["Sb"], eng="act")
            self.act(osq, o, AF.Square, ["o"], ["osq"])
            self.op("dve", "tensor_reduce", ss, osq, AX.X, ALU.add, r=["osq"], w=["ss"])
            self.act(ss, ss, AF.Sqrt, ["ss"], ["ss"], bias=self.eps_t[:, 0:1], scale=1.0 / 128)
            self.op("dve", "reciprocal", ss, ss, r=["ss"], w=["ss"])
            self.tt(o, o, bc(ss), ALU.mult, ["o", "ss"], ["o"])
            pO, kO = self.pslot()
            for h in range(H):
                self.tr(pO[:, h, :], o[:, h, :], self.ident_f[:], ["o"], [kO])
            self.stt(ya[sl], pO, onw, zg[sl], ALU.mult, ALU.mult, [kO, ("zg", sl)], [("ya", sl)])
            self.store(yv[:, 0:8, t0:t0 + 128], ya[sl], ("ya", sl), [("ya", sl)], [("G", "yT", tix)])

    def nsa_stage(self, l):
        S, T = self.S, self.T
        self.stage_begin("nsa%d" % l)
        c = self.carve
        Dm = self.D
        I = self.I
        NQB = S // 128
        NCMP = S // 16 - 1
        NCP = NCMP + 1
        NCC = (NCP + 127) // 128
        NSEL = S // 64
        sh4 = [128, 4, 128]
        cmask = c([128, 16, 128], BF16)
        cmf = c([128, 128], F32)
        for a in range(16):
            self.memset(cmf, 0.0, ["cmf"])
            self.asel(cmf, cmf, [[1, 128]], ALU.is_ge, -30000.0, 128 * a - 15, -16, ["cmf"], ["cmf"])
            self.cp(cmask[:, a, :], cmf, ["cmf"], ["cmask"], eng="pool")
        caus = c([128, 128], BF16)
        self.memset(cmf, 0.0, ["cmf"])
        self.asel(cmf, cmf, [[1, 128]], ALU.is_ge, -30000.0, 0, -1, ["cmf"], ["cmf"])
        self.cp(caus, cmf, ["cmf"], ["caus"], eng="pool")
        wlow = c([128, 128], BF16)
        self.memset(cmf, 0.0, ["cmf"])
        self.asel(cmf, cmf, [[-1, 128]], ALU.is_gt, -30000.0, 0, 1, ["cmf"], ["cmf"])
        self.cp(wlow, cmf, ["cmf"], ["wlow"], eng="pool")
        Ebig = c([128, S], BF16)
        ebf = c([128, 512], F32)
        for k0 in range(0, S, 512):
            self.memset(ebf, 1.0, ["ebf"])
            self.asel(ebf, ebf, [[1, 512]], ALU.is_ge, 0.0, k0, -64, ["ebf"], ["ebf"])
            self.asel(ebf, ebf, [[-1, 512]], ALU.is_ge, 0.0, 63 - k0, 64, ["ebf"], ["ebf"])
            self.cp(Ebig[:, k0:k0 + 512], ebf, ["ebf"], ["Ebig"], eng="pool")
        ovl = c([128, NCC, 128], F32)
        ov2 = c([128, NCC, 128], F32)
        for (t, base) in ((ovl, 0), (ov2, -1)):
            nm = "ovl" if base == 0 else "ov2"
            self.memset(t, 1.0, [nm])
            self.asel(t, t, [[128, NCC], [-4, 128]], ALU.is_ge, 0.0, base, 1, [nm], [nm])
            self.asel(t, t, [[-128, NCC], [4, 128]], ALU.is_ge, 0.0, 3 - base, -1, [nm], [nm])
        self.tt(ovl, ovl, ov2, ALU.add, ["ovl", "ov2"], ["ovl"], eng="pool")
        self.memset(ovl[0:1, 0, :], 0.0, ["ovl"])
        slr = c([1, 8], F32)
        for h in range(8):
            self.memset(slr[:, h:h + 1], SLOPES[h], ["slr"])
        NM = NQB + 16
        tabf = c([1, NM, 8], F32)
        self.sch.op("pool", lambda e: e.iota(tabf, pattern=[[128, NM], [0, 8]], base=-128 * (NQB - 1),
                                             channel_multiplier=0, allow_small_or_imprecise_dtypes=True),
                    (), self.Bs(["tabf"]))
        self.tt(tabf, tabf, slr.unsqueeze(1).to_broadcast([1, NM, 8]), ALU.mult, ["tabf", "slr"], ["tabf"], eng="pool")
        tabm = c([1, NM, 8], BF16)
        self.cp(tabm, tabf, ["tabf"], ["tabm"], eng="pool")
        ones1 = c([1, 128], BF16)
        self.memset(ones1, 1.0, ["ones1"])
        a2f = c([2, 128], F32)
        aLc = c([2, 128], BF16)
        aLs = c([2, 128], BF16)
        self.memset(a2f, 1.0, ["a2f"])
        self.sch.op("pool", lambda e: e.iota(a2f[0:1, :], pattern=[[16, 128]], base=0, channel_multiplier=0,
                                             allow_small_or_imprecise_dtypes=True), (), self.Bs(["a2f"]))
        self.cp(aLc, a2f, ["a2f"], ["aLc"], eng="pool")
        self.sch.op("pool", lambda e: e.iota(a2f[0:1, :], pattern=[[1, 128]], base=0, channel_multiplier=0,
                                             allow_small_or_imprecise_dtypes=True), self.Bs(["a2f"]), self.Bs(["a2f"]))
        self.cp(aLs, a2f, ["a2f"], ["aLs"], eng="pool")
        r2f = c([2, 8, 128], F32)
        aRc = c([2, 8, 128], BF16)
        aRs = c([2, 8, 128], BF16)
        for (dst, nm, b0, st) in ((aRc, "aRc", 15, -1), (aRs, "aRs", 0, -1)):
            self.sch.op("pool", lambda e, b0=b0, st=st: e.iota(r2f, pattern=[[0, 8], [st, 128]], base=b0,
                                                               channel_multiplier=0, allow_small_or_imprecise_dtypes=True),
                        self.Bs(["r2f"]), self.Bs(["r2f"]))
            self.memset(r2f[0:1, :, :], 1.0, ["r2f"])
            self.tt(r2f, r2f, self.slr2(slr, c), ALU.mult, ["r2f", "slr2"], ["r2f"], eng="pool")
            self.cp(dst, r2f, ["r2f"], [nm], eng="pool")
        kcT = c([128, 2, NCC * 128], BF16)
        vca = c([128, NCC, 2, 257], BF16)
        self.memset(kcT, 0.0, ["kcT"])
        self.memset(vca, 0.0, ["vca"])
        for g in range(2):
            self.cp(vca[:, :, g, 129:257], ovl, ["ovl"], ["vca"], eng="pool")
            self.memset(vca[:, :, g, 128:129], 1.0, ["vca"])
            self.memset(vca[0:1, 0, g, 128:129], 0.0, ["vca"])
        amark = self.aoff
        xc = c([128, S], F32)
        xpb = c([128, 32, 512], BF16)
        w1 = c([128, 32, 256], BF16)
        w2 = c([128, 2, 128], BF16)
        hb = c([128, 2, 512], BF16)
        vcf = c([128, NCC * 128], F32)
        cst = c([128, 512], F32)
        csq = c([128, 512], BF16)
        crs = c([128, 512], F32)
        xcv = xc.rearrange("p (i r) -> p i r", r=16)
        for kv in range(2):
            self.load(w1, I["cmp_w1_t"][l, kv], "w1", (), ["w1"], queue="pool")
            self.load(w2, I["cmp_w2_t"][l, kv], "w2", (), ["w2"], queue="pool")
            for g in range(2):
                row = kv * 2 + g
                self.load(xc, Dm["ns_kvc"][row * 128:(row + 1) * 128, :], "xc",
                          [("G", "ns_kvc", i) for i in range(S // T)], ["xc"])
                for p in range(32):
                    src = xcv[:, 0:NCMP, p] if p < 16 else xcv[:, 1:NCMP + 1, p - 16]
                    self.ts(xpb[:, p, 0:NCMP], src, self.n128[:, l, 8 + 32 * kv + p:9 + 32 * kv + p], None, ALU.add, None,
                            ["xc"], ["xpb"], eng=("dve" if p % 2 == 0 else "pool"))
                for hc in range(2):
                    pH = self.bank(hc)
                    for p in range(32):
                        self.mm(pH[:, 0:NCMP], w1[:, p, hc * 128:(hc + 1) * 128], xpb[:, p, 0:NCMP], p == 0, p == 31,
                                ["w1", "xpb"], [("pb", hc)])
                    self.act(hb[:, hc, 0:NCMP], pH[:, 0:NCMP], AF.Silu, [("pb", hc)], ["hb"])
                pO = self.bank(2)
                for hc in range(2):
                    self.mm(pO[:, 0:NCMP], w2[:, hc, :], hb[:, hc, 0:NCMP], hc == 0, hc == 1, ["w2", "hb"], [("pb", 2)])
                if kv == 0:
                    self.act(csq[:, 0:NCMP], pO[:, 0:NCMP], AF.Square, [("pb", 2)], ["csq"])
                    pS = self.bank(3)
                    self.mm(pS[:, 0:NCMP], self.ones_bf[:], csq[:, 0:NCMP], True, True, ["csq"], [("pb", 3)])
                    self.act(crs[:, 0:NCMP], pS[:, 0:NCMP], AF.Sqrt, [("pb", 3)], ["crs"], bias=self.eps_t[:, 0:1],
                             scale=1.0 / 128)
                    self.op("dve", "reciprocal", crs[:, 0:NCMP], crs[:, 0:NCMP], r=["crs"], w=["crs"])
                    self.stt(kcT[:, g, 1:NCP], pO[:, 0:NCMP], self.n128[:, l, 2:3], crs[:, 0:NCMP], ALU.mult, ALU.mult,
                             [("pb", 2), "crs"], ["kcT"])
                else:
                    self.memset(vcf, 0.0, ["vcf"])
                    self.cp(vcf[:, 1:NCP], pO[:, 0:NCMP], [("pb", 2)], ["vcf"], eng="act")
                    pT = self.bank(3)
                    for k in range(NCC):
                        self.tr(pT[:, k * 128:(k + 1) * 128], vcf[:, k * 128:(k + 1) * 128], self.ident_f[:],
                                ["vcf"], [("pb", 3)])
                    self.cp(vca[:, :, g, 0:128], pT[:, 0:NCC * 128].rearrange("p (k d) -> p k d", d=128),
                            [("pb", 3)], ["vca"])
        self.sch.barrier()
        self.aoff = amark
        ksT = c([128, 2, S], BF16)
        vsa = c([128, NQB, 2, 129], BF16)
        for g in range(2):
            self.load(ksT[:, g, :], Dm["ns_k"][g * 128:(g + 1) * 128, :], "ksT",
                      [("G", "ns_k", i) for i in range(S // T)], ["ksT"])
            self.load(vsa[:, :, g, 0:128], Dm["ns_v"][g].rearrange("(n p) d -> p n d", p=128), "vsa",
                      [("G", "ns_v", i) for i in range(S // T)], ["vsa"])
            self.memset(vsa[:, :, g, 128:129], 1.0, ["vsa"])
        kwT = [c([128, 2, 640], BF16) for _ in range(2)]
        vwa = [c([128, 5, 2, 129], BF16) for _ in range(2)]
        for i in range(2):
            self.memset(vwa[i][:, :, :, 128:129], 1.0, [("vwa", i)])
        qT = [c([128, 8, 128], BF16) for _ in range(2)]
        gq = [c([128, 64], F32) for _ in range(2)]
        eT = [c([128, 512], BF16) for _ in range(3)]
        ev = c([128, 4, 257], F32)
        evs = c([128, 4, 129], F32)
        rs = c([128, 8], F32)
        wv = c([128, 8], F32)
        tmp4 = c(sh4, F32)
        imp = c([128, 128], F32)
        imp2 = c([128, 128], F32)
        m8 = c([128, 16], F32)
        nsel = c([128, 128], F32)
        nselT = c([128, 4, 128], BF16)
        onsa = c([128, 8, 128], F32)
        ybT = [c([128, 8, 128], BF16) for _ in range(2)]
        qv = Dm["ns_q"].rearrange("(h d) s -> d h s", h=8)
        yv = Dm["yT"].rearrange("(k p) s -> p k s", p=128)
        pSb = [self.bank(0), self.bank(1)]
        pS3 = [b.rearrange("p (h q) -> p h q", h=4) for b in pSb]
        acc = [self.bank(2 + i) for i in range(6)]

        def exp_chunk(kS, ps):
            e = self.nxt("eT", 3)
            self.act(eT[e], ps, AF.Exp, [kS], [("eT", e)])
            return e

        for qb in range(NQB):
            t0 = qb * 128
            tix = t0 // T
            sl = qb % 2
            self.load(qT[sl], qv[:, :, t0:t0 + 128], ("qT", sl), [("G", "ns_q", tix)], [("qT", sl)])
            self.load(gq[sl], Dm["sm_tok"][t0:t0 + 128, :], ("gq", sl), [("G", "sm_tok", tix)], [("gq", sl)])
            w0 = max(0, qb - 4)
            nw = qb - w0 + 1
            self.load(kwT[sl][:, :, 0:nw * 128],
                      Dm["ns_k"][256:512, w0 * 128:(qb + 1) * 128].rearrange("(g d) s -> d g s", g=2),
                      ("kwT", sl), [("G", "ns_k", i) for i in range(w0 * 128 // T, tix + 1)], [("kwT", sl)])
            for g in range(2):
                self.load(vwa[sl][:, 0:nw, g, 0:128],
                          Dm["ns_v"][2 + g, w0 * 128:(qb + 1) * 128, :].rearrange("(n p) d -> p n d", p=128),
                          ("vwa", sl), [("G", "ns_v", i) for i in range(w0 * 128 // T, tix + 1)], [("vwa", sl)])
            gates = gq[sl][:, 16:40].rearrange("p (b h) -> p b h", b=3)
            for g in range(2):
                q4 = qT[sl][:, 4 * g:4 * g + 4, :]
                aRc4 = aRc[:, 4 * g:4 * g + 4, :]
                aRs4 = aRs[:, 4 * g:4 * g + 4, :]
                def run_branch(items, score_fn, pv_fn):
                    prev = None
                    for it in items:
                        tok = score_fn(it)
                        if prev is not None:
                            pv_fn(*prev)
                        prev = (it, tok)
                    if prev is not None:
                        pv_fn(*prev)

                ncc = (8 * qb + 8 + 127) // 128

                def score_c(k):
                    s = self.nxt("pS")
                    kS = ("pS", s)
                    m = 16 * k - qb
                    self.mm(pS3[s], kcT[:, g, k * 128:(k + 1) * 128], q4, True, False, ["kcT", ("qT", sl)], [kS])
                    self.mm(pS3[s], aLc, aRc4, False, False, ["aLc", "aRc"], [kS])
                    if k == ncc - 1:
                        self.mm(pS3[s], self.ident_b[:], cmask[:, qb % 16, :].unsqueeze(1).to_broadcast(sh4), False, False,
                                ["cmask"], [kS])
                    self.mm(pS3[s], ones1, tabm[:, m + NQB - 1, 4 * g:4 * g + 4].unsqueeze(2).to_broadcast([1, 4, 128]),
                            False, True, ["ones1", "tabm"], [kS])
                    return exp_chunk(kS, pSb[s])

                def pv_c(k, e):
                    for h in range(4):
                        self.mm(acc[h][:, 0:257], eT[e][:, h * 128:(h + 1) * 128], vca[:, k, g, :], k == 0, k == ncc - 1,
                                [("eT", e), "vca"], [("acc", h)])

                def make_sw(br, k_lo):
                    def score(kc):
                        s = self.nxt("pS")
                        kS = ("pS", s)
                        m = kc - qb
                        if br == 1:
                            self.mm(pS3[s], ksT[:, g, kc * 128:(kc + 1) * 128], q4, True, False, ["ksT", ("qT", sl)], [kS])
                            self.mm(pS3[s], Ebig[:, kc * 128:(kc + 1) * 128], nselT, False, False, ["Ebig", "nselT"], [kS])
                        else:
                            self.mm(pS3[s], kwT[sl][:, g, (kc - w0) * 128:(kc - w0 + 1) * 128], q4, True, False,
                                    [("kwT", sl), ("qT", sl)], [kS])
                        self.mm(pS3[s], aLs, aRs4, False, False, ["aLs", "aRs"], [kS])
                        if kc == qb:
                            self.mm(pS3[s], self.ident_b[:], caus.unsqueeze(1).to_broadcast(sh4), False, False, ["caus"], [kS])
                        if br == 2 and kc == qb - 4:
                            self.mm(pS3[s], self.ident_b[:], wlow.unsqueeze(1).to_broadcast(sh4), False, False, ["wlow"], [kS])
                        self.mm(pS3[s], ones1,
                                tabm[:, m + NQB - 1, 4 * g:4 * g + 4].unsqueeze(2).to_broadcast([1, 4, 128]),
                                False, True, ["ones1", "tabm"], [kS])
                        return exp_chunk(kS, pSb[s])

                    def pv(kc, e):
                        for h in range(4):
                            ai = (0 if br == 1 else 4) + h // 2
                            rhs = vsa[:, kc, g, :] if br == 1 else vwa[sl][:, kc - w0, g, :]
                            self.mm(acc[ai][:, (h % 2) * 129:(h % 2) * 129 + 129], eT[e][:, h * 128:(h + 1) * 128], rhs,
                                    kc == k_lo and h % 2 == 0, kc == qb, [("eT", e), "vsa" if br == 1 else ("vwa", sl)],
                                    [("acc", ai)], skip=True)
                    return score, pv

                def finish_sw(br):
                    a0 = 0 if br == 1 else 4
                    for i in range(2):
                        self.cp(evs[:, 2 * i:2 * i + 2, :], acc[a0 + i][:, 0:258].rearrange("p (h c) -> p h c", h=2),
                                [("acc", a0 + i)], ["evs"], eng="act")
                    self.op("dve", "reciprocal", rs[:, 4:8], evs[:, :, 128], r=["evs"], w=["rs"])
                    self.tt(wv[:, 4:8], rs[:, 4:8], gates[:, br, 4 * g:4 * g + 4], ALU.mult, ["rs", ("gq", sl)], ["wv"])
                    self.tt(tmp4, evs[:, :, 0:128], wv[:, 4:8].unsqueeze(2).to_broadcast(sh4), ALU.mult, ["evs", "wv"], ["tmp4"])
                    self.tt(onsa[:, 4 * g:4 * g + 4, :], onsa[:, 4 * g:4 * g + 4, :], tmp4, ALU.add, ["onsa", "tmp4"], ["onsa"])

                run_branch(list(range(ncc)), score_c, pv_c)
                for h in range(4):
                    self.cp(ev[:, h, :], acc[h][:, 0:257], [("acc", h)], ["ev"], eng="act")
                self.ts(rs[:, 0:4], ev[:, :, 128], 1e-30, None, ALU.max, None, ["ev"], ["rs"])
                self.op("dve", "reciprocal", rs[:, 0:4], rs[:, 0:4], r=["rs"], w=["rs"])
                self.tt(wv[:, 0:4], rs[:, 0:4], gates[:, 0, 4 * g:4 * g + 4], ALU.mult, ["rs", ("gq", sl)], ["wv"])
                self.tt(onsa[:, 4 * g:4 * g + 4, :], ev[:, :, 0:128], wv[:, 0:4].unsqueeze(2).to_broadcast(sh4), ALU.mult,
                        ["ev", "wv"], ["onsa"])
                self.tt(tmp4, ev[:, :, 129:257], rs[:, 0:4].unsqueeze(2).to_broadcast(sh4), ALU.mult, ["ev", "rs"], ["tmp4"])
                self.tt(imp, tmp4[:, 0, :], tmp4[:, 1, :], ALU.add, ["tmp4"], ["imp"])
                self.tt(imp2, tmp4[:, 2, :], tmp4[:, 3, :], ALU.add, ["tmp4"], ["imp2"])
                self.tt(imp, imp, imp2, ALU.add, ["imp", "imp2"], ["imp"])
                j0 = 2 * qb
                if j0 + 2 < NSEL:
                    self.memset(imp[:, j0 + 2:NSEL], -1e6, ["imp"], eng="dve")
                self.memset(imp[0:64, j0 + 1:j0 + 2], -1e6, ["imp"], eng="dve")
                if j0 >= 1:
                    self.memset(imp[0:64, j0 - 1:j0 + 1], 1e6, ["imp"], eng="dve")
                else:
                    self.memset(imp[0:64, 0:1], 1e6, ["imp"], eng="dve")
                self.memset(imp[64:128, j0:j0 + 2], 1e6, ["imp"], eng="dve")
                self.memset(imp[:, 0:1], 1e6, ["imp"], eng="dve")
                if NSEL < 128:
                    self.memset(imp[:, NSEL:128], -1e6, ["imp"], eng="dve")
                self.op("dve", "max", out=m8[:, 0:8], in_=imp, r=["imp"], w=["m8"])
                self.op("dve", "match_replace", out=imp2, in_to_replace=m8[:, 0:8], in_values=imp, imm_value=-2e6,
                        r=["imp", "m8"], w=["imp2"])
                self.op("dve", "max", out=m8[:, 8:16], in_=imp2, r=["imp2"], w=["m8"])
                self.ts(nsel, imp, m8[:, 15:16], -32768.0, ALU.is_lt, ALU.mult, ["imp", "m8"], ["nsel"])
                sc_w, pv_w = make_sw(2, w0)
                run_branch(list(range(w0, qb + 1)), sc_w, pv_w)
                pT = acc[2]
                self.tr(pT[:, 0:128], nsel, self.ident_f[:], ["nsel"], [("acc", 2)])
                self.cp(nselT, pT[:, 0:128].unsqueeze(1).to_broadcast(sh4), [("acc", 2)], ["nselT"])
                sc_s, pv_s = make_sw(1, 0)
                run_branch(list(range(0, qb + 1)), sc_s, pv_s)
                finish_sw(2)
                finish_sw(1)
            pY = self.bank(4, 2).rearrange("p (h c) -> p h c", h=8)
            for h in range(8):
                self.tr(pY[:, h, :], onsa[:, h, :], self.ident_f[:], ["onsa"], [("acc", 2), ("acc", 3)])
            self.cp(ybT[sl], pY, [("acc", 2), ("acc", 3)], [("ybT", sl)], eng="act")
            self.store(yv[:, 8:16, t0:t0 + 128], ybT[sl], ("ybT", sl), [("ybT", sl)], [("G", "yT", tix)])

    def slr2(self, slr, c):
        if not hasattr(self, "_slr2") or self._slr2_stage != self.stage:
            t = c([2, 8, 128], F32)
            for h in range(8):
                self.memset(t[:, h, :], SLOPES[h], ["slr2"])
            self._slr2 = t
            self._slr2_stage = self.stage
        return self._slr2

    def convert_weights(self):
        self.Wb = {}
        order = ["ffn1_gate_t", "ffn1_up_t", "ffn1_down_t", "w_in_t", "w_br_t", "w_out_t",
                 "ffn2_gate_t", "ffn2_up_t", "ffn2_down_t"]
        views = {}
        for nm in order:
            src = self.I[nm]
            shp = list(src.shape)
            dst = self.nc.dram_tensor(nm + "_bf", shp, BF16, kind="Internal").ap()
            self.Wb[nm] = dst
            if nm == "w_br_t":
                views[nm] = (src.rearrange("l b c p k n -> l (b c) p (k n)"), dst.rearrange("l b c p k n -> l (b c) p (k n)"))
            else:
                views[nm] = (src.rearrange("l c p k n -> l c p (k n)"), dst.rearrange("l c p k n -> l c p (k n)"))
        for l in range(self.NL):
            for nm in order:
                sv, dv = views[nm]
                nch = sv.shape[1]
                per = sv.shape[3] * 128 * 4
                grp = max(1, (4 << 20) // per)
                for c0 in range(0, nch, grp):
                    c1 = min(nch, c0 + grp)
                    self.load(dv[l, c0:c1], sv[l, c0:c1], ("cvt", nm), (), [("G", "wbf", nm, l)], queue="pool")

    def build(self):
        S, NL = self.S, self.NL
        I = {}
        I["xT_in"] = self.dram_in("xT_in", [D, S])
        I["gam_in"] = self.dram_in("gam_in", [128, 3 * NL, KC])
        I["n128"] = self.dram_in("n128", [NL, 128, 72])
        for f in ("ffn1", "ffn2"):
            I[f + "_gate_t"] = self.dram_in(f + "_gate_t", [NL, FC, 128, KC, 128])
            I[f + "_up_t"] = self.dram_in(f + "_up_t", [NL, FC, 128, KC, 128])
            I[f + "_down_t"] = self.dram_in(f + "_down_t", [NL, KC, 128, FC, 128])
        I["w_in_t"] = self.dram_in("w_in_t", [NL, NWIN, 128, KC, 128])
        I["w_sm_t"] = self.dram_in("w_sm_t", [NL, 128, KC, 64])
        I["w_br_t"] = self.dram_in("w_br_t", [NL, 3, KC, 128, 8, 128])
        I["w_out_t"] = self.dram_in("w_out_t", [NL, KC, 128, KC, 128])
        I["dn_conv_t"] = self.dram_in("dn_conv_t", [NL, 128, 24, 4])
        I["dn_vec"] = self.dram_in("dn_vec", [NL, 16])
        I["cmp_w1_t"] = self.dram_in("cmp_w1_t", [NL, 2, 128, 32, 256])
        I["cmp_w2_t"] = self.dram_in("cmp_w2_t", [NL, 2, 128, 2, 128])
        I["sgu_nw"] = self.dram_in("sgu_nw", [NL, 128, 16])
        I["sgu_wT"] = self.dram_in("sgu_wT", [NL, 128, 8, 128])
        I["sgu_b"] = self.dram_in("sgu_b", [NL, 8, 128])
        self.I = I
        out = self.dram_out("yT_out", [D, S])
        Dm = {}
        Dm["xTd"] = self.dram_tmp("xTd", [D, S])
        Dm["dn_qkv"] = self.dram_tmp("dn_qkv", [3072, S])
        Dm["dn_zg"] = self.dram_tmp("dn_zg", [1024, S])
        Dm["sm_tok"] = self.dram_tmp("sm_tok", [S, 64])
        Dm["ns_q"] = self.dram_tmp("ns_q", [1024, S], BF16)
        Dm["ns_kvc"] = self.dram_tmp("ns_kvc", [512, S])
        Dm["ns_k"] = self.dram_tmp("ns_k", [512, S], BF16)
        Dm["ns_v"] = self.dram_tmp("ns_v", [4, S, 128], BF16)
        Dm["gates"] = self.dram_tmp("gates", [6144, S])
        Dm["yT"] = self.dram_tmp("yT", [3072, S], BF16)
        self.D = Dm
        self.setup_consts()
        self.convert_weights()
        st = self.stages
        mix = lambda l: [f for f in ("dn", "nsa") if st is None or f in st]
        last = NL - 1
        if st is not None and "rl0only" in st:
            self.rl_stage("rl0", None, None, 0, 0, I["xT_in"], out, "out")
            return self.sch.finalize()
        self.rl_stage("rl0", None, None, 0, 0, I["xT_in"], Dm["xTd"], "xTd")
        for l in range(NL):
            if "dn" in mix(l):
                self.dn_stage(l)
            if "nsa" in mix(l):
                self.nsa_stage(l)
            if st is not None and "nomerge" in st:
                continue
            if l < last:
                self.rl_stage("rl%d" % (l + 1), l, l, l + 1, l + 1, Dm["xTd"], Dm["xTd"], "xTd")
            else:
                self.rl_stage("rl%d" % (l + 1), l, l, None, None, Dm["xTd"], out, "out")
        return self.sch.finalize()


IN_SIZES = (1024, 1024, 1024, 1024, 8, 8, 1024, 256, 256, 256, 256, 256, 256, 24, 1024, 1024, 2048, 2048, 2048)


def tile_w(w):
    K, N = w.shape
    return np.ascontiguousarray(w.reshape(K // 128, 128, N // 128, 128).transpose(2, 1, 0, 3))


def prep_weights(inp, NL):
    f32 = np.float32
    o = {}
    gam = np.zeros((128, 3 * NL, KC), f32)
    for l in range(NL):
        for i, nm in enumerate(("ffn1_norm", "mix_norm", "ffn2_norm")):
            gam[:, 3 * l + i, :] = np.asarray(inp[nm][l], f32).reshape(KC, 128).T
    o["gam_in"] = gam
    n128 = np.zeros((NL, 128, 72), f32)
    for l in range(NL):
        n128[l, :, 0] = inp["dn_out_norm"][l]
        n128[l, :, 1] = inp["nsa_q_norm"][l]
        n128[l, :, 2:5] = np.asarray(inp["nsa_k_norm"][l]).T
        n128[l, :, 8:40] = np.asarray(inp["cmpk_pos"][l]).T
        n128[l, :, 40:72] = np.asarray(inp["cmpv_pos"][l]).T
    o["n128"] = n128
    for f in ("ffn1", "ffn2"):
        o[f + "_gate_t"] = np.stack([tile_w(np.asarray(inp[f + "_gate"][l], f32)) for l in range(NL)])
        o[f + "_up_t"] = np.stack([tile_w(np.asarray(inp[f + "_up"][l], f32)) for l in range(NL)])
        o[f + "_down_t"] = np.stack([tile_w(np.asarray(inp[f + "_down"][l], f32)) for l in range(NL)])
    offs = np.concatenate([[0], np.cumsum(IN_SIZES)])
    big = np.concatenate([np.arange(0, 4096), np.arange(4112, 6672), np.arange(6696, 14888)])
    small = np.concatenate([np.arange(4096, 4112), np.arange(6672, 6696)])
    assert big.size == NWIN * 128 and offs[-1] == 14888
    w_in_t, w_sm_t = [], []
    for l in range(NL):
        w = np.asarray(inp["w_in"][l], f32)
        w_in_t.append(tile_w(w[:, big]))
        ws = np.zeros((D, 64), f32)
        ws[:, 0:40] = w[:, small]
        w_sm_t.append(np.ascontiguousarray(ws.reshape(KC, 128, 64).transpose(1, 0, 2)))
    o["w_in_t"] = np.stack(w_in_t)
    o["w_sm_t"] = np.stack(w_sm_t)
    o["w_br_t"] = np.stack([np.stack([tile_w(np.asarray(inp[nm][l], f32))
                                      for nm in ("w_branch_a", "w_branch_b", "w_branch_c")]) for l in range(NL)])
    o["w_out_t"] = np.stack([tile_w(np.asarray(inp["w_out"][l], f32)) for l in range(NL)])
    o["dn_conv_t"] = np.ascontiguousarray(np.asarray(inp["dn_conv"], f32).reshape(NL, 4, 24, 128).transpose(0, 3, 2, 1))
    o["dn_vec"] = np.concatenate([np.asarray(inp["dn_a_log"], f32), np.asarray(inp["dn_dt_bias"], f32)], axis=1)
    w1 = np.stack([np.asarray(inp["cmpk_w1"], f32), np.asarray(inp["cmpv_w1"], f32)], axis=1)
    o["cmp_w1_t"] = np.ascontiguousarray(w1.reshape(NL, 2, 32, 128, 256).transpose(0, 1, 3, 2, 4))
    w2 = np.stack([np.asarray(inp["cmpk_w2"], f32), np.asarray(inp["cmpv_w2"], f32)], axis=1)
    o["cmp_w2_t"] = np.ascontiguousarray(w2.reshape(NL, 2, 2, 128, 128).transpose(0, 1, 3, 2, 4))
    nw = np.zeros((NL, 128, 16), f32)
    for l in range(NL):
        nw[l, :, 0:8] = np.asarray(inp["sgu_norm_w"][l], f32).reshape(8, 128).T
        nw[l, :, 8:16] = np.asarray(inp["sgu_norm_b"][l], f32).reshape(8, 128).T
    o["sgu_nw"] = nw
    o["sgu_wT"] = np.ascontiguousarray(np.asarray(inp["sgu_w"], f32).transpose(0, 3, 1, 2))
    o["sgu_b"] = np.ascontiguousarray(np.asarray(inp["sgu_b"], f32))
    return o


_CACHE = {}


def kernel(**inputs):
    x = np.asarray(inputs["x"], np.float32)
    Bsz, S, _ = x.shape
    NL = int(np.asarray(inputs["ffn1_norm"]).shape[0])
    key = (S, NL)
    if key not in _CACHE:
        mk = MK(S, NL)
        mk.build()
        _CACHE[key] = mk
    mk = _CACHE[key]
    w = prep_weights(inputs, NL)
    n_cores = 8
    in_maps = []
    for c in range(n_cores):
        b = c % Bsz
        m = dict(w)
        m["xT_in"] = np.ascontiguousarray(x[b].T)
        in_maps.append(m)
    res = run_bass_kernel_spmd(mk.nc, in_maps, core_ids=list(range(n_cores)))
    out = np.empty_like(x)
    for b in range(Bsz):
        out[b] = res.results[b]["yT_out"].T
    return out
```
